# Optimizing a Trainium2 kernel written in Bass

```python
import math
import jax, jax.numpy as jnp
from jax import lax
import numpy as np

D_MODEL = 1024
BATCH = 8
SEQ = 4096
DEPTH = 2

EPS = 1e-6
D_FF = 2752
PLE_DIM = 256

S5_WIDTH = 512
S5_GROUP = 16
S5_GROUPS = S5_WIDTH // S5_GROUP
S5_STATE = 64

GLA_HEADS = 4
GLA_DK = 64
GLA_DV = 128
GLA_KEY = GLA_HEADS * GLA_DK
GLA_VAL = GLA_HEADS * GLA_DV
GLA_RANK = 16
GLA_GATE_NORM = 16.0
GLA_CHUNK = 64

SSD_HEADS = 8
SSD_HEADDIM = 64
SSD_INNER = SSD_HEADS * SSD_HEADDIM
SSD_GROUPS = 2
SSD_STATE = 128
SSD_CONV = 4
SSD_CHUNK = 128
SSD_CONV_DIM = SSD_INNER + 2 * SSD_GROUPS * SSD_STATE

N_BRANCH = 3
IN_SIZES = (S5_WIDTH, GLA_KEY, GLA_KEY, GLA_VAL, GLA_VAL, GLA_RANK,
            SSD_INNER, SSD_CONV_DIM, SSD_HEADS, D_MODEL, D_MODEL, D_MODEL)
IN_TOTAL = sum(IN_SIZES)

kernel_name = "hybrid_s5_gla_ssd_gated_macaron"


def rmsnorm(x, w):
    xf = x.astype(jnp.float32)
    r = lax.rsqrt(jnp.mean(xf * xf, axis=-1, keepdims=True) + EPS)
    return (xf * r * w.astype(jnp.float32)).astype(x.dtype)


def swiglu(x, w_gate, w_up, w_down):
    return (jax.nn.silu(x @ w_gate) * (x @ w_up)) @ w_down


def _cplx_combine(e1, e2):
    ar1, ai1, br1, bi1 = e1
    ar2, ai2, br2, bi2 = e2
    return (ar1 * ar2 - ai1 * ai2,
            ar1 * ai2 + ai1 * ar2,
            ar2 * br1 - ai2 * bi1 + br2,
            ar2 * bi1 + ai2 * br1 + bi2)


def s5_mixer(u, lam_re, lam_im, log_step, b_re, b_im, c_re, c_im, d_skip, w_glu):
    bsz, s, _ = u.shape
    uf = u.astype(jnp.float32)
    ug = uf.reshape(bsz, s, S5_GROUPS, S5_GROUP)
    lr = jnp.minimum(lam_re.astype(jnp.float32), -1e-4)
    li = lam_im.astype(jnp.float32)
    step = jnp.exp(log_step.astype(jnp.float32))[:, None]
    mag = jnp.exp(lr * step)
    ar = mag * jnp.cos(li * step)
    ai = mag * jnp.sin(li * step)
    den = lr * lr + li * li
    nr = ar - 1.0
    fr = (nr * lr + ai * li) / den
    fi = (ai * lr - nr * li) / den
    br = b_re.astype(jnp.float32)
    bi = b_im.astype(jnp.float32)
    bbr = fr[..., None] * br - fi[..., None] * bi
    bbi = fr[..., None] * bi + fi[..., None] * br
    xr = jnp.einsum('bsgh,gph->sbgp', ug, bbr)
    xi = jnp.einsum('bsgh,gph->sbgp', ug, bbi)
    a_r = jnp.broadcast_to(ar[None, None], (s, 1) + ar.shape)
    a_i = jnp.broadcast_to(ai[None, None], (s, 1) + ai.shape)
    _, _, hr, hi = lax.associative_scan(_cplx_combine, (a_r, a_i, xr, xi), axis=0)
    y = (jnp.einsum('sbgp,ghp->bsgh', hr, c_re.astype(jnp.float32))
         - jnp.einsum('sbgp,ghp->bsgh', hi, c_im.astype(jnp.float32)))
    y = y.reshape(bsz, s, S5_WIDTH) + d_skip.astype(jnp.float32) * uf
    g = jax.nn.gelu(y).astype(u.dtype)
    return g * jax.nn.sigmoid(g @ w_glu)


def _gla_chunk(state, inp):
    qc, kc, vc, bc = inp
    n = qc.shape[2]
    causal = jnp.tril(jnp.ones((n, n), dtype=bool))
    inter = jnp.einsum('bhik,bhkv->bhiv', qc * jnp.exp(bc), state)
    diff = bc[:, :, :, None, :] - bc[:, :, None, :, :]
    decay = jnp.exp(jnp.where(causal[:, :, None], diff, -jnp.inf))
    attn = jnp.einsum('bhijk,bhjk->bhij', qc[:, :, :, None, :] * decay, kc)
    intra = jnp.einsum('bhij,bhjv->bhiv', attn, vc)
    b_last = bc[:, :, -1:, :]
    new_state = (jnp.exp(b_last[:, :, 0, :])[..., None] * state
                 + jnp.einsum('bhjk,bhjv->bhkv', kc * jnp.exp(b_last - bc), vc))
    return new_state, inter + intra


def gla_mixer(q, k, v, g_out, gate_lr, w_gate2, b_gate2, norm_w):
    bsz, s, _ = q.shape
    nc = s // GLA_CHUNK
    log_a = jax.nn.log_sigmoid((gate_lr @ w_gate2 + b_gate2).astype(jnp.float32)) / GLA_GATE_NORM

    def to_chunks(t, d):
        t = t.astype(jnp.float32).reshape(bsz, nc, GLA_CHUNK, GLA_HEADS, d)
        return t.transpose(1, 0, 3, 2, 4)

    qc = to_chunks(q, GLA_DK) * (GLA_DK ** -0.5)
    kc = to_chunks(k, GLA_DK)
    vc = to_chunks(v, GLA_DV)
    bc = jnp.cumsum(to_chunks(log_a, GLA_DK), axis=3)
    state0 = jnp.zeros((bsz, GLA_HEADS, GLA_DK, GLA_DV), jnp.float32)
    _, o = lax.scan(_gla_chunk, state0, (qc, kc, vc, bc))
    o = o.transpose(1, 0, 3, 2, 4).reshape(bsz, s, GLA_HEADS, GLA_DV)
    o = rmsnorm(o, norm_w).reshape(bsz, s, GLA_VAL)
    return (o * jax.nn.silu(g_out.astype(jnp.float32))).astype(q.dtype)


def causal_dwconv(x, w, b):
    c = x.shape[-1]
    out = lax.conv_general_dilated(x, w[:, None, :], window_strides=(1,),
                                   padding=((SSD_CONV - 1, 0),),
                                   dimension_numbers=('NWC', 'WIO', 'NWC'),
                                   feature_group_count=c)
    return out + b


def _ssd_chunk_state(h, inp):
    da, st = inp
    return jnp.exp(da)[..., None, None] * h + st, h


def ssd_mixer(z, xbc, dt_raw, conv_w, conv_b, dt_bias, a_log, d_skip, norm_w):
    bsz, s, _ = z.shape
    nc = s // SSD_CHUNK
    rep = SSD_HEADS // SSD_GROUPS
    xbc = jax.nn.silu(causal_dwconv(xbc, conv_w, conv_b)).astype(jnp.float32)
    xs = xbc[..., :SSD_INNER].reshape(bsz, s, SSD_HEADS, SSD_HEADDIM)
    bm = xbc[..., SSD_INNER:SSD_INNER + SSD_GROUPS * SSD_STATE].reshape(bsz, nc, SSD_CHUNK, SSD_GROUPS, SSD_STATE)
    cm = xbc[..., SSD_INNER + SSD_GROUPS * SSD_STATE:].reshape(bsz, nc, SSD_CHUNK, SSD_GROUPS, SSD_STATE)
    dt = jax.nn.softplus((dt_raw + dt_bias).astype(jnp.float32))
    a = -jnp.exp(a_log.astype(jnp.float32))
    la = (dt * a).reshape(bsz, nc, SSD_CHUNK, SSD_GROUPS, rep)
    xdt = (xs * dt[..., None]).reshape(bsz, nc, SSD_CHUNK, SSD_GROUPS, rep, SSD_HEADDIM)
    cum = jnp.cumsum(la, axis=2)
    causal = jnp.tril(jnp.ones((SSD_CHUNK, SSD_CHUNK), dtype=bool))
    diff = cum[:, :, :, None] - cum[:, :, None, :]
    decay = jnp.exp(jnp.where(causal[:, :, None, None], diff, -jnp.inf))
    scores = jnp.einsum('bclgn,bcmgn->bclmg', cm, bm)
    y_diag = jnp.einsum('bclmgr,bcmgrp->bclgrp', scores[..., None] * decay, xdt)
    decay_states = jnp.exp(cum[:, :, -1:] - cum)
    states = jnp.einsum('bclgn,bclgr,bclgrp->bcgrpn', bm, decay_states, xdt)
    h0 = jnp.zeros((bsz, SSD_GROUPS, rep, SSD_HEADDIM, SSD_STATE), jnp.float32)
    _, h_in = lax.scan(_ssd_chunk_state, h0,
                       (cum[:, :, -1].transpose(1, 0, 2, 3), states.transpose(1, 0, 2, 3, 4, 5)))
    h_in = h_in.transpose(1, 0, 2, 3, 4, 5)
    y_off = jnp.einsum('bclgn,bcgrpn,bclgr->bclgrp', cm, h_in, jnp.exp(cum))
    y = (y_diag + y_off).reshape(bsz, s, SSD_HEADS, SSD_HEADDIM)
    y = y + d_skip.astype(jnp.float32)[:, None] * xs
    y = y.reshape(bsz, s, SSD_INNER) * jax.nn.silu(z.astype(jnp.float32))
    y = rmsnorm(y.reshape(bsz, s, SSD_GROUPS, SSD_INNER // SSD_GROUPS),
                norm_w.reshape(SSD_GROUPS, SSD_INNER // SSD_GROUPS))
    return y.reshape(bsz, s, SSD_INNER).astype(z.dtype)


def hybrid_mix(u, w_in, s5_lam_re, s5_lam_im, s5_log_step, s5_b_re, s5_b_im, s5_c_re, s5_c_im,
               s5_d, s5_glu, gla_gate_w2, gla_gate_b2, gla_norm, ssd_conv_w, ssd_conv_b,
               ssd_dt_bias, ssd_a_log, ssd_d, ssd_norm, w_br_s5, w_br_gla, w_br_ssd, w_out):
    split_pts = np.cumsum(IN_SIZES)[:-1].tolist()
    (u_s5, q, k, v, g_out, gate_lr, z, xbc, dt_raw,
     gate_s5, gate_gla, gate_ssd) = jnp.split(u @ w_in, split_pts, axis=-1)
    y_s5 = s5_mixer(u_s5, s5_lam_re, s5_lam_im, s5_log_step, s5_b_re, s5_b_im,
                    s5_c_re, s5_c_im, s5_d, s5_glu)
    y_gla = gla_mixer(q, k, v, g_out, gate_lr, gla_gate_w2, gla_gate_b2, gla_norm)
    y_ssd = ssd_mixer(z, xbc, dt_raw, ssd_conv_w, ssd_conv_b, ssd_dt_bias, ssd_a_log, ssd_d, ssd_norm)
    merged = (jax.nn.sigmoid(gate_s5) * (y_s5 @ w_br_s5)
              + jax.nn.sigmoid(gate_gla) * (y_gla @ w_br_gla)
              + jax.nn.sigmoid(gate_ssd) * (y_ssd @ w_br_ssd))
    return merged @ w_out


def setup_inputs(seed: int = 0) -> dict:
    key = jax.random.key(seed)
    ks = iter(jax.random.split(key, 48))
    L = DEPTH

    def nrm(shape, scale):
        return jax.random.normal(next(ks), shape, jnp.float32) * scale

    def gain(shape):
        return 1.0 + nrm(shape, 0.01)

    def unif(shape, lo, hi):
        return jax.random.uniform(next(ks), shape, jnp.float32, lo, hi)

    n_idx = jnp.arange(S5_STATE, dtype=jnp.float32)
    dt_init = jnp.exp(unif((L, SSD_HEADS), math.log(1e-3), math.log(1e-1)))
    return {
        "x": nrm((BATCH, SEQ, D_MODEL), 1.0),
        "p": nrm((DEPTH, BATCH, SEQ, PLE_DIM), 1.0),
        "ffn1_norm": gain((L, D_MODEL)),
        "ffn1_gate": nrm((L, D_MODEL, D_FF), D_MODEL ** -0.5),
        "ffn1_up": nrm((L, D_MODEL, D_FF), D_MODEL ** -0.5),
        "ffn1_down": nrm((L, D_FF, D_MODEL), D_FF ** -0.5),
        "mix_norm": gain((L, D_MODEL)),
        "w_in": nrm((L, D_MODEL, IN_TOTAL), D_MODEL ** -0.5),
        "s5_lam_re": -0.5 + nrm((L, S5_GROUPS, S5_STATE), 0.01),
        "s5_lam_im": math.pi * n_idx + nrm((L, S5_GROUPS, S5_STATE), 0.01),
        "s5_log_step": unif((L, S5_GROUPS), math.log(1e-3), math.log(1e-1)),
        "s5_b_re": nrm((L, S5_GROUPS, S5_STATE, S5_GROUP), (2.0 * S5_GROUP) ** -0.5),
        "s5_b_im": nrm((L, S5_GROUPS, S5_STATE, S5_GROUP), (2.0 * S5_GROUP) ** -0.5),
        "s5_c_re": nrm((L, S5_GROUPS, S5_GROUP, S5_STATE), S5_STATE ** -0.5),
        "s5_c_im": nrm((L, S5_GROUPS, S5_GROUP, S5_STATE), S5_STATE ** -0.5),
        "s5_d": nrm((L, S5_WIDTH), 1.0),
        "s5_glu": nrm((L, S5_WIDTH, S5_WIDTH), S5_WIDTH ** -0.5),
        "gla_gate_w2": nrm((L, GLA_RANK, GLA_KEY), GLA_RANK ** -0.5),
        "gla_gate_b2": nrm((L, GLA_KEY), 0.1),
        "gla_norm": gain((L, GLA_DV)),
        "ssd_conv_w": nrm((L, SSD_CONV, SSD_CONV_DIM), SSD_CONV ** -0.5),
        "ssd_conv_b": nrm((L, SSD_CONV_DIM), 0.01),
        "ssd_dt_bias": dt_init + jnp.log(-jnp.expm1(-dt_init)),
        "ssd_a_log": jnp.log(unif((L, SSD_HEADS), 1.0, 16.0)),
        "ssd_d": gain((L, SSD_HEADS)),
        "ssd_norm": gain((L, SSD_INNER)),
        "w_br_s5": nrm((L, S5_WIDTH, D_MODEL), S5_WIDTH ** -0.5),
        "w_br_gla": nrm((L, GLA_VAL, D_MODEL), GLA_VAL ** -0.5),
        "w_br_ssd": nrm((L, SSD_INNER, D_MODEL), SSD_INNER ** -0.5),
        "w_out": nrm((L, D_MODEL, D_MODEL), D_MODEL ** -0.5),
        "ffn2_norm": gain((L, D_MODEL)),
        "ffn2_gate": nrm((L, D_MODEL, D_FF), D_MODEL ** -0.5),
        "ffn2_up": nrm((L, D_MODEL, D_FF), D_MODEL ** -0.5),
        "ffn2_down": nrm((L, D_FF, D_MODEL), D_FF ** -0.5),
        "ple_norm": gain((L, D_MODEL)),
        "ple_gate": nrm((L, D_MODEL, D_MODEL), D_MODEL ** -0.5),
        "ple_proj": nrm((L, PLE_DIM, D_MODEL), PLE_DIM ** -0.5),
        "final_norm": gain((D_MODEL,)),
    }


def reference(x, p, ffn1_norm, ffn1_gate, ffn1_up, ffn1_down, mix_norm, w_in,
              s5_lam_re, s5_lam_im, s5_log_step, s5_b_re, s5_b_im, s5_c_re, s5_c_im, s5_d, s5_glu,
              gla_gate_w2, gla_gate_b2, gla_norm, ssd_conv_w, ssd_conv_b, ssd_dt_bias, ssd_a_log,
              ssd_d, ssd_norm, w_br_s5, w_br_gla, w_br_ssd, w_out, ffn2_norm, ffn2_gate, ffn2_up,
              ffn2_down, ple_norm, ple_gate, ple_proj, final_norm):
    h = x
    for i in range(DEPTH):
        h = h + 0.5 * swiglu(rmsnorm(h, ffn1_norm[i]), ffn1_gate[i], ffn1_up[i], ffn1_down[i])
        h = h + hybrid_mix(rmsnorm(h, mix_norm[i]), w_in[i],
                           s5_lam_re[i], s5_lam_im[i], s5_log_step[i], s5_b_re[i], s5_b_im[i],
                           s5_c_re[i], s5_c_im[i], s5_d[i], s5_glu[i],
                           gla_gate_w2[i], gla_gate_b2[i], gla_norm[i],
                           ssd_conv_w[i], ssd_conv_b[i], ssd_dt_bias[i], ssd_a_log[i], ssd_d[i], ssd_norm[i],
                           w_br_s5[i], w_br_gla[i], w_br_ssd[i], w_out[i])
        h = h + 0.5 * swiglu(rmsnorm(h, ffn2_norm[i]), ffn2_gate[i], ffn2_up[i], ffn2_down[i])
        h = h + jax.nn.sigmoid(rmsnorm(h, ple_norm[i]) @ ple_gate[i]) * (p[i] @ ple_proj[i])
    return rmsnorm(h, final_norm)
```

```python
import contextlib
import numpy as np
import concourse.bass as bass
import concourse.mybir as mybir
from concourse.alu_op_type import AluOpType as ALU
from concourse.bass_utils import run_bass_kernel_spmd

F32 = mybir.dt.float32
BF16 = mybir.dt.bfloat16
AF = mybir.ActivationFunctionType

S = 4096
D = 1024
DFF = 2752
DEPTH = 2
TT = 512
NT = S // TT
IN_TOTAL = 6680
EPS = 1e-6

STREAMS = ['pe', 'act', 'dve', 'pool', 'sp']
NCH = 8
SEM_ROLL = 12000


class Sched:
    def __init__(self):
        self.ops = []

    def add(self, eng, fn, reads=(), writes=(), dma=False):
        self.ops.append(dict(eng=eng, fn=fn, reads=tuple(reads), writes=tuple(writes),
                             dma=dma, barrier=False))

    def barrier(self):
        self.ops.append(dict(barrier=True))

    def analyze(self):
        last_w = {}
        readers = {}
        last_on_stream = {}
        last_on_chan = {}
        pending = {s: set() for s in STREAMS}
        ch_rr = 0
        for i, op in enumerate(self.ops):
            if op['barrier']:
                allp = set(last_on_stream.values()) | set(last_on_chan.values())
                for s in STREAMS:
                    pending[s] |= allp
                last_w = {}
                readers = {}
                continue
            deps = {}
            eng = op['eng']
            for r in op['reads']:
                j = last_w.get(r)
                if j is not None:
                    deps[j] = 'RAW'
            for w in op['writes']:
                j = last_w.get(w)
                if j is not None and j not in deps:
                    deps[j] = 'WAW'
                for j in readers.get(w, ()):
                    if j not in deps:
                        deps[j] = 'WAR'
            for j in pending[eng]:
                if j not in deps:
                    deps[j] = 'BAR'
            pending[eng] = set()
            if op['dma']:
                op['chan'] = ch_rr
                ch_rr = (ch_rr + 1) % NCH
                j = last_on_chan.get(op['chan'])
                if j is not None:
                    deps[j] = 'BAR'
                last_on_chan[op['chan']] = i
            else:
                last_on_stream[eng] = i
            fdeps = []
            for j, kind in deps.items():
                pj = self.ops[j]
                if (not pj['dma']) and (not op['dma']) and pj['eng'] == eng:
                    if eng == 'pe' or kind != 'RAW':
                        continue
                fdeps.append(j)
            op['deps'] = fdeps
            for j in fdeps:
                self.ops[j]['needs_inc'] = True
            for r in op['reads']:
                readers.setdefault(r, []).append(i)
            for w in op['writes']:
                last_w[w] = i
                readers[w] = []
        self.last_on_chan = last_on_chan

    def emit(self, nc):
        self.analyze()
        with contextlib.ExitStack() as es:
            def newsem(name):
                return es.enter_context(nc.semaphore(name))

            cur = {s: [newsem(f"s_{s}_0"), 0, 0] for s in ['pe', 'act', 'dve', 'pool']}
            chs = [[newsem(f"s_ch{c}_0"), 0, 0] for c in range(NCH)]
            for op in self.ops:
                if op['barrier']:
                    continue
                if op['dma']:
                    st = chs[op['chan']]
                    nm = f"s_ch{op['chan']}"
                    inc = 16
                elif op.get('needs_inc'):
                    st = cur[op['eng']]
                    nm = f"s_{op['eng']}"
                    inc = 1
                else:
                    continue
                if st[1] + inc > SEM_ROLL:
                    st[2] += 1
                    st[0] = newsem(f"{nm}_{st[2]}")
                    st[1] = 0
                st[1] += inc
                op['sem'], op['val'], op['inc'] = st[0], st[1], inc
            lists = {s: [] for s in STREAMS}
            waited = {s: {} for s in STREAMS}
            for op in self.ops:
                if op['barrier']:
                    continue
                s = op['eng']
                for j in op['deps']:
                    pj = self.ops[j]
                    key = id(pj['sem'])
                    if waited[s].get(key, 0) >= pj['val']:
                        continue
                    waited[s][key] = pj['val']
                    lists[s].append(('wait', pj['sem'], pj['val']))
                lists[s].append(('op', op))
            for c, j in self.last_on_chan.items():
                pj = self.ops[j]
                lists['sp'].append(('wait', pj['sem'], pj['val']))

            def run(e, items):
                for it in items:
                    if it[0] == 'wait':
                        e.wait_ge(it[1], it[2])
                    else:
                        op = it[1]
                        ins = op['fn'](e)
                        if 'sem' in op:
                            ins.then_inc(op['sem'], op['inc'])

            with nc.Block() as block:
                @block.tensor
                def _(e):
                    run(e, lists['pe'])

                @block.scalar
                def _(e):
                    run(e, lists['act'])

                @block.vector
                def _(e):
                    run(e, lists['dve'])

                @block.gpsimd
                def _(e):
                    run(e, lists['pool'])

                @block.sync
                def _(e):
                    run(e, lists['sp'])


PARAM_NAMES = ["ffn1_norm", "ffn1_gate", "ffn1_up", "ffn1_down", "mix_norm", "w_in",
               "s5_lam_re", "s5_lam_im", "s5_log_step", "s5_b_re", "s5_b_im", "s5_c_re", "s5_c_im",
               "s5_d", "s5_glu", "gla_gate_w2", "gla_gate_b2", "gla_norm", "ssd_conv_w", "ssd_conv_b",
               "ssd_dt_bias", "ssd_a_log", "ssd_d", "ssd_norm", "w_br_s5", "w_br_gla", "w_br_ssd",
               "w_out", "ffn2_norm", "ffn2_gate", "ffn2_up", "ffn2_down", "ple_norm", "ple_gate",
               "ple_proj", "final_norm"]
PARAM_SHAPES = {
    "ffn1_norm": (2, 1024), "ffn1_gate": (2, 1024, 2752), "ffn1_up": (2, 1024, 2752),
    "ffn1_down": (2, 2752, 1024), "mix_norm": (2, 1024), "w_in": (2, 1024, 6680),
    "s5_lam_re": (2, 32, 64), "s5_lam_im": (2, 32, 64), "s5_log_step": (2, 32),
    "s5_b_re": (2, 32, 64, 16), "s5_b_im": (2, 32, 64, 16), "s5_c_re": (2, 32, 16, 64),
    "s5_c_im": (2, 32, 16, 64), "s5_d": (2, 512), "s5_glu": (2, 512, 512),
    "gla_gate_w2": (2, 16, 256), "gla_gate_b2": (2, 256), "gla_norm": (2, 128),
    "ssd_conv_w": (2, 4, 1024), "ssd_conv_b": (2, 1024), "ssd_dt_bias": (2, 8), "ssd_a_log": (2, 8),
    "ssd_d": (2, 8), "ssd_norm": (2, 512), "w_br_s5": (2, 512, 1024), "w_br_gla": (2, 512, 1024),
    "w_br_ssd": (2, 512, 1024), "w_out": (2, 1024, 1024), "ffn2_norm": (2, 1024),
    "ffn2_gate": (2, 1024, 2752), "ffn2_up": (2, 1024, 2752), "ffn2_down": (2, 2752, 1024),
    "ple_norm": (2, 1024), "ple_gate": (2, 1024, 1024), "ple_proj": (2, 256, 1024),
    "final_norm": (1024,),
}


def host_consts():
    c = {}
    c["c_ident"] = np.eye(128, dtype=np.float32)
    i = np.arange(128)
    c["c_tri128"] = (i[:, None] <= i[None, :]).astype(np.float32)
    same = (i[:, None] // 64) == (i[None, :] // 64)
    c["c_tri64"] = ((i[:, None] <= i[None, :]) & same).astype(np.float32)
    c["c_blk64"] = same.astype(np.float32)
    c["c_mask64"] = ((i[:, None] % 64) <= np.arange(64)[None, :]).astype(np.float32)
    c["c_ones"] = np.ones((128, 128), np.float32)
    g = np.arange(512) // 16
    c["c_selb"] = (np.arange(32)[:, None] == g[None, :]).astype(np.float32)
    c["c_gmask"] = ((i[:, None] // 16) == np.arange(8)[None, :]).astype(np.float32)
    c["c_hmask"] = ((i[:, None] // 64) == np.arange(2)[None, :]).astype(np.float32)
    c["c_iota"] = np.tile(np.arange(512, dtype=np.float32)[None, :], (128, 1))
    return c


class KB:
    def __init__(self, nc, debug=False, stages=None):
        self.nc = nc
        self.sc = Sched()
        self.debug = debug
        self.stages = stages
        self.pbank = 0
        self.uid = 0

    def alloc(self, n, dt=F32):
        if dt == BF16:
            m = (n + 1) // 2
            a = self.arena[:, self.off:self.off + m].bitcast(BF16)
        else:
            m = n
            a = self.arena[:, self.off:self.off + m]
        self.off += m
        assert self.off <= self.arena_n, f"arena overflow {self.off}"
        return a

    def mark(self):
        return self.off

    def release(self, m):
        self.off = m
        self.sc.barrier()

    def bank(self):
        b = self.pbank % 8
        self.pbank += 1
        return b, self.ps[:, b * 512:(b + 1) * 512]

    def key(self, name):
        self.uid += 1
        return (name, self.uid)

    def dram(self, name, shape, dt):
        kind = "ExternalOutput" if (self.debug and name in self.debug) else "Internal"
        return self.nc.dram_tensor(name, list(shape), dt, kind=kind).ap()

    def A(self, eng, fn, r=(), w=()):
        self.sc.add(eng, fn, r, w)

    def dma(self, out, in_, r=(), w=(), slow=False):
        if slow:
            self.sc.add('sp', lambda e: e.dma_start(out=out, in_=in_, allow_slow_non_contiguous=True), r, w, dma=True)
        else:
            self.sc.add('sp', lambda e: e.dma_start(out=out, in_=in_), r, w, dma=True)

    def mm(self, out, lhsT, rhs, start, stop, r, w):
        self.sc.add('pe', lambda e: e.matmul(out, lhsT=lhsT, rhs=rhs, start=start, stop=stop), r, w)

    def act(self, out, in_, func, r, w, bias=None, scale=None):
        kw = {}
        if bias is not None:
            kw['bias'] = bias
        if scale is not None:
            kw['scale'] = scale
        self.sc.add('act', lambda e: e.activation(out=out, in_=in_, func=func, **kw), r, w)

    def tt(self, eng, out, in0, in1, op, r, w):
        self.sc.add(eng, lambda e: e.tensor_tensor(out=out, in0=in0, in1=in1, op=op), r, w)

    def ts(self, eng, out, in0, s1, s2, op0, op1, r, w):
        if op1 is None:
            self.sc.add(eng, lambda e: e.tensor_scalar(out=out, in0=in0, scalar1=s1, scalar2=None, op0=op0), r, w)
        else:
            self.sc.add(eng, lambda e: e.tensor_scalar(out=out, in0=in0, scalar1=s1, scalar2=s2, op0=op0, op1=op1), r, w)

    def stt(self, out, in0, scalar, in1, op0, op1, r, w):
        self.sc.add('dve', lambda e: e.scalar_tensor_tensor(out=out, in0=in0, scalar=scalar, in1=in1,
                                                           op0=op0, op1=op1), r, w)

    def copy(self, eng, out, in_, r, w):
        if eng == 'act':
            self.sc.add('act', lambda e: e.activation(out=out, in_=in_, func=AF.Copy), r, w)
        else:
            self.sc.add(eng, lambda e: e.tensor_copy(out=out, in_=in_), r, w)

    def load_cols(self, dst, vec_ap, nk):
        k = self.key('col')
        self.dma(dst, vec_ap.rearrange("(k p) -> p k", p=128), w=[k], slow=True)
        return k

    def load_weight(self, w_ap, kc_sizes, c0, ncols, stg, wb, kstg, kwb, cast_eng='pool'):
        KC = len(kc_sizes)
        full = [i for i, s in enumerate(kc_sizes) if s == 128]
        nf = len(full)
        stg3 = stg[:, :KC * ncols].rearrange("p (k n) -> p k n", k=KC)
        wb3 = wb[:, :KC * ncols].rearrange("p (k n) -> p k n", k=KC)
        if nf > 0:
            self.dma(stg3[:, :nf, :], w_ap[0:nf * 128, c0:c0 + ncols].rearrange("(k p) n -> p k n", p=128),
                     w=[kstg])
            self.copy(cast_eng, wb3[:, :nf, :], stg3[:, :nf, :], r=[kstg], w=[kwb])
        if nf < KC:
            rem = kc_sizes[-1]
            self.dma(stg3[:rem, nf, :], w_ap[nf * 128:nf * 128 + rem, c0:c0 + ncols], w=[kstg])
            self.copy(cast_eng, wb3[:rem, nf, :], stg3[:rem, nf, :], r=[kstg], w=[kwb])
        return wb3

    def rmsnorm_tile(self, h, hkeys, wcol, kw, xn_out, xnkeys, tmp):
        sq, sd, rstd = tmp
        ksq = [self.key('sq') for _ in range(8)]
        for k in range(8):
            self.act(sq[:, k * 512:(k + 1) * 512], h[:, k * 512:(k + 1) * 512], AF.Square,
                     r=[hkeys[k]], w=[ksq[k]])
        b, pb = self.bank()
        for k in range(8):
            self.mm(pb, self.ones_b, sq[:, k * 512:(k + 1) * 512], k == 0, k == 7,
                    r=[ksq[k], 'ones'], w=[('ps', b)])
        ksd = self.key('sd')
        self.act(sd, pb, AF.Sqrt, r=[('ps', b)], w=[ksd], bias=EPS, scale=1.0 / D)
        krs = self.key('rstd')
        self.A('dve', lambda e: e.reciprocal(out=rstd, in_=sd), r=[ksd], w=[krs])
        for k in range(8):
            eng = 'dve'
            self.stt(xn_out(k), h[:, k * 512:(k + 1) * 512], wcol[:, k:k + 1], rstd, ALU.mult, ALU.mult,
                     r=[hkeys[k], krs, kw], w=[xnkeys[k]])

    def build(self):
        nc = self.nc
        self.x = nc.dram_tensor("x", [S, D], F32, kind="ExternalInput").ap()
        self.p = nc.dram_tensor("p", [DEPTH, S, 256], F32, kind="ExternalInput").ap()
        self.prm = {}
        for n in PARAM_NAMES:
            self.prm[n] = nc.dram_tensor(n, list(PARAM_SHAPES[n]), F32, kind="ExternalInput").ap()
        self.cst = {}
        for n, v in host_consts().items():
            self.cst[n] = nc.dram_tensor(n, list(v.shape), F32, kind="ExternalInput").ap()
        self.out = nc.dram_tensor("out", [S, D], F32, kind="ExternalOutput").ap()
        self.H = self.dram("H", [8, 128, S], F32)
        self.HID = self.dram("HID", [22, 128, S], BF16)
        self.US5 = self.dram("US5", [4, 128, S], BF16)
        self.Q = self.dram("Q", [2, 128, S], BF16)
        self.Kf = self.dram("Kf", [2, 128, S], BF16)
        self.KT = self.dram("KT", [S, 256], BF16)
        self.VT = self.dram("VT", [S, 512], BF16)
        self.GO = self.dram("GO", [4, 128, S], BF16)
        self.GLR = self.dram("GLR", [1, 128, S], F32)
        self.Z = self.dram("Z", [4, 128, S], BF16)
        self.XBC = self.dram("XBC", [8, 128, S], F32)
        self.DTT = self.dram("DTT", [S, 8], F32)
        self.SIG = self.dram("SIG", [24, 128, S], BF16)
        self.Y = self.dram("Y", [3, 4, 128, S], BF16)
        self.S5T = self.dram("S5T", [2, 32, 64], F32)
        self.arena_n = 52000
        with nc.sbuf_tensor("arena", [128, self.arena_n], F32) as arena, \
                nc.psum_tensor("ps", [128, 4096], F32) as ps:
            self.arena = arena
            self.ps = ps
            self.off = 0
            self.ident = self.alloc(128)
            self.ones_b = self.alloc(128, BF16)
            self.eps_col = self.alloc(1)
            self.dma(self.ident, self.cst["c_ident"], w=['ident'])
            self.A('pool', lambda e: e.memset(self.ones_b, 1.0), w=['ones'])
            self.A('pool', lambda e: e.memset(self.eps_col, EPS), w=['eps'])
            self.alloc_consts()
            self.XNf = self.arena[:, self.off:self.off + 4 * S]
            self.XN = self.alloc(8 * S, BF16)
            self.base = self.mark()
            self.sc.barrier()

            self.stage_input()
            if self.debug:
                self.stage_ffn(0, 1)
                self.stage_mixer(0)
            else:
                for l in range(DEPTH):
                    self.stage_ffn(l, 1)
                    self.stage_mixer(l)
                    self.stage_ffn(l, 2)
                    self.stage_ple(l)
            self.sc.emit(nc)
        return nc

    def xn_ap(self, k, t):
        return self.XN[:, k * S + t * TT:k * S + (t + 1) * TT]

    def stage_input(self):
        m = self.mark()
        xt = [[self.alloc(1024) for _ in range(4)] for _ in range(2)]
        hb = [self.alloc(8 * 512) for _ in range(2)]
        tmp = (self.alloc(8 * 512, BF16), self.alloc(512), self.alloc(512))
        wcol = self.alloc(8)
        kw = self.load_cols(wcol, self.prm["ffn1_norm"][0], 8)
        for t in range(NT):
            xb = xt[t % 2]
            h = hb[t % 2]
            for sub in range(4):
                r0 = t * TT + sub * 128
                self.dma(xb[sub], self.x[r0:r0 + 128, :], w=[('xt', t % 2, sub)])
            hkeys = [('h', t % 2, k) for k in range(8)]
            for k in range(8):
                b, pb = self.bank()
                for sub in range(4):
                    self.A('pe', lambda e, pb=pb, sub=sub, k=k, xb=xb: e.transpose(
                        out=pb[:, sub * 128:(sub + 1) * 128], in_=xb[sub][:, k * 128:(k + 1) * 128],
                        identity=self.ident), r=[('xt', t % 2, sub), 'ident'], w=[('ps', b)])
                self.copy('act' if k % 2 == 0 else 'dve', h[:, k * 512:(k + 1) * 512], pb,
                          r=[('ps', b)], w=[hkeys[k]])
            self.dma(self.H[:, :, t * TT:(t + 1) * TT].rearrange("k p n -> p k n"),
                     h.rearrange("p (k n) -> p k n", k=8), r=hkeys)
            self.rmsnorm_tile(h, hkeys, wcol, kw, lambda k, t=t: self.xn_ap(k, t),
                              [('xn', k, t) for k in range(8)], tmp)
        self.release(m)

    def stage_ffn(self, l, which):
        pre = f"ffn{which}_"
        Wg, Wu, Wd = self.prm[pre + "gate"][l], self.prm[pre + "up"][l], self.prm[pre + "down"][l]
        m = self.mark()
        stg = [self.alloc(8 * 512) for _ in range(2)]
        wgb = [self.alloc(8 * 512, BF16) for _ in range(2)]
        wub = [self.alloc(8 * 512, BF16) for _ in range(2)]
        sil = [self.alloc(512) for _ in range(2)]
        hid = [self.alloc(512, BF16) for _ in range(4)]
        groups = [(c0, min(512, DFF - c0)) for c0 in range(0, DFF, 512)]
        it = 0
        hi = 0
        for gi, (c0, ncols) in enumerate(groups):
            s = gi % 2
            g3 = self.load_weight(Wg, [128] * 8, c0, ncols, stg[0], wgb[s], ('stg', 0), ('wgb', s))
            u3 = self.load_weight(Wu, [128] * 8, c0, ncols, stg[1], wub[s], ('stg', 1), ('wub', s))
            for t in range(NT):
                for mc in range((ncols + 127) // 128):
                    mw = min(128, ncols - mc * 128)
                    j = (c0 // 128) + mc
                    bg, pg = self.bank()
                    for k in range(8):
                        self.mm(pg[:mw, :], g3[:, k, mc * 128:mc * 128 + mw], self.xn_ap(k, t), k == 0, k == 7,
                                r=[('wgb', s), ('xn', k, t)], w=[('ps', bg)])
                    bu, pu = self.bank()
                    for k in range(8):
                        self.mm(pu[:mw, :], u3[:, k, mc * 128:mc * 128 + mw], self.xn_ap(k, t), k == 0, k == 7,
                                r=[('wub', s), ('xn', k, t)], w=[('ps', bu)])
                    sl = sil[it % 2]
                    hd = hid[hi % 4]
                    self.act(sl[:mw, :], pg[:mw, :], AF.Silu, r=[('ps', bg)], w=[('sil', it % 2)])
                    self.tt('dve', hd[:mw, :], sl[:mw, :], pu[:mw, :], ALU.mult,
                            r=[('sil', it % 2), ('ps', bu)], w=[('hid', hi % 4)])
                    self.dma(self.HID[j, :mw, t * TT:(t + 1) * TT], hd[:mw, :], r=[('hid', hi % 4)],
                             w=[('HID', j, t)])
                    it += 1
                    hi += 1
        self.release(m)
        kc_sizes = [128] * 21 + [64]
        wd = self.alloc(22 * 1024, BF16)
        stg = [self.alloc(8 * 512)] * 2
        wd3 = wd.rearrange("p (k n) -> p k n", k=22)
        for q, k0 in enumerate(range(0, 22, 4)):
            nk = min(4, 22 - k0)
            st = stg[q % 2][:, :nk * 1024].rearrange("p (k n) -> p k n", k=nk)
            for kk in range(nk):
                rows = kc_sizes[k0 + kk]
                self.dma(st[:rows, kk, :], Wd[(k0 + kk) * 128:(k0 + kk) * 128 + rows, :], w=[('stg', 0)])
            if k0 + nk == 22:
                self.copy('pool', wd3[:, k0:k0 + nk - 1, :], st[:, :nk - 1, :], r=[('stg', 0)], w=['wd'])
                self.copy('pool', wd3[:64, 21, :], st[:64, nk - 1, :], r=[('stg', 0)], w=['wd'])
            else:
                self.copy('pool', wd3[:, k0:k0 + nk, :], st, r=[('stg', 0)], w=['wd'])
        hidt = [self.alloc(22 * 512, BF16)] * 2
        hb = [self.alloc(8 * 512) for _ in range(2)]
        tmp = (self.alloc(8 * 512, BF16), self.alloc(512), self.alloc(512))
        wcol = self.alloc(8)
        nxt = self.prm["mix_norm"][l] if which == 1 else self.prm["ple_norm"][l]
        kw = self.load_cols(wcol, nxt, 8)
        for t in range(NT):
            ht3 = hidt[t % 2].rearrange("p (k n) -> p k n", k=22)
            h = hb[t % 2]
            self.dma(ht3[:, :21, :], self.HID[0:21, :, t * TT:(t + 1) * TT].rearrange("k p n -> p k n"),
                     r=[('HID', j, t) for j in range(21)], w=[('hidt', 0)])
            self.dma(ht3[:64, 21, :], self.HID[21, :64, t * TT:(t + 1) * TT],
                     r=[('HID', 21, t)], w=[('hidt', 0)])
            hkeys = [('h', t % 2, k) for k in range(8)]
            self.dma(h.rearrange("p (k n) -> p k n", k=8),
                     self.H[:, :, t * TT:(t + 1) * TT].rearrange("k p n -> p k n"), w=hkeys)
            for mo in range(8):
                b, pb = self.bank()
                for j in range(22):
                    rows = kc_sizes[j]
                    self.mm(pb, wd3[:rows, j, mo * 128:(mo + 1) * 128], ht3[:rows, j, :], j == 0, j == 21,
                            r=['wd', ('hidt', 0)], w=[('ps', b)])
                hk = h[:, mo * 512:(mo + 1) * 512]
                self.stt(hk, pb, 0.5, hk, ALU.mult, ALU.add, r=[('ps', b), hkeys[mo]], w=[hkeys[mo]])
            self.dma(self.H[:, :, t * TT:(t + 1) * TT].rearrange("k p n -> p k n"),
                     h.rearrange("p (k n) -> p k n", k=8), r=hkeys)
            self.rmsnorm_tile(h, hkeys, wcol, kw, lambda k, t=t: self.xn_ap(k, t),
                              [('xn', k, t) for k in range(8)], tmp)
        self.release(m)

    def stage_mixer(self, l):
        st = self.stages
        self.stage_proj(l)
        if st is None or 'ssd' in st:
            self.stage_ssd(l)
        if st is None or 'gla' in st:
            self.stage_gla(l)
        if st is None or 's5' in st:
            self.stage_s5(l)
        if st is None or 'merge' in st:
            self.stage_merge(l)
        else:
            self.renorm(self.prm["ffn2_norm"][l])

    def load_consts(self):
        return

    def alloc_consts(self):
        def ld(name, n, rows=128):
            a = self.alloc(n)
            self.dma(a[:rows, :], self.cst[name], w=[name])
            return a
        self.tri128 = ld("c_tri128", 128)
        self.tri64 = ld("c_tri64", 128)
        self.blk64 = ld("c_blk64", 128)
        self.mask64 = ld("c_mask64", 64)
        self.onesf = ld("c_ones", 128)
        self.gmask = ld("c_gmask", 8)
        self.hmask = ld("c_hmask", 2)

    def stage_proj(self, l):
        W = self.prm["w_in"][l]
        m = self.mark()
        stg = [self.alloc(8 * 512) for _ in range(2)]
        wbb = [self.alloc(8 * 512, BF16) for _ in range(2)]
        of = [self.alloc(512) for _ in range(3)]
        ob = [self.alloc(512, BF16) for _ in range(3)]
        segs = [(0, 512, AF.Copy, 1.0, self.US5, BF16), (512, 256, AF.Copy, 0.125, self.Q, BF16),
                (768, 256, AF.Copy, 1.0, self.Kf, BF16), (1536, 512, AF.Silu, 1.0, self.GO, BF16),
                (2048, 16, AF.Copy, 1.0, self.GLR, F32), (2064, 512, AF.Silu, 1.0, self.Z, BF16),
                (2576, 1024, AF.Copy, 1.0, self.XBC, F32), (3608, 3072, AF.Sigmoid, 1.0, self.SIG, BF16)]
        gi = 0
        oi = 0
        for (c0s, n, func, scale, dest, dt) in segs:
            for g0 in range(0, n, 512):
                ncols = min(512, n - g0)
                sidx = gi % 2
                gi += 1
                w3 = self.load_weight(W, [128] * 8, c0s + g0, ncols, stg[sidx], wbb[sidx], ('stg', sidx),
                                      ('wbb', sidx))
                for t in range(NT):
                    for mc in range((ncols + 127) // 128):
                        mw = min(128, ncols - mc * 128)
                        j = g0 // 128 + mc
                        b, pb = self.bank()
                        for k in range(8):
                            self.mm(pb[:mw, :], w3[:, k, mc * 128:mc * 128 + mw], self.xn_ap(k, t), k == 0, k == 7,
                                    r=[('wbb', sidx), ('xn', k, t)], w=[('ps', b)])
                        o = (of if dt == F32 else ob)[oi % 3]
                        okey = ('of' if dt == F32 else 'ob', oi % 3)
                        oi += 1
                        self.act(o[:mw, :], pb[:mw, :], func, r=[('ps', b)], w=[okey], scale=scale)
                        self.dma(dest[j, :mw, t * TT:(t + 1) * TT], o[:mw, :], r=[okey])
        for (c0s, n, dest, dt) in [(768, 256, self.KT, BF16), (1024, 512, self.VT, BF16), (3600, 8, self.DTT, F32)]:
            sidx = gi % 2
            gi += 1
            w3 = self.load_weight(W, [128] * 8, c0s, n, stg[sidx], wbb[sidx], ('stg', sidx), ('wbb', sidx))
            for blk in range(S // 128):
                t, sub = blk // 4, blk % 4
                b, pb = self.bank()
                for k in range(8):
                    xs_ = self.XN[:, k * S + blk * 128:k * S + (blk + 1) * 128]
                    self.mm(pb[:, :n], xs_, w3[:, k, :n], k == 0, k == 7, r=[('wbb', sidx), ('xn', k, t)],
                            w=[('ps', b)])
                o = (of if dt == F32 else ob)[oi % 3]
                okey = ('of' if dt == F32 else 'ob', oi % 3)
                oi += 1
                self.copy('act' if blk % 2 == 0 else 'dve', o[:, :n], pb[:, :n], r=[('ps', b)], w=[okey])
                self.dma(dest[blk * 128:(blk + 1) * 128, :], o[:, :n], r=[okey])
        self.release(m)

    def stage_ssd(self, l):
        m = self.mark()
        P = self.prm
        f3 = lambda a, k: a.rearrange("p (k n) -> p k n", k=k)
        cw = self.alloc(32)
        cb = self.alloc(8)
        for k in range(4):
            self.dma(cw[:, k * 8:(k + 1) * 8], P["ssd_conv_w"][l][k].rearrange("(c p) -> p c", p=128), w=['cw'],
                     slow=True)
        self.dma(cb, P["ssd_conv_b"][l].rearrange("(c p) -> p c", p=128), w=['cb'], slow=True)
        dtb = self.alloc(8)
        abc = self.alloc(8)
        dbc = self.alloc(8)
        self.dma(dtb, P["ssd_dt_bias"][l].partition_broadcast(128), w=['dtb'], slow=True)
        self.dma(abc, P["ssd_a_log"][l].partition_broadcast(128), w=['abc'], slow=True)
        self.dma(dbc, P["ssd_d"][l].partition_broadcast(128), w=['dbc'], slow=True)
        self.act(abc, abc, AF.Exp, r=['abc'], w=['abc'])
        self.ts('dve', abc, abc, -1.0, None, ALU.mult, None, r=['abc'], w=['abc'])
        dcol = self.alloc(4)
        for i in range(4):
            self.copy('dve', dcol[0:64, i:i + 1], dbc[0:64, 2 * i:2 * i + 1], r=['dbc'], w=['dcol'])
            self.copy('dve', dcol[64:128, i:i + 1], dbc[64:128, 2 * i + 1:2 * i + 2], r=['dbc'], w=['dcol'])
        nw = self.alloc(4)
        self.dma(nw, P["ssd_norm"][l].rearrange("(c p) -> p c", p=128), w=['nw'], slow=True)
        hT = [self.alloc(256) for _ in range(2)]
        hTb = [self.alloc(256, BF16) for _ in range(2)]
        for g in range(2):
            self.A('pool', lambda e, g=g: e.memset(hT[g], 0.0), w=[('hT', g)])
            self.A('pool', lambda e, g=g: e.memset(hTb[g], 0.0), w=[('hTb', g)])
        xin = [self.alloc(8 * 520) for _ in range(1)]
        acc = self.alloc(512)
        xc = self.alloc(8 * 512)
        xcb = self.alloc(4 * 512, BF16)
        zt = self.alloc(4 * 512, BF16)
        ybuf = self.alloc(4 * 512, BF16)
        dtr = self.alloc(8)
        dt = self.alloc(8)
        la = self.alloc(8)
        cum = self.alloc(8)
        latri = self.alloc(1024)
        dec = self.alloc(1024)
        ecr = self.alloc(1024)
        ecl = self.alloc(8)
        ds = self.alloc(8)
        Cp = self.alloc(1024, BF16)
        scm = self.alloc(256)
        SdT = self.alloc(1024, BF16)
        xdt = self.alloc(512, BF16)
        xdtd = self.alloc(512, BF16)
        Bt = self.alloc(256, BF16)
        yv = self.alloc(512)
        sq = self.alloc(512, BF16)
        sd = self.alloc(256)
        rstd = self.alloc(256)
        xc3 = f3(xc, 8)
        xcb3 = f3(xcb, 4)
        for t in range(NT):
            t0 = t * TT
            xi3 = xin[0].rearrange("p (k n) -> p k n", k=8)
            if t == 0:
                self.A('pool', lambda e: e.memset(xi3[:, :, 0:3], 0.0), w=['xin'])
                self.dma(xi3[:, :, 3:515], self.XBC[:, :, 0:TT].rearrange("k p n -> p k n"), w=['xin'])
            else:
                self.dma(xi3[:, :, 0:515], self.XBC[:, :, t0 - 3:t0 + TT].rearrange("k p n -> p k n"), w=['xin'])
            self.dma(f3(zt, 4), self.Z[:, :, t0:t0 + TT].rearrange("k p n -> p k n"), w=['zt'])
            for c in range(8):
                self.ts('dve', acc, xi3[:, c, 3:515], cw[:, 24 + c:25 + c], None, ALU.mult, None,
                        r=['xin', 'cw'], w=['acc'])
                for k in (2, 1, 0):
                    self.stt(acc, xi3[:, c, k:k + 512], cw[:, k * 8 + c:k * 8 + c + 1], acc, ALU.mult, ALU.add,
                             r=['xin', 'cw', 'acc'], w=['acc'])
                self.act(xc3[:, c, :], acc, AF.Silu, r=['acc', 'cb'], w=[('xc', c)], bias=cb[:, c:c + 1])
                if c >= 4:
                    self.copy('pool', xcb3[:, c - 4, :], xc3[:, c, :], r=[('xc', c)], w=[('xcb', c)])
            for sub in range(4):
                r0 = t0 + sub * 128
                tk = slice(sub * 128, (sub + 1) * 128)
                self.dma(dtr, self.DTT[r0:r0 + 128, :], w=['dtr'])
                self.tt('dve', dt, dtr, dtb, ALU.add, r=['dtr', 'dtb'], w=['dt'])
                self.act(dt, dt, AF.Exp, r=['dt'], w=['dt'])
                self.act(dt, dt, AF.Ln, r=['dt'], w=['dt'], bias=1.0)
                self.tt('dve', la, dt, abc, ALU.mult, r=['dt', 'abc'], w=['la'])
                bc_, pc = self.bank()
                self.mm(pc[:, 0:8], self.tri128, la, True, True, r=['c_tri128', 'la'], w=[('ps', bc_)])
                self.copy('dve', cum, pc[:, 0:8], r=[('ps', bc_)], w=['cum'])
                lt3 = f3(latri, 8)
                for r in range(8):
                    self.ts('pool' if r % 2 else 'dve', lt3[:, r, :], self.tri128, la[:, r:r + 1], None, ALU.mult, None,
                            r=['c_tri128', 'la'], w=[('latri', r)])
                b1, p1 = self.bank()
                b2, p2 = self.bank()
                self.mm(p1, self.onesf, latri[:, 0:512], True, True, r=['c_ones'] + [('latri', r) for r in range(4)],
                        w=[('ps', b1)])
                self.mm(p2, self.onesf, latri[:, 512:1024], True, True,
                        r=['c_ones'] + [('latri', r) for r in range(4, 8)], w=[('ps', b2)])
                pr = [f3(p1, 4), f3(p2, 4)]
                d3 = f3(dec, 8)
                e3 = f3(ecr, 8)
                for r in range(8):
                    self.ts('dve', d3[:, r, :], pr[r // 4][:, r % 4, :], cum[:, r:r + 1], 0.0, ALU.subtract, ALU.min,
                            r=[('ps', b1 if r < 4 else b2), 'cum'], w=[('dec', r // 4)])
                for hh in range(2):
                    self.act(dec[:, hh * 512:(hh + 1) * 512], dec[:, hh * 512:(hh + 1) * 512], AF.Exp,
                             r=[('dec', hh)], w=[('dec', hh)])
                    self.act(ecr[:, hh * 512:(hh + 1) * 512], [p1, p2][hh], AF.Exp, r=[('ps', [b1, b2][hh])],
                             w=[('ecr', hh)])
                    self.copy('dve', ecl[:, hh * 4:(hh + 1) * 4], e3[:, hh * 4:(hh + 1) * 4, 127], r=[('ecr', hh)],
                              w=['ecl'])
                    self.tt('dve', ds[:, hh * 4:(hh + 1) * 4], pr[hh][:, :, 127], cum[:, hh * 4:(hh + 1) * 4],
                            ALU.subtract, r=[('ps', [b1, b2][hh]), 'cum'], w=['ds'])
                self.act(ds, ds, AF.Exp, r=['ds'], w=['ds'])
                C3 = f3(Cp, 8)
                for g in range(2):
                    cin = xc3[:, 6 + g, tk].unsqueeze(1).broadcast_to([128, 4, 128])
                    self.tt('pool', C3[:, 4 * g:4 * g + 4, :], e3[:, 4 * g:4 * g + 4, :], cin, ALU.mult,
                            r=[('ecr', g), ('xc', 6 + g)], w=[('Cp', g)])
                bs, psc = self.bank()
                for g in range(2):
                    self.mm(psc[:, g * 128:(g + 1) * 128], xcb3[:, g, tk], xcb3[:, 2 + g, tk], True, True,
                            r=[('xcb', 4 + g), ('xcb', 6 + g)], w=[('ps', bs)])
                s3 = f3(scm, 2)
                self.tt('dve', s3, f3(psc[:, 0:256], 2), self.tri128.unsqueeze(1).broadcast_to([128, 2, 128]), ALU.mult,
                        r=[('ps', bs), 'c_tri128'], w=['scm'])
                S3 = f3(SdT, 8)
                for g in range(2):
                    self.tt('pool' if g else 'dve', S3[:, 4 * g:4 * g + 4, :], d3[:, 4 * g:4 * g + 4, :],
                            s3[:, g, :].unsqueeze(1).broadcast_to([128, 4, 128]), ALU.mult,
                            r=[('dec', g), 'scm'], w=[('SdT', g)])
                bx, px = self.bank()
                for c in range(4):
                    self.A('pe', lambda e, px=px, c=c, tk=tk: e.transpose(out=px[:, c * 128:(c + 1) * 128],
                                                                          in_=xc3[:, c, tk], identity=self.ident),
                           r=[('xc', c), 'ident'], w=[('ps', bx)])
                bb, pbt = self.bank()
                for c in range(2):
                    self.A('pe', lambda e, pbt=pbt, c=c, tk=tk: e.transpose(out=pbt[:, c * 128:(c + 1) * 128],
                                                                            in_=xc3[:, 4 + c, tk], identity=self.ident),
                           r=[('xc', 4 + c), 'ident'], w=[('ps', bb)])
                x3 = f3(xdt, 8)
                xd3 = f3(xdtd, 8)
                self.tt('dve', x3, f3(px, 8), dt.unsqueeze(2).broadcast_to([128, 8, 64]), ALU.mult,
                        r=[('ps', bx), 'dt'], w=['xdt'])
                self.tt('pool', xd3, x3, ds.unsqueeze(2).broadcast_to([128, 8, 64]), ALU.mult, r=['xdt', 'ds'],
                        w=['xdtd'])
                self.copy('act', Bt, pbt[:, 0:256], r=[('ps', bb)], w=['Bt'])
                by, py = self.bank()
                for i in range(4):
                    for r2 in range(2):
                        r = 2 * i + r2
                        g = r // 4
                        o_ = py[64 * r2:64 * r2 + 64, i * 128:(i + 1) * 128]
                        self.mm(o_, x3[:, r, :], S3[:, r, :], True, False, r=['xdt', ('SdT', g)], w=[('ps', by)])
                        self.mm(o_, hTb[g][:, (r % 4) * 64:(r % 4) * 64 + 64], C3[:, r, :], False, True,
                                r=[('hTb', g), ('Cp', g)], w=[('ps', by)])
                for g in range(2):
                    bst, pst = self.bank()
                    self.mm(pst[:, 0:256], Bt[:, g * 128:(g + 1) * 128], xdtd[:, g * 256:(g + 1) * 256], True, True,
                            r=['Bt', 'xdtd'], w=[('ps', bst)])
                    h3 = f3(hT[g], 4)
                    self.tt('dve', h3, h3, ecl[:, 4 * g:4 * g + 4].unsqueeze(2).broadcast_to([128, 4, 64]), ALU.mult,
                            r=[('hT', g), 'ecl'], w=[('hT', g)])
                    self.tt('dve', hT[g], hT[g], pst[:, 0:256], ALU.add, r=[('hT', g), ('ps', bst)], w=[('hT', g)])
                    self.copy('act', hTb[g], hT[g], r=[('hT', g)], w=[('hTb', g)])
                y3 = f3(yv, 4)
                z3 = f3(zt, 4)
                for i in range(4):
                    self.stt(y3[:, i, :], xc3[:, i, tk], dcol[:, i:i + 1], py[:, i * 128:(i + 1) * 128], ALU.mult,
                             ALU.add, r=[('xc', i), 'dcol', ('ps', by)], w=['yv'])
                self.tt('dve', y3, y3, z3[:, :, tk], ALU.mult, r=['yv', 'zt'], w=['yv'])
                self.act(sq, yv, AF.Square, r=['yv'], w=['sq'])
                bn, pn = self.bank()
                for g in range(2):
                    for j in range(2):
                        c = 2 * g + j
                        self.mm(pn[:, g * 128:(g + 1) * 128], self.ones_b, sq[:, c * 128:(c + 1) * 128], j == 0, j == 1,
                                r=['ones', 'sq'], w=[('ps', bn)])
                self.act(sd, pn[:, 0:256], AF.Sqrt, r=[('ps', bn)], w=['sd'], bias=EPS, scale=1.0 / 256)
                self.A('dve', lambda e: e.reciprocal(out=rstd, in_=sd), r=['sd'], w=['rstd'])
                yb3 = f3(ybuf, 4)
                for i in range(4):
                    self.stt(yb3[:, i, tk], y3[:, i, :], nw[:, i:i + 1], rstd[:, (i // 2) * 128:(i // 2 + 1) * 128],
                             ALU.mult, ALU.mult, r=['yv', 'nw', 'rstd'], w=['ybuf'])
            self.dma(self.Y[2, :, :, t0:t0 + TT].rearrange("k p n -> p k n"), f3(ybuf, 4), r=['ybuf'])
        self.release(m)

    def stage_gla(self, l):
        m = self.mark()
        P = self.prm
        f3 = lambda a, k: a.rearrange("p (k n) -> p k n", k=k)
        w2 = self.alloc(256)
        b2b = self.alloc(256)
        gnw = self.alloc(1)
        self.A('pool', lambda e: e.memset(w2, 0.0), w=['w2'])
        self.dma(w2[0:16, :], P["gla_gate_w2"][l], w=['w2'])
        self.dma(b2b, P["gla_gate_b2"][l].partition_broadcast(128), w=['b2b'], slow=True)
        self.dma(gnw, P["gla_norm"][l].rearrange("(p o) -> p o", o=1), w=['gnw'], slow=True)
        Sf = [self.alloc(128) for _ in range(2)]
        Sb = [self.alloc(128, BF16) for _ in range(2)]
        for kc in range(2):
            self.A('pool', lambda e, kc=kc: e.memset(Sf[kc], 0.0), w=[('Sf', kc)])
            self.A('pool', lambda e, kc=kc: e.memset(Sb[kc], 0.0), w=[('Sb', kc)])
        qt = self.alloc(2 * 512, BF16)
        kt_ = self.alloc(2 * 512, BF16)
        got = self.alloc(4 * 512, BF16)
        glr = self.alloc(512)
        self.A('pool', lambda e: e.memset(glr, 0.0), w=['glr'])
        ybuf = self.alloc(4 * 512, BF16)
        ktm = self.alloc(256, BF16)
        vtm = self.alloc(512, BF16)
        xb = self.alloc(256)
        loga = self.alloc(256)
        bsb = self.alloc(256)
        eb = self.alloc(256)
        enb = self.alloc(256)
        qe = self.alloc(256, BF16)
        qem = [self.alloc(256, BF16) for _ in range(2)]
        kem = [self.alloc(256, BF16) for _ in range(2)]
        kl = self.alloc(256)
        klm = [self.alloc(256, BF16) for _ in range(2)]
        ATs = [self.alloc(256, BF16) for _ in range(2)]
        for c2 in range(2):
            self.A('pool', lambda e, c2=c2: e.memset(ATs[c2], 0.0), w=[('AT', c2)])
        sq = self.alloc(512, BF16)
        sd = self.alloc(512)
        rstd = self.alloc(512)
        yg = self.alloc(512)
        q3, k3, go3, yb3 = f3(qt, 2), f3(kt_, 2), f3(got, 4), f3(ybuf, 4)
        eb3, enb3, qe3 = f3(eb, 2), f3(enb, 2), f3(qe, 2)
        qem3 = [f3(a, 2) for a in qem]
        kem3 = [f3(a, 2) for a in kem]
        for t in range(NT):
            t0 = t * TT
            self.dma(q3, self.Q[:, :, t0:t0 + TT].rearrange("k p n -> p k n"), w=['qt'])
            self.dma(k3, self.Kf[:, :, t0:t0 + TT].rearrange("k p n -> p k n"), w=['kt'])
            self.dma(go3, self.GO[:, :, t0:t0 + TT].rearrange("k p n -> p k n"), w=['got'])
            self.dma(glr[0:16, :], self.GLR[0, 0:16, t0:t0 + TT], w=['glr'])
            for sub in range(4):
                r0 = t0 + sub * 128
                tk = slice(sub * 128, (sub + 1) * 128)
                self.dma(ktm, self.KT[r0:r0 + 128, :], w=['ktm'])
                self.dma(vtm, self.VT[r0:r0 + 128, :], w=['vtm'])
                bx, px = self.bank()
                self.mm(px[:, 0:256], glr[:, tk], w2, True, True, r=['glr', 'w2'], w=[('ps', bx)])
                self.tt('dve', xb, px[:, 0:256], b2b, ALU.add, r=[('ps', bx), 'b2b'], w=['xb'])
                self.act(xb, xb, AF.Exp, r=['xb'], w=['xb'], scale=-1.0)
                self.act(xb, xb, AF.Ln, r=['xb'], w=['xb'], bias=1.0)
                self.ts('dve', loga, xb, -1.0 / 16.0, None, ALU.mult, None, r=['xb'], w=['loga'])
                bb_, pbm = self.bank()
                self.mm(pbm[:, 0:256], self.tri64, loga, True, True, r=['c_tri64', 'loga'], w=[('ps', bb_)])
                self.mm(pbm[:, 256:512], self.blk64, loga, True, True, r=['c_blk64', 'loga'], w=[('ps', bb_)])
                bf_, pbf = self.bank()
                for kc in range(2):
                    self.mm(pbf[:, kc * 128:(kc + 1) * 128], loga[:, kc * 128:(kc + 1) * 128], self.tri64, True, True,
                            r=['loga', 'c_tri64'], w=[('ps', bf_)])
                self.act(eb, pbf[:, 0:256], AF.Exp, r=[('ps', bf_)], w=['eb'])
                self.act(enb, pbf[:, 0:256], AF.Exp, r=[('ps', bf_)], w=['enb'], scale=-1.0)
                self.tt('dve', qe3, q3[:, :, tk], eb3, ALU.mult, r=['qt', 'eb'], w=['qe'])
                for h2 in range(2):
                    for kc in range(2):
                        self.stt(kem3[h2][:, kc, :], k3[:, kc, tk], self.hmask[:, h2:h2 + 1], enb3[:, kc, :], ALU.mult,
                                 ALU.mult, r=['kt', 'enb', 'c_hmask'], w=[('kem', h2)])
                    self.ts('pool', qem[h2], qe, self.hmask[:, h2:h2 + 1], None, ALU.mult, None,
                            r=['qe', 'c_hmask'], w=[('qem', h2)])
                self.copy('dve', bsb, pbm[:, 0:256], r=[('ps', bb_)], w=['bsb'])
                self.tt('dve', bsb, pbm[:, 256:512], bsb, ALU.subtract, r=[('ps', bb_), 'bsb'], w=['bsb'])
                self.act(bsb, bsb, AF.Exp, r=['bsb'], w=['bsb'])
                self.tt('dve', kl, ktm, bsb, ALU.mult, r=['ktm', 'bsb'], w=['kl'])
                for c2 in range(2):
                    self.ts('pool', klm[c2], kl, self.hmask[:, c2:c2 + 1], None, ALU.mult, None,
                            r=['kl', 'c_hmask'], w=[('klm', c2)])
                bo, po = self.bank()
                for c2 in range(2):
                    cs = slice(64 * c2, 64 * c2 + 64)
                    ba, pa = self.bank()
                    for hd in range(4):
                        kc, h2 = hd // 2, hd % 2
                        hs = slice(64 * h2, 64 * h2 + 64)
                        self.mm(pa[cs, hd * 64:(hd + 1) * 64], kem3[h2][:, kc, cs], qe3[:, kc, cs], True, True,
                                r=[('kem', h2), 'qe'], w=[('ps', ba)])
                    AT3 = f3(ATs[c2], 4)
                    self.tt('dve', AT3[cs, :, :], f3(pa[cs, 0:256], 4),
                            self.mask64[cs, :].unsqueeze(1).broadcast_to([64, 4, 64]), ALU.mult,
                            r=[('ps', ba), 'c_mask64'], w=[('AT', c2)])
                    for hd in range(4):
                        kc, h2 = hd // 2, hd % 2
                        hs = slice(64 * h2, 64 * h2 + 64)
                        o_ = po[:, hd * 128 + 64 * c2:hd * 128 + 64 * c2 + 64]
                        self.mm(o_, vtm[:, hd * 128:(hd + 1) * 128], AT3[:, hd, :], True, False,
                                r=['vtm', ('AT', c2)], w=[('ps', bo)])
                        self.mm(o_, Sb[kc], qem3[h2][:, kc, cs], False, True, r=[('Sb', kc), ('qem', h2)],
                                w=[('ps', bo)])
                    bs_, pS = self.bank()
                    for kc in range(2):
                        for h2 in range(2):
                            hd = 2 * kc + h2
                            self.mm(pS[64 * h2:64 * h2 + 64, kc * 128:(kc + 1) * 128],
                                    klm[c2][:, hd * 64:(hd + 1) * 64],
                                    vtm[:, hd * 128:(hd + 1) * 128], True, True, r=[('klm', c2), 'vtm'],
                                    w=[('ps', bs_)])
                        self.stt(Sf[kc], Sf[kc], eb3[:, kc, 64 * c2 + 63:64 * c2 + 64], pS[:, kc * 128:(kc + 1) * 128],
                                 ALU.mult, ALU.add, r=[('Sf', kc), 'eb', ('ps', bs_)], w=[('Sf', kc)])
                        self.copy('act', Sb[kc], Sf[kc], r=[('Sf', kc)], w=[('Sb', kc)])
                self.act(sq, po, AF.Square, r=[('ps', bo)], w=['sq'])
                bn, pn = self.bank()
                self.mm(pn, self.ones_b, sq, True, True, r=['ones', 'sq'], w=[('ps', bn)])
                self.act(sd, pn, AF.Sqrt, r=[('ps', bn)], w=['sd'], bias=EPS, scale=1.0 / 128)
                self.A('dve', lambda e: e.reciprocal(out=rstd, in_=sd), r=['sd'], w=['rstd'])
                self.stt(yg, po, gnw[:, 0:1], rstd, ALU.mult, ALU.mult, r=[('ps', bo), 'gnw', 'rstd'], w=['yg'])
                self.tt('dve', yb3[:, :, tk], f3(yg, 4), go3[:, :, tk], ALU.mult, r=['yg', 'got'], w=['ybuf'])
            self.dma(self.Y[1, :, :, t0:t0 + TT].rearrange("k p n -> p k n"), yb3, r=['ybuf'])
        self.release(m)

    def stage_s5(self, l):
        m = self.mark()
        P = self.prm
        PI = float(np.pi)
        f3 = lambda a, k: a.rearrange("p (k n) -> p k n", k=k)
        NS = 32
        nat = {n: self.alloc(64) for n in ["lr", "li", "x1", "th", "mag", "c", "s", "cc", "ss", "cs", "ar", "ai",
                                           "den", "nr", "t1", "t2"]}
        stc = self.alloc(1)
        ff = self.alloc(128)
        N = lambda n: nat[n][:NS, :]
        self.dma(N("lr"), P["s5_lam_re"][l], w=['n_lr'])
        self.dma(N("li"), P["s5_lam_im"][l], w=['n_li'])
        self.dma(stc[:NS, :], P["s5_log_step"][l].rearrange("(g o) -> g o", o=1), w=['stc'], slow=True)
        self.act(stc[:NS, :], stc[:NS, :], AF.Exp, r=['stc'], w=['stc'])
        self.ts('dve', N("lr"), N("lr"), -1e-4, None, ALU.min, None, r=['n_lr'], w=['n_lr'])
        self.ts('dve', N("x1"), N("lr"), stc[:NS, 0:1], None, ALU.mult, None, r=['n_lr', 'stc'], w=['n_x1'])
        self.ts('dve', N("th"), N("li"), stc[:NS, 0:1], None, ALU.mult, None, r=['n_li', 'stc'], w=['n_th'])
        self.act(N("mag"), N("x1"), AF.Exp, r=['n_x1'], w=['n_mag'])
        hp = self.alloc(1)
        self.A('pool', lambda e: e.memset(hp, PI / 2), w=['hp'])
        self.act(N("s"), N("th"), AF.Sin, r=['n_th'], w=['n_s'], scale=1.0 / 16)
        self.act(N("c"), N("th"), AF.Sin, r=['n_th', 'hp'], w=['n_c'], scale=1.0 / 16, bias=hp[:NS, 0:1])
        for it in range(4):
            self.tt('dve', N("cc"), N("c"), N("c"), ALU.mult, r=['n_c'], w=['n_cc'])
            self.tt('dve', N("ss"), N("s"), N("s"), ALU.mult, r=['n_s'], w=['n_ss'])
            self.tt('dve', N("cs"), N("c"), N("s"), ALU.mult, r=['n_c', 'n_s'], w=['n_cs'])
            self.tt('dve', N("c"), N("cc"), N("ss"), ALU.subtract, r=['n_cc', 'n_ss'], w=['n_c'])
            self.ts('dve', N("s"), N("cs"), 2.0, None, ALU.mult, None, r=['n_cs'], w=['n_s'])
        self.tt('dve', N("ar"), N("mag"), N("c"), ALU.mult, r=['n_mag', 'n_c'], w=['n_ar'])
        self.tt('dve', N("ai"), N("mag"), N("s"), ALU.mult, r=['n_mag', 'n_s'], w=['n_ai'])
        self.tt('dve', N("t1"), N("lr"), N("lr"), ALU.mult, r=['n_lr'], w=['n_t1'])
        self.tt('dve', N("t2"), N("li"), N("li"), ALU.mult, r=['n_li'], w=['n_t2'])
        self.tt('dve', N("den"), N("t1"), N("t2"), ALU.add, r=['n_t1', 'n_t2'], w=['n_den'])
        self.A('dve', lambda e: e.reciprocal(out=N("den"), in_=N("den")), r=['n_den'], w=['n_den'])
        self.ts('dve', N("nr"), N("ar"), -1.0, None, ALU.add, None, r=['n_ar'], w=['n_nr'])
        self.tt('dve', N("t1"), N("nr"), N("lr"), ALU.mult, r=['n_nr', 'n_lr'], w=['n_t1'])
        self.tt('dve', N("t2"), N("ai"), N("li"), ALU.mult, r=['n_ai', 'n_li'], w=['n_t2'])
        self.tt('dve', N("t1"), N("t1"), N("t2"), ALU.add, r=['n_t1', 'n_t2'], w=['n_t1'])
        self.tt('dve', ff[:NS, 0:64], N("t1"), N("den"), ALU.mult, r=['n_t1', 'n_den'], w=['ff'])
        self.tt('dve', N("t1"), N("ai"), N("lr"), ALU.mult, r=['n_ai', 'n_lr'], w=['n_t1'])
        self.tt('dve', N("t2"), N("nr"), N("li"), ALU.mult, r=['n_nr', 'n_li'], w=['n_t2'])
        self.tt('dve', N("t1"), N("t1"), N("t2"), ALU.subtract, r=['n_t1', 'n_t2'], w=['n_t1'])
        self.tt('dve', ff[:NS, 64:128], N("t1"), N("den"), ALU.mult, r=['n_t1', 'n_den'], w=['ff'])
        self.dma(self.S5T[0], N("mag"), r=['n_mag'], w=['S5T0'])
        self.dma(self.S5T[1], N("th"), r=['n_th'], w=['S5T1'])
        rho_c = self.alloc(16)
        th_c = self.alloc(16)
        for t2 in range(2):
            self.dma(rho_c[t2 * 64:(t2 + 1) * 64, :], self.S5T[0, t2::2, :].rearrange("j p -> p j"), r=['S5T0'],
                     w=['rho_c'], slow=True)
            self.dma(th_c[t2 * 64:(t2 + 1) * 64, :], self.S5T[1, t2::2, :].rearrange("j p -> p j"), r=['S5T1'],
                     w=['th_c'], slow=True)
        selb = self.alloc(512)
        self.dma(selb[:NS, :], self.cst["c_selb"], w=['selb'])
        bnat = [self.alloc(512) for _ in range(2)]
        self.dma(f3(bnat[0][:64, :], 32), P["s5_b_re"][l].rearrange("g p h -> p g h"), w=[('bnat', 0)])
        self.dma(f3(bnat[1][:64, :], 32), P["s5_b_im"][l].rearrange("g p h -> p g h"), w=[('bnat', 1)])
        BbT = self.alloc(16 * 2 * 128, BF16)
        BbT4 = BbT.rearrange("p (j c n) -> p j c n", j=16, c=2)
        fq = self.alloc(128)
        bq = [self.alloc(64) for _ in range(2)]
        bbq = [self.alloc(64) for _ in range(2)]
        tq = [self.alloc(64) for _ in range(2)]
        for q in range(4):
            b, pb = self.bank()
            self.mm(pb[:, 0:128], selb[:NS, q * 128:(q + 1) * 128], ff[:NS, :], True, True, r=['selb', 'ff'],
                    w=[('ps', b)])
            self.copy('act', fq, pb[:, 0:128], r=[('ps', b)], w=['fq'])
            b2, pb2 = self.bank()
            for c in range(2):
                self.A('pe', lambda e, pb2=pb2, c=c, q=q: e.transpose(out=pb2[:, c * 64:(c + 1) * 64],
                                                                      in_=bnat[c][:64, q * 128:(q + 1) * 128],
                                                                      identity=self.ident[:64, :64]),
                       r=[('bnat', c), 'ident'], w=[('ps', b2)])
                self.copy('act', bq[c], pb2[:, c * 64:(c + 1) * 64], r=[('ps', b2)], w=[('bq', c)])
            self.tt('dve', tq[0], fq[:, 0:64], bq[0], ALU.mult, r=['fq', ('bq', 0)], w=[('tq', 0)])
            self.tt('dve', tq[1], fq[:, 64:128], bq[1], ALU.mult, r=['fq', ('bq', 1)], w=[('tq', 1)])
            self.tt('dve', bbq[0], tq[0], tq[1], ALU.subtract, r=[('tq', 0), ('tq', 1)], w=[('bbq', 0)])
            self.tt('dve', tq[0], fq[:, 0:64], bq[1], ALU.mult, r=['fq', ('bq', 1)], w=[('tq', 0)])
            self.tt('dve', tq[1], fq[:, 64:128], bq[0], ALU.mult, r=['fq', ('bq', 0)], w=[('tq', 1)])
            self.tt('dve', bbq[1], tq[0], tq[1], ALU.add, r=[('tq', 0), ('tq', 1)], w=[('bbq', 1)])
            for gl in range(8):
                g = q * 8 + gl
                j, t2 = g // 2, g % 2
                for c in range(2):
                    self.ts('dve', BbT4[:, j, c, t2 * 64:(t2 + 1) * 64], bbq[c], self.gmask[:, gl:gl + 1], None,
                            ALU.mult, None, r=[('bbq', c), 'c_gmask'], w=['BbT'])
        CT = self.alloc(16 * 2 * 128, BF16)
        CT4 = CT.rearrange("p (j c n) -> p j c n", j=16, c=2)
        self.A('pool', lambda e: e.memset(CT, 0.0), w=['CT'])
        cdup = self.alloc(128)
        cT = self.alloc(128)
        for c, nm in enumerate(["s5_c_re", "s5_c_im"]):
            src = P[nm][l].rearrange("g h p -> (g h) p")
            for q in range(4):
                self.dma(cdup[:, 0:64], src[q * 128:(q + 1) * 128, :], w=['cdup'])
                self.dma(cdup[:, 64:128], src[q * 128:(q + 1) * 128, :], w=['cdup'])
                b, pb = self.bank()
                self.A('pe', lambda e, pb=pb: e.transpose(out=pb[:, 0:128], in_=cdup, identity=self.ident),
                       r=['cdup', 'ident'], w=[('ps', b)])
                self.act(cT, pb[:, 0:128], AF.Copy, r=[('ps', b)], w=['cT'], scale=(1.0 if c == 0 else -1.0))
                for gl in range(8):
                    g = q * 8 + gl
                    j, t2 = g // 2, g % 2
                    rs = slice(t2 * 64, (t2 + 1) * 64)
                    self.copy('dve' if gl % 2 else 'pool', CT4[rs, j, c, gl * 16:(gl + 1) * 16],
                              cT[rs, gl * 16:(gl + 1) * 16], r=['cT'], w=['CT'])
        cs_t = self.XNf[:, 0:8192]
        sn_t = self.XNf[:, 8192:16384]
        cs3, sn3 = f3(cs_t, 16), f3(sn_t, 16)
        iota = self.alloc(512)
        self.dma(iota, self.cst["c_iota"], w=['iota'])
        ph = self.alloc(512)
        kq = self.alloc(512)
        ki = self.alloc(512).bitcast(mybir.dt.int32)
        ta = self.alloc(512)
        tb = self.alloc(512)
        for j in range(16):
            self.ts('dve', ph, iota, th_c[:, j:j + 1], None, ALU.mult, None, r=['iota', 'th_c'], w=['ph'])
            self.ts('dve', kq, ph, 1.0 / (2 * PI), None, ALU.mult, None, r=['ph'], w=['kq'])
            self.copy('dve', ki, kq, r=['kq'], w=['ki'])
            self.copy('dve', kq, ki, r=['ki'], w=['kq'])
            self.stt(ph, kq, -2 * PI, ph, ALU.mult, ALU.add, r=['kq', 'ph'], w=['ph'])
            self.act(sn3[:, j, :], ph, AF.Sin, r=['ph'], w=[('sn', j)], scale=0.25)
            self.act(cs3[:, j, :], ph, AF.Sin, r=['ph', 'hp'], w=[('cs', j)], scale=0.25, bias=hp[:, 0:1])
            for it in range(2):
                self.tt('dve', ta, cs3[:, j, :], cs3[:, j, :], ALU.mult, r=[('cs', j)], w=['ta'])
                self.tt('pool', tb, sn3[:, j, :], sn3[:, j, :], ALU.mult, r=[('sn', j)], w=['tb'])
                self.tt('pool', kq, cs3[:, j, :], sn3[:, j, :], ALU.mult, r=[('cs', j), ('sn', j)], w=['kq'])
                self.tt('dve', cs3[:, j, :], ta, tb, ALU.subtract, r=['ta', 'tb'], w=[('cs', j)])
                self.ts('pool', sn3[:, j, :], kq, 2.0, None, ALU.mult, None, r=['kq'], w=[('sn', j)])
        e5r = self.alloc(16)
        e5i = self.alloc(16)
        t16a = self.alloc(16)
        t16b = self.alloc(16)
        allk = [('cs', j) for j in range(16)] + [('sn', j) for j in range(16)]
        self.tt('dve', t16a, cs3[:, :, 511], cs3[:, :, 1], ALU.mult, r=allk, w=['t16a'])
        self.tt('dve', t16b, sn3[:, :, 511], sn3[:, :, 1], ALU.mult, r=allk, w=['t16b'])
        self.tt('dve', e5r, t16a, t16b, ALU.subtract, r=['t16a', 't16b'], w=['e5r'])
        self.tt('dve', t16a, cs3[:, :, 511], sn3[:, :, 1], ALU.mult, r=allk, w=['t16a'])
        self.tt('dve', t16b, sn3[:, :, 511], cs3[:, :, 1], ALU.mult, r=allk, w=['t16b'])
        self.tt('dve', e5i, t16a, t16b, ALU.add, r=['t16a', 't16b'], w=['e5i'])
        dcol = self.alloc(4)
        self.dma(dcol, P["s5_d"][l].rearrange("(c p) -> p c", p=128), w=['dcol'], slow=True)
        stg = self.alloc(4 * 512)
        wgl = self.alloc(4 * 512, BF16)
        self.dma(f3(stg, 4), P["s5_glu"][l].rearrange("(k p) n -> p k n", p=128), w=['stg'])
        self.copy('pool', wgl, stg, r=['stg'], w=['wgl'])
        wgl3 = f3(wgl, 4)
        ut = self.alloc(4 * 512, BF16)
        ut3 = f3(ut, 4)
        p1, p2, p3, p4 = [self.alloc(512) for _ in range(4)]
        wr, wi = self.alloc(512), self.alloc(512)
        gr, gi = self.alloc(512), self.alloc(512)
        Hr, Hi = self.alloc(512, BF16), self.alloc(512, BF16)
        glr_, gli_ = self.alloc(16), self.alloc(16)
        inr, ini = self.alloc(16), self.alloc(16)
        yq = self.alloc(512)
        x2 = self.alloc(512)
        gq = self.alloc(4 * 512, BF16)
        gq3 = f3(gq, 4)
        sgl = self.alloc(512)
        ybuf = self.alloc(4 * 512, BF16)
        yb3 = f3(ybuf, 4)
        self.A('pool', lambda e: e.memset(inr, 0.0), w=['inr'])
        self.A('pool', lambda e: e.memset(ini, 0.0), w=['ini'])
        xbk = 0
        for t in range(NT):
            t0 = t * TT
            self.dma(ut3, self.US5[:, :, t0:t0 + TT].rearrange("k p n -> p k n"), w=['ut'])
            if t > 0:
                self.tt('dve', t16a, e5r, glr_, ALU.mult, r=['e5r', 'glr_'], w=['t16a'])
                self.tt('dve', t16b, e5i, gli_, ALU.mult, r=['e5i', 'gli_'], w=['t16b'])
                self.tt('dve', inr, t16a, t16b, ALU.subtract, r=['t16a', 't16b'], w=['inr'])
                self.tt('dve', t16a, e5r, gli_, ALU.mult, r=['e5r', 'gli_'], w=['t16a'])
                self.tt('dve', t16b, e5i, glr_, ALU.mult, r=['e5i', 'glr_'], w=['t16b'])
                self.tt('dve', ini, t16a, t16b, ALU.add, r=['t16a', 't16b'], w=['ini'])
            for j in range(16):
                q = j // 4
                bA = xbk % 6
                bB = (xbk + 1) % 6
                xbk += 2
                pA = self.ps[:, bA * 512:(bA + 1) * 512]
                pB = self.ps[:, bB * 512:(bB + 1) * 512]
                self.mm(pA, BbT4[:, j, 0, :], ut3[:, q, :], True, True, r=['BbT', 'ut'], w=[('ps', bA)])
                self.mm(pB, BbT4[:, j, 1, :], ut3[:, q, :], True, True, r=['BbT', 'ut'], w=[('ps', bB)])
                cj, sj = cs3[:, j, :], sn3[:, j, :]
                tabk = [('cs', j), ('sn', j)]
                self.tt('dve', p1, pA, cj, ALU.mult, r=[('ps', bA)] + tabk, w=['p1'])
                self.tt('dve', p2, pB, sj, ALU.mult, r=[('ps', bB)] + tabk, w=['p2'])
                self.tt('dve', p3, pB, cj, ALU.mult, r=[('ps', bB)] + tabk, w=['p3'])
                self.tt('dve', p4, pA, sj, ALU.mult, r=[('ps', bA)] + tabk, w=['p4'])
                self.tt('pool', wr, p1, p2, ALU.add, r=['p1', 'p2'], w=['wr'])
                self.tt('pool', wi, p3, p4, ALU.subtract, r=['p3', 'p4'], w=['wi'])
                rb = rho_c[:, j:j + 1].broadcast_to([128, 512])
                self.A('dve', lambda e, rb=rb, j=j: e.tensor_tensor_scan(out=gr, data0=rb, data1=wr,
                                                                          initial=inr[:, j:j + 1], op0=ALU.mult,
                                                                          op1=ALU.add),
                       r=['wr', 'rho_c', 'inr'], w=['gr'])
                self.A('dve', lambda e, rb=rb, j=j: e.tensor_tensor_scan(out=gi, data0=rb, data1=wi,
                                                                          initial=ini[:, j:j + 1], op0=ALU.mult,
                                                                          op1=ALU.add),
                       r=['wi', 'rho_c', 'ini'], w=['gi'])
                self.copy('act', glr_[:, j:j + 1], gr[:, 511:512], r=['gr'], w=['glr_'])
                self.copy('act', gli_[:, j:j + 1], gi[:, 511:512], r=['gi'], w=['gli_'])
                self.tt('pool', p1, gr, cj, ALU.mult, r=['gr'] + tabk, w=['p1'])
                self.tt('pool', p2, gi, sj, ALU.mult, r=['gi'] + tabk, w=['p2'])
                self.tt('pool', p3, gi, cj, ALU.mult, r=['gi'] + tabk, w=['p3'])
                self.tt('pool', p4, gr, sj, ALU.mult, r=['gr'] + tabk, w=['p4'])
                self.tt('dve', Hr, p1, p2, ALU.subtract, r=['p1', 'p2'], w=['Hr'])
                self.tt('dve', Hi, p3, p4, ALU.add, r=['p3', 'p4'], w=['Hi'])
                bY = 6 + q % 2
                pY = self.ps[:, bY * 512:(bY + 1) * 512]
                first = (j % 4 == 0)
                last = (j % 4 == 3)
                self.mm(pY, CT4[:, j, 0, :], Hr, first, False, r=['CT', 'Hr'], w=[('ps', bY)])
                self.mm(pY, CT4[:, j, 1, :], Hi, False, last, r=['CT', 'Hi'], w=[('ps', bY)])
                if last:
                    self.stt(yq, ut3[:, q, :], dcol[:, q:q + 1], pY, ALU.mult, ALU.add, r=['ut', 'dcol', ('ps', bY)],
                             w=['yq'])
                    self.tt('dve', x2, yq, yq, ALU.mult, r=['yq'], w=['x2'])
                    self.ts('dve', x2, x2, 0.044715, 1.0, ALU.mult, ALU.add, r=['x2'], w=['x2'])
                    self.tt('dve', x2, x2, yq, ALU.mult, r=['x2', 'yq'], w=['x2'])
                    self.act(x2, x2, AF.Tanh, r=['x2'], w=['x2'], scale=0.7978845608028654)
                    self.stt(x2, x2, 1.0, yq, ALU.add, ALU.mult, r=['x2', 'yq'], w=['x2'])
                    self.ts('dve', gq3[:, q, :], x2, 0.5, None, ALU.mult, None, r=['x2'], w=[('gq', q)])
            for mo in range(4):
                b, pb = self.bank()
                for q in range(4):
                    self.mm(pb, wgl3[:, q, mo * 128:(mo + 1) * 128], gq3[:, q, :], q == 0, q == 3,
                            r=['wgl', ('gq', q)], w=[('ps', b)])
                self.act(sgl, pb, AF.Sigmoid, r=[('ps', b)], w=['sgl'])
                self.tt('dve', yb3[:, mo, :], gq3[:, mo, :], sgl, ALU.mult, r=[('gq', mo), 'sgl'], w=['ybuf'])
            self.dma(self.Y[0, :, :, t0:t0 + TT].rearrange("k p n -> p k n"), yb3, r=['ybuf'])
        self.release(m)

    def stage_merge(self, l):
        m = self.mark()
        P = self.prm
        f3 = lambda a, k: a.rearrange("p (k n) -> p k n", k=k)
        stg = self.alloc(8 * 512)
        wbr = [self.alloc(4 * 1024, BF16) for _ in range(3)]
        wo = self.alloc(8 * 1024, BF16)
        for bi, nm in enumerate(["w_br_s5", "w_br_gla", "w_br_ssd"]):
            st = f3(stg[:, :4096], 4)
            self.dma(st, P[nm][l].rearrange("(k p) n -> p k n", p=128), w=['stg'])
            self.copy('pool', f3(wbr[bi], 4), st, r=['stg'], w=[('wbr', bi)])
        wo3 = f3(wo, 8)
        for half in range(2):
            st = f3(stg, 8)
            self.dma(st, P["w_out"][l][:, half * 512:(half + 1) * 512].rearrange("(k p) n -> p k n", p=128), w=['stg'])
            self.copy('pool', wo3[:, :, half * 512:(half + 1) * 512], st, r=['stg'], w=['wo'])
        yt = [self.alloc(4 * 512, BF16) for _ in range(3)]
        sg = self.alloc(24 * 512, BF16)
        hb = self.alloc(8 * 512)
        mg = self.alloc(512)
        tmpm = self.alloc(512)
        mgb = self.alloc(8 * 512, BF16)
        tmp = (self.alloc(8 * 512, BF16), self.alloc(512), self.alloc(512))
        wcol = self.alloc(8)
        kw = self.load_cols(wcol, P["ffn2_norm"][l], 8)
        sg3 = f3(sg, 24)
        for t in range(NT):
            t0 = t * TT
            for bi in range(3):
                self.dma(f3(yt[bi], 4), self.Y[bi, :, :, t0:t0 + TT].rearrange("k p n -> p k n"), w=[('yt', bi)])
            self.dma(sg3, self.SIG[:, :, t0:t0 + TT].rearrange("k p n -> p k n"), w=['sg'])
            hkeys = [('h', 0, k) for k in range(8)]
            self.dma(f3(hb, 8), self.H[:, :, t0:t0 + TT].rearrange("k p n -> p k n"), w=hkeys)
            for mo in range(8):
                for bi in range(3):
                    b, pb = self.bank()
                    y3 = f3(yt[bi], 4)
                    w3 = f3(wbr[bi], 4)
                    for kc in range(4):
                        self.mm(pb, w3[:, kc, mo * 128:(mo + 1) * 128], y3[:, kc, :], kc == 0, kc == 3,
                                r=[('wbr', bi), ('yt', bi)], w=[('ps', b)])
                    if bi == 0:
                        self.tt('dve', mg, pb, sg3[:, bi * 8 + mo, :], ALU.mult, r=[('ps', b), 'sg'], w=['mg'])
                    else:
                        self.tt('dve', tmpm, pb, sg3[:, bi * 8 + mo, :], ALU.mult, r=[('ps', b), 'sg'], w=['tmpm'])
                        self.tt('pool', mg, mg, tmpm, ALU.add, r=['mg', 'tmpm'], w=['mg'])
                self.copy('act', mgb[:, mo * 512:(mo + 1) * 512], mg, r=['mg'], w=[('mgb', mo)])
            for mo2 in range(8):
                b, pb = self.bank()
                for mo in range(8):
                    self.mm(pb, wo3[:, mo, mo2 * 128:(mo2 + 1) * 128], mgb[:, mo * 512:(mo + 1) * 512], mo == 0,
                            mo == 7, r=['wo', ('mgb', mo)], w=[('ps', b)])
                hk = hb[:, mo2 * 512:(mo2 + 1) * 512]
                self.tt('dve', hk, hk, pb, ALU.add, r=[hkeys[mo2], ('ps', b)], w=[hkeys[mo2]])
            self.dma(self.H[:, :, t0:t0 + TT].rearrange("k p n -> p k n"), f3(hb, 8), r=hkeys)
            self.rmsnorm_tile(hb, hkeys, wcol, kw, lambda k, t=t: self.xn_ap(k, t),
                              [('xn', k, t) for k in range(8)], tmp)
        self.release(m)

    def renorm(self, vec):
        m = self.mark()
        hb = [self.alloc(8 * 512) for _ in range(2)]
        tmp = (self.alloc(8 * 512, BF16), self.alloc(512), self.alloc(512))
        wcol = self.alloc(8)
        kw = self.load_cols(wcol, vec, 8)
        for t in range(NT):
            h = hb[t % 2]
            hkeys = [('h', t % 2, k) for k in range(8)]
            self.dma(h.rearrange("p (k n) -> p k n", k=8),
                     self.H[:, :, t * TT:(t + 1) * TT].rearrange("k p n -> p k n"), w=hkeys)
            self.rmsnorm_tile(h, hkeys, wcol, kw, lambda k, t=t: self.xn_ap(k, t),
                              [('xn', k, t) for k in range(8)], tmp)
        self.release(m)

    def stage_ple(self, l):
        m = self.mark()
        Wpg, Wpp = self.prm["ple_gate"][l], self.prm["ple_proj"][l]
        stg = [self.alloc(8 * 512)] * 2
        wg = self.alloc(8 * 1024, BF16)
        wp = self.alloc(2 * 1024, BF16)
        wg3 = wg.rearrange("p (k n) -> p k n", k=8)
        wp3 = wp.rearrange("p (k n) -> p k n", k=2)
        for half in range(2):
            st = stg[half][:, :8 * 512].rearrange("p (k n) -> p k n", k=8)
            self.dma(st, Wpg[:, half * 512:(half + 1) * 512].rearrange("(k p) n -> p k n", p=128),
                     w=[('stg', 0)])
            self.copy('pool', wg3[:, :, half * 512:(half + 1) * 512], st, r=[('stg', 0)], w=['wg'])
        st = stg[0][:, :2048].rearrange("p (k n) -> p k n", k=2)
        self.dma(st, Wpp.rearrange("(k p) n -> p k n", p=128), w=[('stg', 0)])
        self.copy('pool', wp3, st, r=[('stg', 0)], w=['wp'])
        pt = [[self.alloc(256) for _ in range(4)] for _ in range(2)]
        pf = [self.alloc(2 * 512, BF16) for _ in range(2)]
        hb = [self.alloc(8 * 512) for _ in range(2)]
        sg = [self.alloc(512) for _ in range(2)]
        tmp = (self.alloc(8 * 512, BF16), self.alloc(512), self.alloc(512))
        wcol = self.alloc(8)
        last = (l == DEPTH - 1)
        nxt = self.prm["final_norm"] if last else self.prm["ffn1_norm"][l + 1]
        kw = self.load_cols(wcol, nxt, 8)
        if last:
            yb = [self.alloc(8 * 512)] * 2
            ot = [self.alloc(1024) for _ in range(2)]
        it = 0
        for t in range(NT):
            h = hb[t % 2]
            hkeys = [('h', t % 2, k) for k in range(8)]
            self.dma(h.rearrange("p (k n) -> p k n", k=8),
                     self.H[:, :, t * TT:(t + 1) * TT].rearrange("k p n -> p k n"), w=hkeys)
            pb_ = pt[t % 2]
            for sub in range(4):
                r0 = t * TT + sub * 128
                self.dma(pb_[sub], self.p[l, r0:r0 + 128, :], w=[('pt', t % 2, sub)])
            pfb = pf[t % 2]
            for kc in range(2):
                b, pb = self.bank()
                for sub in range(4):
                    self.A('pe', lambda e, pb=pb, sub=sub, kc=kc, pb_=pb_: e.transpose(
                        out=pb[:, sub * 128:(sub + 1) * 128], in_=pb_[sub][:, kc * 128:(kc + 1) * 128],
                        identity=self.ident), r=[('pt', t % 2, sub), 'ident'], w=[('ps', b)])
                self.copy('act', pfb[:, kc * 512:(kc + 1) * 512], pb, r=[('ps', b)], w=[('pf', t % 2, kc)])
            for mo in range(8):
                bg, pg = self.bank()
                for k in range(8):
                    self.mm(pg, wg3[:, k, mo * 128:(mo + 1) * 128], self.xn_ap(k, t), k == 0, k == 7,
                            r=['wg', ('xn', k, t)], w=[('ps', bg)])
                bp, pp = self.bank()
                for kc in range(2):
                    self.mm(pp, wp3[:, kc, mo * 128:(mo + 1) * 128], pfb[:, kc * 512:(kc + 1) * 512],
                            kc == 0, kc == 1, r=['wp', ('pf', t % 2, kc)], w=[('ps', bp)])
                s_ = sg[it % 2]
                self.act(s_, pg, AF.Sigmoid, r=[('ps', bg)], w=[('sg', it % 2)])
                self.tt('dve', s_, s_, pp, ALU.mult, r=[('sg', it % 2), ('ps', bp)], w=[('sg', it % 2)])
                hk = h[:, mo * 512:(mo + 1) * 512]
                self.tt('pool', hk, hk, s_, ALU.add, r=[('sg', it % 2), hkeys[mo]], w=[hkeys[mo]])
                it += 1
            if not last:
                self.dma(self.H[:, :, t * TT:(t + 1) * TT].rearrange("k p n -> p k n"),
                         h.rearrange("p (k n) -> p k n", k=8), r=hkeys)
                self.rmsnorm_tile(h, hkeys, wcol, kw, lambda k, t=t: self.xn_ap(k, t),
                                  [('xn', k, t) for k in range(8)], tmp)
            else:
                y = yb[t % 2]
                ykeys = [('y', 0, k) for k in range(8)]
                self.rmsnorm_tile(h, hkeys, wcol, kw, lambda k, y=y: y[:, k * 512:(k + 1) * 512], ykeys, tmp)
                for sub in range(4):
                    ob = ot[sub % 2]
                    for half in range(2):
                        b, pb = self.bank()
                        for kk in range(4):
                            k = half * 4 + kk
                            self.A('pe', lambda e, pb=pb, kk=kk, k=k, sub=sub, y=y: e.transpose(
                                out=pb[:, kk * 128:(kk + 1) * 128],
                                in_=y[:, k * 512 + sub * 128:k * 512 + (sub + 1) * 128], identity=self.ident),
                                r=[ykeys[k], 'ident'], w=[('ps', b)])
                        self.copy('act' if half == 0 else 'dve', ob[:, half * 512:(half + 1) * 512], pb,
                                  r=[('ps', b)], w=[('ot', sub % 2, half)])
                    r0 = t * TT + sub * 128
                    self.dma(self.out[r0:r0 + 128, :], ob, r=[('ot', sub % 2, 0), ('ot', sub % 2, 1)])
        self.release(m)

    def stage_final(self):
        pass


_NC_CACHE = {}


def kernel(**inputs):
    if "nc" not in _NC_CACHE:
        nc = bass.Bass("TRN2", target_bir_lowering=False)
        kb = KB(nc)
        kb.build()
        _NC_CACHE["nc"] = nc
    nc = _NC_CACHE["nc"]
    consts = host_consts()
    x = np.ascontiguousarray(inputs["x"], dtype=np.float32)
    p = np.ascontiguousarray(inputs["p"], dtype=np.float32)
    in_maps = []
    for c in range(8):
        m = {"x": x[c], "p": np.ascontiguousarray(p[:, c])}
        for n in PARAM_NAMES:
            m[n] = np.ascontiguousarray(inputs[n], dtype=np.float32)
        m.update(consts)
        in_maps.append(m)
    res = run_bass_kernel_spmd(nc, in_maps, core_ids=list(range(8)))
    return np.stack([np.asarray(res.results[c]["out"]) for c in range(8)], axis=0).astype(np.float32)
```

```python
import contextlib
import numpy as np
import concourse.bass as bass
import concourse.mybir as mybir
from concourse.alu_op_type import AluOpType as ALU
from concourse.bass_utils import run_bass_kernel_spmd

F32 = mybir.dt.float32
BF16 = mybir.dt.bfloat16
AF = mybir.ActivationFunctionType

S = 4096
D = 1024
DFF = 2752
DEPTH = 2
TT = 512
NT = S // TT
IN_TOTAL = 6680
EPS = 1e-6

STREAMS = ['pe', 'act', 'dve', 'pool', 'sp']
NCH = 8
SEM_ROLL = 12000


class Sched:
    def __init__(self):
        self.ops = []
        self.ns = None

    @staticmethod
    def _nk(k, ns):
        if isinstance(k, tuple) and k[0] == 'ps':
            return k
        if isinstance(k, str) and (k.startswith('c_') or k in ('ident', 'ones')):
            return k
        return (ns, k)

    def add(self, eng, fn, reads=(), writes=(), dma=False):
        if self.ns is not None:
            reads = tuple(self._nk(k, self.ns) for k in reads)
            writes = tuple(self._nk(k, self.ns) for k in writes)
        self.ops.append(dict(eng=eng, fn=fn, reads=tuple(reads), writes=tuple(writes),
                             dma=dma, barrier=False))

    def barrier(self):
        self.ops.append(dict(barrier=True))

    def analyze(self):
        last_w = {}
        readers = {}
        last_on_stream = {}
        last_on_chan = {}
        pending = {s: set() for s in STREAMS}
        ch_rr = 0
        for i, op in enumerate(self.ops):
            if op['barrier']:
                allp = set(last_on_stream.values()) | set(last_on_chan.values())
                for s in STREAMS:
                    pending[s] |= allp
                last_w = {}
                readers = {}
                continue
            deps = {}
            eng = op['eng']
            for r in op['reads']:
                j = last_w.get(r)
                if j is not None:
                    deps[j] = 'RAW'
            for w in op['writes']:
                j = last_w.get(w)
                if j is not None and j not in deps:
                    deps[j] = 'WAW'
                for j in readers.get(w, ()):
                    if j not in deps:
                        deps[j] = 'WAR'
            for j in pending[eng]:
                if j not in deps:
                    deps[j] = 'BAR'
            pending[eng] = set()
            if op['dma']:
                op['chan'] = ch_rr
                ch_rr = (ch_rr + 1) % NCH
                j = last_on_chan.get(op['chan'])
                if j is not None:
                    deps[j] = 'BAR'
                last_on_chan[op['chan']] = i
            else:
                last_on_stream[eng] = i
            fdeps = []
            for j, kind in deps.items():
                pj = self.ops[j]
                if (not pj['dma']) and (not op['dma']) and pj['eng'] == eng:
                    if eng == 'pe' or kind != 'RAW':
                        continue
                fdeps.append(j)
            op['deps'] = fdeps
            for j in fdeps:
                self.ops[j]['needs_inc'] = True
            for r in op['reads']:
                readers.setdefault(r, []).append(i)
            for w in op['writes']:
                last_w[w] = i
                readers[w] = []
        self.last_on_chan = last_on_chan

    def emit(self, nc):
        self.analyze()
        with contextlib.ExitStack() as es:
            def newsem(name):
                return es.enter_context(nc.semaphore(name))

            cur = {s: [newsem(f"s_{s}_0"), 0, 0] for s in ['pe', 'act', 'dve', 'pool']}
            chs = [[newsem(f"s_ch{c}_0"), 0, 0] for c in range(NCH)]
            for op in self.ops:
                if op['barrier']:
                    continue
                if op['dma']:
                    st = chs[op['chan']]
                    nm = f"s_ch{op['chan']}"
                    inc = 16
                elif op.get('needs_inc'):
                    st = cur[op['eng']]
                    nm = f"s_{op['eng']}"
                    inc = 1
                else:
                    continue
                if st[1] + inc > SEM_ROLL:
                    st[2] += 1
                    st[0] = newsem(f"{nm}_{st[2]}")
                    st[1] = 0
                st[1] += inc
                op['sem'], op['val'], op['inc'] = st[0], st[1], inc
            lists = {s: [] for s in STREAMS}
            waited = {s: {} for s in STREAMS}
            for op in self.ops:
                if op['barrier']:
                    continue
                s = op['eng']
                for j in op['deps']:
                    pj = self.ops[j]
                    key = id(pj['sem'])
                    if waited[s].get(key, 0) >= pj['val']:
                        continue
                    waited[s][key] = pj['val']
                    lists[s].append(('wait', pj['sem'], pj['val']))
                lists[s].append(('op', op))
            for c, j in self.last_on_chan.items():
                pj = self.ops[j]
                lists['sp'].append(('wait', pj['sem'], pj['val']))

            def run(e, items):
                for it in items:
                    if it[0] == 'wait':
                        e.wait_ge(it[1], it[2])
                    else:
                        op = it[1]
                        ins = op['fn'](e)
                        if 'sem' in op:
                            ins.then_inc(op['sem'], op['inc'])

            with nc.Block() as block:
                @block.tensor
                def _(e):
                    run(e, lists['pe'])

                @block.scalar
                def _(e):
                    run(e, lists['act'])

                @block.vector
                def _(e):
                    run(e, lists['dve'])

                @block.gpsimd
                def _(e):
                    run(e, lists['pool'])

                @block.sync
                def _(e):
                    run(e, lists['sp'])


PARAM_NAMES = ["ffn1_norm", "ffn1_gate", "ffn1_up", "ffn1_down", "mix_norm", "w_in",
               "s5_lam_re", "s5_lam_im", "s5_log_step", "s5_b_re", "s5_b_im", "s5_c_re", "s5_c_im",
               "s5_d", "s5_glu", "gla_gate_w2", "gla_gate_b2", "gla_norm", "ssd_conv_w", "ssd_conv_b",
               "ssd_dt_bias", "ssd_a_log", "ssd_d", "ssd_norm", "w_br_s5", "w_br_gla", "w_br_ssd",
               "w_out", "ffn2_norm", "ffn2_gate", "ffn2_up", "ffn2_down", "ple_norm", "ple_gate",
               "ple_proj", "final_norm"]
PARAM_SHAPES = {
    "ffn1_norm": (2, 1024), "ffn1_gate": (2, 1024, 2752), "ffn1_up": (2, 1024, 2752),
    "ffn1_down": (2, 2752, 1024), "mix_norm": (2, 1024), "w_in": (2, 1024, 6680),
    "s5_lam_re": (2, 32, 64), "s5_lam_im": (2, 32, 64), "s5_log_step": (2, 32),
    "s5_b_re": (2, 32, 64, 16), "s5_b_im": (2, 32, 64, 16), "s5_c_re": (2, 32, 16, 64),
    "s5_c_im": (2, 32, 16, 64), "s5_d": (2, 512), "s5_glu": (2, 512, 512),
    "gla_gate_w2": (2, 16, 256), "gla_gate_b2": (2, 256), "gla_norm": (2, 128),
    "ssd_conv_w": (2, 4, 1024), "ssd_conv_b": (2, 1024), "ssd_dt_bias": (2, 8), "ssd_a_log": (2, 8),
    "ssd_d": (2, 8), "ssd_norm": (2, 512), "w_br_s5": (2, 512, 1024), "w_br_gla": (2, 512, 1024),
    "w_br_ssd": (2, 512, 1024), "w_out": (2, 1024, 1024), "ffn2_norm": (2, 1024),
    "ffn2_gate": (2, 1024, 2752), "ffn2_up": (2, 1024, 2752), "ffn2_down": (2, 2752, 1024),
    "ple_norm": (2, 1024), "ple_gate": (2, 1024, 1024), "ple_proj": (2, 256, 1024),
    "final_norm": (1024,),
}


def host_consts():
    c = {}
    c["c_ident"] = np.eye(128, dtype=np.float32)
    i = np.arange(128)
    c["c_tri128"] = (i[:, None] <= i[None, :]).astype(np.float32)
    same = (i[:, None] // 64) == (i[None, :] // 64)
    c["c_tri64"] = ((i[:, None] <= i[None, :]) & same).astype(np.float32)
    c["c_blk64"] = same.astype(np.float32)
    c["c_mask64"] = ((i[:, None] % 64) <= np.arange(64)[None, :]).astype(np.float32)
    c["c_ones"] = np.ones((128, 128), np.float32)
    g = np.arange(512) // 16
    c["c_selb"] = (np.arange(32)[:, None] == g[None, :]).astype(np.float32)
    c["c_gmask"] = ((i[:, None] // 16) == np.arange(8)[None, :]).astype(np.float32)
    c["c_hmask"] = ((i[:, None] // 64) == np.arange(2)[None, :]).astype(np.float32)
    c["c_iota"] = np.tile(np.arange(512, dtype=np.float32)[None, :], (128, 1))
    return c


class KB:
    def __init__(self, nc, debug=False, stages=None):
        self.nc = nc
        self.sc = Sched()
        self.debug = debug
        self.stages = stages
        self.pbank = 0
        self.uid = 0

    def alloc(self, n, dt=F32):
        if dt == BF16:
            m = (n + 1) // 2
            a = self.arena[:, self.off:self.off + m].bitcast(BF16)
        else:
            m = n
            a = self.arena[:, self.off:self.off + m]
        self.off += m
        assert self.off <= self.arena_n, f"arena overflow {self.off}"
        return a

    def mark(self):
        return self.off

    def release(self, m):
        self.off = m
        self.sc.barrier()

    def bank(self):
        b = self.pbank % 8
        self.pbank += 1
        return b, self.ps[:, b * 512:(b + 1) * 512]

    def key(self, name):
        self.uid += 1
        return (name, self.uid)

    def dram(self, name, shape, dt):
        kind = "ExternalOutput" if (self.debug and name in self.debug) else "Internal"
        return self.nc.dram_tensor(name, list(shape), dt, kind=kind).ap()

    def A(self, eng, fn, r=(), w=()):
        self.sc.add(eng, fn, r, w)

    def dma(self, out, in_, r=(), w=(), slow=False):
        if slow:
            self.sc.add('sp', lambda e: e.dma_start(out=out, in_=in_, allow_slow_non_contiguous=True), r, w, dma=True)
        else:
            self.sc.add('sp', lambda e: e.dma_start(out=out, in_=in_), r, w, dma=True)

    def mm(self, out, lhsT, rhs, start, stop, r, w):
        self.sc.add('pe', lambda e: e.matmul(out, lhsT=lhsT, rhs=rhs, start=start, stop=stop), r, w)

    def act(self, out, in_, func, r, w, bias=None, scale=None):
        kw = {}
        if bias is not None:
            kw['bias'] = bias
        if scale is not None:
            kw['scale'] = scale
        self.sc.add('act', lambda e: e.activation(out=out, in_=in_, func=func, **kw), r, w)

    def tt(self, eng, out, in0, in1, op, r, w):
        self.sc.add(eng, lambda e: e.tensor_tensor(out=out, in0=in0, in1=in1, op=op), r, w)

    def ts(self, eng, out, in0, s1, s2, op0, op1, r, w):
        if op1 is None:
            self.sc.add(eng, lambda e: e.tensor_scalar(out=out, in0=in0, scalar1=s1, scalar2=None, op0=op0), r, w)
        else:
            self.sc.add(eng, lambda e: e.tensor_scalar(out=out, in0=in0, scalar1=s1, scalar2=s2, op0=op0, op1=op1), r, w)

    def stt(self, out, in0, scalar, in1, op0, op1, r, w):
        self.sc.add('dve', lambda e: e.scalar_tensor_tensor(out=out, in0=in0, scalar=scalar, in1=in1,
                                                           op0=op0, op1=op1), r, w)

    def copy(self, eng, out, in_, r, w):
        if eng == 'act':
            self.sc.add('act', lambda e: e.activation(out=out, in_=in_, func=AF.Copy), r, w)
        else:
            self.sc.add(eng, lambda e: e.tensor_copy(out=out, in_=in_), r, w)

    def load_cols(self, dst, vec_ap, nk):
        k = self.key('col')
        self.dma(dst, vec_ap.rearrange("(k p) -> p k", p=128), w=[k], slow=True)
        return k

    def load_weight(self, w_ap, kc_sizes, c0, ncols, stg, wb, kstg, kwb, cast_eng='pool'):
        KC = len(kc_sizes)
        full = [i for i, s in enumerate(kc_sizes) if s == 128]
        nf = len(full)
        stg3 = stg[:, :KC * ncols].rearrange("p (k n) -> p k n", k=KC)
        wb3 = wb[:, :KC * ncols].rearrange("p (k n) -> p k n", k=KC)
        if nf > 0:
            self.dma(stg3[:, :nf, :], w_ap[0:nf * 128, c0:c0 + ncols].rearrange("(k p) n -> p k n", p=128),
                     w=[kstg])
            self.copy(cast_eng, wb3[:, :nf, :], stg3[:, :nf, :], r=[kstg], w=[kwb])
        if nf < KC:
            rem = kc_sizes[-1]
            self.dma(stg3[:rem, nf, :], w_ap[nf * 128:nf * 128 + rem, c0:c0 + ncols], w=[kstg])
            self.copy(cast_eng, wb3[:rem, nf, :], stg3[:rem, nf, :], r=[kstg], w=[kwb])
        return wb3

    def rmsnorm_tile(self, h, hkeys, wcol, kw, xn_out, xnkeys, tmp):
        sq, sd, rstd = tmp
        ksq = [self.key('sq') for _ in range(8)]
        for k in range(8):
            self.act(sq[:, k * 512:(k + 1) * 512], h[:, k * 512:(k + 1) * 512], AF.Square,
                     r=[hkeys[k]], w=[ksq[k]])
        b, pb = self.bank()
        for k in range(8):
            self.mm(pb, self.ones_b, sq[:, k * 512:(k + 1) * 512], k == 0, k == 7,
                    r=[ksq[k], 'ones'], w=[('ps', b)])
        ksd = self.key('sd')
        self.act(sd, pb, AF.Sqrt, r=[('ps', b)], w=[ksd], bias=EPS, scale=1.0 / D)
        krs = self.key('rstd')
        self.A('dve', lambda e: e.reciprocal(out=rstd, in_=sd), r=[ksd], w=[krs])
        for k in range(8):
            eng = 'dve'
            self.stt(xn_out(k), h[:, k * 512:(k + 1) * 512], wcol[:, k:k + 1], rstd, ALU.mult, ALU.mult,
                     r=[hkeys[k], krs, kw], w=[xnkeys[k]])

    def build(self):
        nc = self.nc
        self.x = nc.dram_tensor("x", [S, D], F32, kind="ExternalInput").ap()
        self.p = nc.dram_tensor("p", [DEPTH, S, 256], F32, kind="ExternalInput").ap()
        self.prm = {}
        for n in PARAM_NAMES:
            self.prm[n] = nc.dram_tensor(n, list(PARAM_SHAPES[n]), F32, kind="ExternalInput").ap()
        self.cst = {}
        for n, v in host_consts().items():
            self.cst[n] = nc.dram_tensor(n, list(v.shape), F32, kind="ExternalInput").ap()
        self.out = nc.dram_tensor("out", [S, D], F32, kind="ExternalOutput").ap()
        self.H = self.dram("H", [8, 128, S], F32)
        self.HID = self.dram("HID", [22, 128, S], BF16)
        self.US5 = self.dram("US5", [4, 128, S], BF16)
        self.Q = self.dram("Q", [2, 128, S], BF16)
        self.Kf = self.dram("Kf", [2, 128, S], BF16)
        self.KT = self.dram("KT", [S, 256], BF16)
        self.VT = self.dram("VT", [S, 512], BF16)
        self.GO = self.dram("GO", [4, 128, S], BF16)
        self.GLR = self.dram("GLR", [1, 128, S], F32)
        self.Z = self.dram("Z", [4, 128, S], BF16)
        self.XBC = self.dram("XBC", [8, 128, S], F32)
        self.DTT = self.dram("DTT", [S, 8], F32)
        self.SIG = self.dram("SIG", [24, 128, S], BF16)
        self.Y = self.dram("Y", [3, 4, 128, S], BF16)
        self.S5T = self.dram("S5T", [2, 32, 64], F32)
        self.arena_n = 52000
        with nc.sbuf_tensor("arena", [128, self.arena_n], F32) as arena, \
                nc.psum_tensor("ps", [128, 4096], F32) as ps:
            self.arena = arena
            self.ps = ps
            self.off = 0
            self.ident = self.alloc(128)
            self.ones_b = self.alloc(128, BF16)
            self.eps_col = self.alloc(1)
            self.dma(self.ident, self.cst["c_ident"], w=['ident'])
            self.A('pool', lambda e: e.memset(self.ones_b, 1.0), w=['ones'])
            self.A('pool', lambda e: e.memset(self.eps_col, EPS), w=['eps'])
            self.alloc_consts()
            self.XNf = self.arena[:, self.off:self.off + 4 * S]
            self.XN = self.alloc(8 * S, BF16)
            self.base = self.mark()
            self.sc.barrier()

            self.stage_input()
            if self.debug:
                self.stage_ffn(0, 1)
                self.stage_mixer(0)
            else:
                for l in range(DEPTH):
                    self.stage_ffn(l, 1)
                    self.stage_mixer(l)
                    self.stage_ffn(l, 2)
                    self.stage_ple(l)
            self.sc.emit(nc)
        return nc

    def xn_ap(self, k, t):
        return self.XN[:, k * S + t * TT:k * S + (t + 1) * TT]

    def stage_input(self):
        m = self.mark()
        xt = [[self.alloc(1024) for _ in range(4)] for _ in range(2)]
        hb = [self.alloc(8 * 512) for _ in range(2)]
        tmp = (self.alloc(8 * 512, BF16), self.alloc(512), self.alloc(512))
        wcol = self.alloc(8)
        kw = self.load_cols(wcol, self.prm["ffn1_norm"][0], 8)
        for t in range(NT):
            xb = xt[t % 2]
            h = hb[t % 2]
            for sub in range(4):
                r0 = t * TT + sub * 128
                self.dma(xb[sub], self.x[r0:r0 + 128, :], w=[('xt', t % 2, sub)])
            hkeys = [('h', t % 2, k) for k in range(8)]
            for k in range(8):
                b, pb = self.bank()
                for sub in range(4):
                    self.A('pe', lambda e, pb=pb, sub=sub, k=k, xb=xb: e.transpose(
                        out=pb[:, sub * 128:(sub + 1) * 128], in_=xb[sub][:, k * 128:(k + 1) * 128],
                        identity=self.ident), r=[('xt', t % 2, sub), 'ident'], w=[('ps', b)])
                self.copy('act' if k % 2 == 0 else 'dve', h[:, k * 512:(k + 1) * 512], pb,
                          r=[('ps', b)], w=[hkeys[k]])
            self.dma(self.H[:, :, t * TT:(t + 1) * TT].rearrange("k p n -> p k n"),
                     h.rearrange("p (k n) -> p k n", k=8), r=hkeys)
            self.rmsnorm_tile(h, hkeys, wcol, kw, lambda k, t=t: self.xn_ap(k, t),
                              [('xn', k, t) for k in range(8)], tmp)
        self.release(m)

    def stage_ffn(self, l, which):
        pre = f"ffn{which}_"
        Wg, Wu, Wd = self.prm[pre + "gate"][l], self.prm[pre + "up"][l], self.prm[pre + "down"][l]
        m0 = self.mark()
        kc_sizes = [128] * 21 + [64]
        wd = self.alloc(22 * 1024, BF16)
        wd3 = wd.rearrange("p (k n) -> p k n", k=22)
        m = self.mark()
        stgd = self.alloc(8 * 512)
        stg = [self.alloc(8 * 512) for _ in range(2)]
        wgb = [self.alloc(8 * 512, BF16) for _ in range(2)]
        wub = [self.alloc(8 * 512, BF16) for _ in range(2)]
        sil = [self.alloc(512) for _ in range(2)]
        hid = [self.alloc(512, BF16) for _ in range(4)]
        groups = [(c0, min(512, DFF - c0)) for c0 in range(0, DFF, 512)]
        loaded = {}

        def issue_load(gi):
            c0, ncols = groups[gi]
            s_ = gi % 2
            g3 = self.load_weight(Wg, [128] * 8, c0, ncols, stg[0], wgb[s_], ('stg', 0), ('wgb', s_))
            u3 = self.load_weight(Wu, [128] * 8, c0, ncols, stg[1], wub[s_], ('stg', 1), ('wub', s_))
            loaded[gi] = (g3, u3)

        def issue_wd(q):
            k0 = q * 4
            nk = min(4, 22 - k0)
            st = stgd[:, :nk * 1024].rearrange("p (k n) -> p k n", k=nk)
            for kk in range(nk):
                rows = kc_sizes[k0 + kk]
                self.dma(st[:rows, kk, :], Wd[(k0 + kk) * 128:(k0 + kk) * 128 + rows, :], w=['stgd'])
            if k0 + nk == 22:
                self.copy('pool', wd3[:, k0:k0 + nk - 1, :], st[:, :nk - 1, :], r=['stgd'], w=['wd'])
                self.copy('pool', wd3[:64, 21, :], st[:64, nk - 1, :], r=['stgd'], w=['wd'])
            else:
                self.copy('pool', wd3[:, k0:k0 + nk, :], st, r=['stgd'], w=['wd'])

        issue_load(0)
        it = 0
        hi = 0
        for gi, (c0, ncols) in enumerate(groups):
            s = gi % 2
            if gi + 1 < len(groups):
                issue_load(gi + 1)
            issue_wd(gi)
            g3, u3 = loaded[gi]
            for t in range(NT):
                for mc in range((ncols + 127) // 128):
                    mw = min(128, ncols - mc * 128)
                    j = (c0 // 128) + mc
                    bg, pg = self.bank()
                    for k in range(8):
                        self.mm(pg[:mw, :], g3[:, k, mc * 128:mc * 128 + mw], self.xn_ap(k, t), k == 0, k == 7,
                                r=[('wgb', s), ('xn', k, t)], w=[('ps', bg)])
                    bu, pu = self.bank()
                    for k in range(8):
                        self.mm(pu[:mw, :], u3[:, k, mc * 128:mc * 128 + mw], self.xn_ap(k, t), k == 0, k == 7,
                                r=[('wub', s), ('xn', k, t)], w=[('ps', bu)])
                    sl = sil[it % 2]
                    hd = hid[hi % 4]
                    self.act(sl[:mw, :], pg[:mw, :], AF.Silu, r=[('ps', bg)], w=[('sil', it % 2)])
                    self.tt('dve', hd[:mw, :], sl[:mw, :], pu[:mw, :], ALU.mult,
                            r=[('sil', it % 2), ('ps', bu)], w=[('hid', hi % 4)])
                    self.dma(self.HID[j, :mw, t * TT:(t + 1) * TT], hd[:mw, :], r=[('hid', hi % 4)],
                             w=[('HID', j, t)])
                    it += 1
                    hi += 1
        assert len(groups) == 6
        self.release(m)
        hidt = [self.alloc(22 * 512, BF16) for _ in range(2)]
        hb = [self.alloc(8 * 512) for _ in range(2)]
        tmp = (self.alloc(8 * 512, BF16), self.alloc(512), self.alloc(512))
        wcol = self.alloc(8)
        nxt = self.prm["mix_norm"][l] if which == 1 else self.prm["ple_norm"][l]
        kw = self.load_cols(wcol, nxt, 8)

        def issue_tile_loads(t):
            ht3 = hidt[t % 2].rearrange("p (k n) -> p k n", k=22)
            self.dma(ht3[:, :21, :], self.HID[0:21, :, t * TT:(t + 1) * TT].rearrange("k p n -> p k n"),
                     w=[('hidt', t % 2)])
            self.dma(ht3[:64, 21, :], self.HID[21, :64, t * TT:(t + 1) * TT], w=[('hidt', t % 2)])
            self.dma(hb[t % 2].rearrange("p (k n) -> p k n", k=8),
                     self.H[:, :, t * TT:(t + 1) * TT].rearrange("k p n -> p k n"),
                     w=[('h', t % 2, k) for k in range(8)])

        issue_tile_loads(0)
        for t in range(NT):
            if t + 1 < NT:
                issue_tile_loads(t + 1)
            ht3 = hidt[t % 2].rearrange("p (k n) -> p k n", k=22)
            h = hb[t % 2]
            hkeys = [('h', t % 2, k) for k in range(8)]
            for mo in range(8):
                b, pb = self.bank()
                for j in range(22):
                    rows = kc_sizes[j]
                    self.mm(pb, wd3[:rows, j, mo * 128:(mo + 1) * 128], ht3[:rows, j, :], j == 0, j == 21,
                            r=['wd', ('hidt', t % 2)], w=[('ps', b)])
                hk = h[:, mo * 512:(mo + 1) * 512]
                self.stt(hk, pb, 0.5, hk, ALU.mult, ALU.add, r=[('ps', b), hkeys[mo]], w=[hkeys[mo]])
            self.dma(self.H[:, :, t * TT:(t + 1) * TT].rearrange("k p n -> p k n"),
                     h.rearrange("p (k n) -> p k n", k=8), r=hkeys)
            self.rmsnorm_tile(h, hkeys, wcol, kw, lambda k, t=t: self.xn_ap(k, t),
                              [('xn', k, t) for k in range(8)], tmp)
        self.release(m0)

    def stage_mixer(self, l):
        st = self.stages
        self.stage_proj(l)
        gens = []
        if st is None or 'ssd' in st:
            gens.append(('ssd', self.stage_ssd(l)))
        if st is None or 'gla' in st:
            gens.append(('gla', self.stage_gla(l)))
        m_ = self.mark()
        while gens:
            for it_ in list(gens):
                self.sc.ns = it_[0]
                try:
                    next(it_[1])
                except StopIteration:
                    gens.remove(it_)
        self.sc.ns = None
        self.release(m_)
        if st is None or 's5' in st:
            self.stage_s5(l)
        if st is None or 'merge' in st:
            self.stage_merge(l)
        else:
            self.renorm(self.prm["ffn2_norm"][l])

    def load_consts(self):
        return

    def alloc_consts(self):
        def ld(name, n, rows=128):
            a = self.alloc(n)
            self.dma(a[:rows, :], self.cst[name], w=[name])
            return a
        self.tri128 = ld("c_tri128", 128)
        self.tri64 = ld("c_tri64", 128)
        self.blk64 = ld("c_blk64", 128)
        self.mask64 = ld("c_mask64", 64)
        self.onesf = ld("c_ones", 128)
        self.gmask = ld("c_gmask", 8)
        self.hmask = ld("c_hmask", 2)

    def stage_proj(self, l):
        W = self.prm["w_in"][l]
        m = self.mark()
        stg = [self.alloc(8 * 512) for _ in range(2)]
        wbb = [self.alloc(8 * 512, BF16) for _ in range(2)]
        of = [self.alloc(512) for _ in range(3)]
        ob = [self.alloc(512, BF16) for _ in range(3)]
        segs = [(0, 512, AF.Copy, 1.0, self.US5, BF16), (512, 256, AF.Copy, 0.125, self.Q, BF16),
                (768, 256, AF.Copy, 1.0, self.Kf, BF16), (1536, 512, AF.Silu, 1.0, self.GO, BF16),
                (2048, 16, AF.Copy, 1.0, self.GLR, F32), (2064, 512, AF.Silu, 1.0, self.Z, BF16),
                (2576, 1024, AF.Copy, 1.0, self.XBC, F32), (3608, 3072, AF.Sigmoid, 1.0, self.SIG, BF16)]
        work = []
        for (c0s, n, func, scale, dest, dt) in segs:
            for g0 in range(0, n, 512):
                work.append(('fm', c0s + g0, min(512, n - g0), func, scale, dest, dt, g0))
        for (c0s, n, dest, dt) in [(768, 256, self.KT, BF16), (1024, 512, self.VT, BF16), (3600, 8, self.DTT, F32)]:
            work.append(('tm', c0s, n, None, None, dest, dt, 0))
        loaded = {}

        def issue_load(gi):
            kind, c0, ncols = work[gi][0], work[gi][1], work[gi][2]
            sidx = gi % 2
            loaded[gi] = self.load_weight(W, [128] * 8, c0, ncols, stg[sidx], wbb[sidx], ('stg', sidx),
                                          ('wbb', sidx))

        issue_load(0)
        oi = 0
        for gi, (kind, c0, ncols, func, scale, dest, dt, g0) in enumerate(work):
            sidx = gi % 2
            if gi + 1 < len(work):
                issue_load(gi + 1)
            w3 = loaded[gi]
            if kind == 'fm':
                for t in range(NT):
                    for mc in range((ncols + 127) // 128):
                        mw = min(128, ncols - mc * 128)
                        j = g0 // 128 + mc
                        b, pb = self.bank()
                        for k in range(8):
                            self.mm(pb[:mw, :], w3[:, k, mc * 128:mc * 128 + mw], self.xn_ap(k, t), k == 0, k == 7,
                                    r=[('wbb', sidx), ('xn', k, t)], w=[('ps', b)])
                        o = (of if dt == F32 else ob)[oi % 3]
                        okey = ('of' if dt == F32 else 'ob', oi % 3)
                        oi += 1
                        self.act(o[:mw, :], pb[:mw, :], func, r=[('ps', b)], w=[okey], scale=scale)
                        self.dma(dest[j, :mw, t * TT:(t + 1) * TT], o[:mw, :], r=[okey])
            else:
                n = ncols
                for blk in range(S // 128):
                    t, sub = blk // 4, blk % 4
                    b, pb = self.bank()
                    for k in range(8):
                        xs_ = self.XN[:, k * S + blk * 128:k * S + (blk + 1) * 128]
                        self.mm(pb[:, :n], xs_, w3[:, k, :n], k == 0, k == 7, r=[('wbb', sidx), ('xn', k, t)],
                                w=[('ps', b)])
                    o = (of if dt == F32 else ob)[oi % 3]
                    okey = ('of' if dt == F32 else 'ob', oi % 3)
                    oi += 1
                    self.copy('act' if blk % 2 == 0 else 'dve', o[:, :n], pb[:, :n], r=[('ps', b)], w=[okey])
                    self.dma(dest[blk * 128:(blk + 1) * 128, :], o[:, :n], r=[okey])
        self.release(m)

    def stage_ssd(self, l):
        P = self.prm
        f3 = lambda a, k: a.rearrange("p (k n) -> p k n", k=k)
        cw = self.alloc(32)
        cb = self.alloc(8)
        for k in range(4):
            self.dma(cw[:, k * 8:(k + 1) * 8], P["ssd_conv_w"][l][k].rearrange("(c p) -> p c", p=128), w=['cw'],
                     slow=True)
        self.dma(cb, P["ssd_conv_b"][l].rearrange("(c p) -> p c", p=128), w=['cb'], slow=True)
        dtb = self.alloc(8)
        abc = self.alloc(8)
        dbc = self.alloc(8)
        self.dma(dtb, P["ssd_dt_bias"][l].partition_broadcast(128), w=['dtb'], slow=True)
        self.dma(abc, P["ssd_a_log"][l].partition_broadcast(128), w=['abc'], slow=True)
        self.dma(dbc, P["ssd_d"][l].partition_broadcast(128), w=['dbc'], slow=True)
        self.act(abc, abc, AF.Exp, r=['abc'], w=['abc'])
        self.ts('dve', abc, abc, -1.0, None, ALU.mult, None, r=['abc'], w=['abc'])
        dcol = self.alloc(4)
        for i in range(4):
            self.copy('dve', dcol[0:64, i:i + 1], dbc[0:64, 2 * i:2 * i + 1], r=['dbc'], w=['dcol'])
            self.copy('dve', dcol[64:128, i:i + 1], dbc[64:128, 2 * i + 1:2 * i + 2], r=['dbc'], w=['dcol'])
        nw = self.alloc(4)
        self.dma(nw, P["ssd_norm"][l].rearrange("(c p) -> p c", p=128), w=['nw'], slow=True)
        hT = [self.alloc(256) for _ in range(2)]
        hTb = [self.alloc(256, BF16) for _ in range(2)]
        for g in range(2):
            self.A('pool', lambda e, g=g: e.memset(hT[g], 0.0), w=[('hT', g)])
            self.A('pool', lambda e, g=g: e.memset(hTb[g], 0.0), w=[('hTb', g)])
        xin = [self.alloc(8 * 520) for _ in range(1)]
        acc = self.alloc(512)
        xc = self.alloc(8 * 512)
        xcb = self.alloc(4 * 512, BF16)
        zt = self.alloc(4 * 512, BF16)
        ybuf = self.alloc(4 * 512, BF16)
        dtr = self.alloc(8)
        dt = self.alloc(8)
        la = self.alloc(8)
        cum = self.alloc(8)
        latri = self.alloc(1024)
        dec = self.alloc(1024)
        ecr = self.alloc(1024)
        ecl = self.alloc(8)
        ds = self.alloc(8)
        Cp = self.alloc(1024, BF16)
        scm = self.alloc(256)
        SdT = self.alloc(1024, BF16)
        xdt = self.alloc(512, BF16)
        xdtd = self.alloc(512, BF16)
        Bt = self.alloc(256, BF16)
        yv = self.alloc(512)
        sq = self.alloc(512, BF16)
        sd = self.alloc(256)
        rstd = self.alloc(256)
        xc3 = f3(xc, 8)
        xcb3 = f3(xcb, 4)
        for t in range(NT):
            t0 = t * TT
            xi3 = xin[0].rearrange("p (k n) -> p k n", k=8)
            if t == 0:
                self.A('pool', lambda e: e.memset(xi3[:, :, 0:3], 0.0), w=['xin'])
                self.dma(xi3[:, :, 3:515], self.XBC[:, :, 0:TT].rearrange("k p n -> p k n"), w=['xin'])
            else:
                self.dma(xi3[:, :, 0:515], self.XBC[:, :, t0 - 3:t0 + TT].rearrange("k p n -> p k n"), w=['xin'])
            self.dma(f3(zt, 4), self.Z[:, :, t0:t0 + TT].rearrange("k p n -> p k n"), w=['zt'])
            for c in range(8):
                self.ts('dve', acc, xi3[:, c, 3:515], cw[:, 24 + c:25 + c], None, ALU.mult, None,
                        r=['xin', 'cw'], w=['acc'])
                for k in (2, 1, 0):
                    self.stt(acc, xi3[:, c, k:k + 512], cw[:, k * 8 + c:k * 8 + c + 1], acc, ALU.mult, ALU.add,
                             r=['xin', 'cw', 'acc'], w=['acc'])
                self.act(xc3[:, c, :], acc, AF.Silu, r=['acc', 'cb'], w=[('xc', c)], bias=cb[:, c:c + 1])
                if c >= 4:
                    self.copy('pool', xcb3[:, c - 4, :], xc3[:, c, :], r=[('xc', c)], w=[('xcb', c)])
            for sub in range(4):
                r0 = t0 + sub * 128
                tk = slice(sub * 128, (sub + 1) * 128)
                self.dma(dtr, self.DTT[r0:r0 + 128, :], w=['dtr'])
                self.tt('dve', dt, dtr, dtb, ALU.add, r=['dtr', 'dtb'], w=['dt'])
                self.act(dt, dt, AF.Exp, r=['dt'], w=['dt'])
                self.act(dt, dt, AF.Ln, r=['dt'], w=['dt'], bias=1.0)
                self.tt('dve', la, dt, abc, ALU.mult, r=['dt', 'abc'], w=['la'])
                bc_, pc = self.bank()
                self.mm(pc[:, 0:8], self.tri128, la, True, True, r=['c_tri128', 'la'], w=[('ps', bc_)])
                self.copy('dve', cum, pc[:, 0:8], r=[('ps', bc_)], w=['cum'])
                lt3 = f3(latri, 8)
                for r in range(8):
                    self.ts('pool' if r % 2 else 'dve', lt3[:, r, :], self.tri128, la[:, r:r + 1], None, ALU.mult, None,
                            r=['c_tri128', 'la'], w=[('latri', r)])
                b1, p1 = self.bank()
                b2, p2 = self.bank()
                self.mm(p1, self.onesf, latri[:, 0:512], True, True, r=['c_ones'] + [('latri', r) for r in range(4)],
                        w=[('ps', b1)])
                self.mm(p2, self.onesf, latri[:, 512:1024], True, True,
                        r=['c_ones'] + [('latri', r) for r in range(4, 8)], w=[('ps', b2)])
                pr = [f3(p1, 4), f3(p2, 4)]
                d3 = f3(dec, 8)
                e3 = f3(ecr, 8)
                for r in range(8):
                    self.ts('dve', d3[:, r, :], pr[r // 4][:, r % 4, :], cum[:, r:r + 1], 0.0, ALU.subtract, ALU.min,
                            r=[('ps', b1 if r < 4 else b2), 'cum'], w=[('dec', r // 4)])
                for hh in range(2):
                    self.act(dec[:, hh * 512:(hh + 1) * 512], dec[:, hh * 512:(hh + 1) * 512], AF.Exp,
                             r=[('dec', hh)], w=[('dec', hh)])
                    self.act(ecr[:, hh * 512:(hh + 1) * 512], [p1, p2][hh], AF.Exp, r=[('ps', [b1, b2][hh])],
                             w=[('ecr', hh)])
                    self.copy('dve', ecl[:, hh * 4:(hh + 1) * 4], e3[:, hh * 4:(hh + 1) * 4, 127], r=[('ecr', hh)],
                              w=['ecl'])
                    self.tt('dve', ds[:, hh * 4:(hh + 1) * 4], pr[hh][:, :, 127], cum[:, hh * 4:(hh + 1) * 4],
                            ALU.subtract, r=[('ps', [b1, b2][hh]), 'cum'], w=['ds'])
                self.act(ds, ds, AF.Exp, r=['ds'], w=['ds'])
                C3 = f3(Cp, 8)
                for g in range(2):
                    cin = xc3[:, 6 + g, tk].unsqueeze(1).broadcast_to([128, 4, 128])
                    self.tt('pool', C3[:, 4 * g:4 * g + 4, :], e3[:, 4 * g:4 * g + 4, :], cin, ALU.mult,
                            r=[('ecr', g), ('xc', 6 + g)], w=[('Cp', g)])
                bs, psc = self.bank()
                for g in range(2):
                    self.mm(psc[:, g * 128:(g + 1) * 128], xcb3[:, g, tk], xcb3[:, 2 + g, tk], True, True,
                            r=[('xcb', 4 + g), ('xcb', 6 + g)], w=[('ps', bs)])
                s3 = f3(scm, 2)
                self.tt('dve', s3, f3(psc[:, 0:256], 2), self.tri128.unsqueeze(1).broadcast_to([128, 2, 128]), ALU.mult,
                        r=[('ps', bs), 'c_tri128'], w=['scm'])
                S3 = f3(SdT, 8)
                for g in range(2):
                    self.tt('pool' if g else 'dve', S3[:, 4 * g:4 * g + 4, :], d3[:, 4 * g:4 * g + 4, :],
                            s3[:, g, :].unsqueeze(1).broadcast_to([128, 4, 128]), ALU.mult,
                            r=[('dec', g), 'scm'], w=[('SdT', g)])
                bx, px = self.bank()
                for c in range(4):
                    self.A('pe', lambda e, px=px, c=c, tk=tk: e.transpose(out=px[:, c * 128:(c + 1) * 128],
                                                                          in_=xc3[:, c, tk], identity=self.ident),
                           r=[('xc', c), 'ident'], w=[('ps', bx)])
                bb, pbt = self.bank()
                for c in range(2):
                    self.A('pe', lambda e, pbt=pbt, c=c, tk=tk: e.transpose(out=pbt[:, c * 128:(c + 1) * 128],
                                                                            in_=xc3[:, 4 + c, tk], identity=self.ident),
                           r=[('xc', 4 + c), 'ident'], w=[('ps', bb)])
                x3 = f3(xdt, 8)
                xd3 = f3(xdtd, 8)
                self.tt('dve', x3, f3(px, 8), dt.unsqueeze(2).broadcast_to([128, 8, 64]), ALU.mult,
                        r=[('ps', bx), 'dt'], w=['xdt'])
                self.tt('pool', xd3, x3, ds.unsqueeze(2).broadcast_to([128, 8, 64]), ALU.mult, r=['xdt', 'ds'],
                        w=['xdtd'])
                self.copy('act', Bt, pbt[:, 0:256], r=[('ps', bb)], w=['Bt'])
                by, py = self.bank()
                for i in range(4):
                    for r2 in range(2):
                        r = 2 * i + r2
                        g = r // 4
                        o_ = py[64 * r2:64 * r2 + 64, i * 128:(i + 1) * 128]
                        self.mm(o_, x3[:, r, :], S3[:, r, :], True, False, r=['xdt', ('SdT', g)], w=[('ps', by)])
                        self.mm(o_, hTb[g][:, (r % 4) * 64:(r % 4) * 64 + 64], C3[:, r, :], False, True,
                                r=[('hTb', g), ('Cp', g)], w=[('ps', by)])
                for g in range(2):
                    bst, pst = self.bank()
                    self.mm(pst[:, 0:256], Bt[:, g * 128:(g + 1) * 128], xdtd[:, g * 256:(g + 1) * 256], True, True,
                            r=['Bt', 'xdtd'], w=[('ps', bst)])
                    h3 = f3(hT[g], 4)
                    self.tt('dve', h3, h3, ecl[:, 4 * g:4 * g + 4].unsqueeze(2).broadcast_to([128, 4, 64]), ALU.mult,
                            r=[('hT', g), 'ecl'], w=[('hT', g)])
                    self.tt('dve', hT[g], hT[g], pst[:, 0:256], ALU.add, r=[('hT', g), ('ps', bst)], w=[('hT', g)])
                    self.copy('act', hTb[g], hT[g], r=[('hT', g)], w=[('hTb', g)])
                y3 = f3(yv, 4)
                z3 = f3(zt, 4)
                for i in range(4):
                    self.stt(y3[:, i, :], xc3[:, i, tk], dcol[:, i:i + 1], py[:, i * 128:(i + 1) * 128], ALU.mult,
                             ALU.add, r=[('xc', i), 'dcol', ('ps', by)], w=['yv'])
                self.tt('dve', y3, y3, z3[:, :, tk], ALU.mult, r=['yv', 'zt'], w=['yv'])
                self.act(sq, yv, AF.Square, r=['yv'], w=['sq'])
                bn, pn = self.bank()
                for g in range(2):
                    for j in range(2):
                        c = 2 * g + j
                        self.mm(pn[:, g * 128:(g + 1) * 128], self.ones_b, sq[:, c * 128:(c + 1) * 128], j == 0, j == 1,
                                r=['ones', 'sq'], w=[('ps', bn)])
                self.act(sd, pn[:, 0:256], AF.Sqrt, r=[('ps', bn)], w=['sd'], bias=EPS, scale=1.0 / 256)
                self.A('dve', lambda e: e.reciprocal(out=rstd, in_=sd), r=['sd'], w=['rstd'])
                yb3 = f3(ybuf, 4)
                for i in range(4):
                    self.stt(yb3[:, i, tk], y3[:, i, :], nw[:, i:i + 1], rstd[:, (i // 2) * 128:(i // 2 + 1) * 128],
                             ALU.mult, ALU.mult, r=['yv', 'nw', 'rstd'], w=['ybuf'])
                yield
            self.dma(self.Y[2, :, :, t0:t0 + TT].rearrange("k p n -> p k n"), f3(ybuf, 4), r=['ybuf'])

    def stage_gla(self, l):
        P = self.prm
        f3 = lambda a, k: a.rearrange("p (k n) -> p k n", k=k)
        w2 = self.alloc(256)
        b2b = self.alloc(256)
        gnw = self.alloc(1)
        self.A('pool', lambda e: e.memset(w2, 0.0), w=['w2'])
        self.dma(w2[0:16, :], P["gla_gate_w2"][l], w=['w2'])
        self.dma(b2b, P["gla_gate_b2"][l].partition_broadcast(128), w=['b2b'], slow=True)
        self.dma(gnw, P["gla_norm"][l].rearrange("(p o) -> p o", o=1), w=['gnw'], slow=True)
        Sf = [self.alloc(128) for _ in range(2)]
        Sb = [self.alloc(128, BF16) for _ in range(2)]
        for kc in range(2):
            self.A('pool', lambda e, kc=kc: e.memset(Sf[kc], 0.0), w=[('Sf', kc)])
            self.A('pool', lambda e, kc=kc: e.memset(Sb[kc], 0.0), w=[('Sb', kc)])
        qt = self.alloc(2 * 512, BF16)
        kt_ = self.alloc(2 * 512, BF16)
        got = self.alloc(4 * 512, BF16)
        glr = self.alloc(512)
        self.A('pool', lambda e: e.memset(glr, 0.0), w=['glr'])
        ybuf = self.alloc(4 * 512, BF16)
        ktm = self.alloc(256, BF16)
        vtm = self.alloc(512, BF16)
        xb = self.alloc(256)
        loga = self.alloc(256)
        bsb = self.alloc(256)
        eb = self.alloc(256)
        enb = self.alloc(256)
        qe = self.alloc(256, BF16)
        qem = [self.alloc(256, BF16) for _ in range(2)]
        kem = [self.alloc(256, BF16) for _ in range(2)]
        kl = self.alloc(256)
        klm = [self.alloc(256, BF16) for _ in range(2)]
        ATs = [self.alloc(256, BF16) for _ in range(2)]
        for c2 in range(2):
            self.A('pool', lambda e, c2=c2: e.memset(ATs[c2], 0.0), w=[('AT', c2)])
        sq = self.alloc(512, BF16)
        sd = self.alloc(512)
        rstd = self.alloc(512)
        yg = self.alloc(512)
        q3, k3, go3, yb3 = f3(qt, 2), f3(kt_, 2), f3(got, 4), f3(ybuf, 4)
        eb3, enb3, qe3 = f3(eb, 2), f3(enb, 2), f3(qe, 2)
        qem3 = [f3(a, 2) for a in qem]
        kem3 = [f3(a, 2) for a in kem]
        for t in range(NT):
            t0 = t * TT
            self.dma(q3, self.Q[:, :, t0:t0 + TT].rearrange("k p n -> p k n"), w=['qt'])
            self.dma(k3, self.Kf[:, :, t0:t0 + TT].rearrange("k p n -> p k n"), w=['kt'])
            self.dma(go3, self.GO[:, :, t0:t0 + TT].rearrange("k p n -> p k n"), w=['got'])
            self.dma(glr[0:16, :], self.GLR[0, 0:16, t0:t0 + TT], w=['glr'])
            for sub in range(4):
                r0 = t0 + sub * 128
                tk = slice(sub * 128, (sub + 1) * 128)
                self.dma(ktm, self.KT[r0:r0 + 128, :], w=['ktm'])
                self.dma(vtm, self.VT[r0:r0 + 128, :], w=['vtm'])
                bx, px = self.bank()
                self.mm(px[:, 0:256], glr[:, tk], w2, True, True, r=['glr', 'w2'], w=[('ps', bx)])
                self.tt('dve', xb, px[:, 0:256], b2b, ALU.add, r=[('ps', bx), 'b2b'], w=['xb'])
                self.act(xb, xb, AF.Exp, r=['xb'], w=['xb'], scale=-1.0)
                self.act(xb, xb, AF.Ln, r=['xb'], w=['xb'], bias=1.0)
                self.ts('dve', loga, xb, -1.0 / 16.0, None, ALU.mult, None, r=['xb'], w=['loga'])
                bb_, pbm = self.bank()
                self.mm(pbm[:, 0:256], self.tri64, loga, True, True, r=['c_tri64', 'loga'], w=[('ps', bb_)])
                self.mm(pbm[:, 256:512], self.blk64, loga, True, True, r=['c_blk64', 'loga'], w=[('ps', bb_)])
                bf_, pbf = self.bank()
                for kc in range(2):
                    self.mm(pbf[:, kc * 128:(kc + 1) * 128], loga[:, kc * 128:(kc + 1) * 128], self.tri64, True, True,
                            r=['loga', 'c_tri64'], w=[('ps', bf_)])
                self.act(eb, pbf[:, 0:256], AF.Exp, r=[('ps', bf_)], w=['eb'])
                self.act(enb, pbf[:, 0:256], AF.Exp, r=[('ps', bf_)], w=['enb'], scale=-1.0)
                self.tt('dve', qe3, q3[:, :, tk], eb3, ALU.mult, r=['qt', 'eb'], w=['qe'])
                for h2 in range(2):
                    for kc in range(2):
                        self.stt(kem3[h2][:, kc, :], k3[:, kc, tk], self.hmask[:, h2:h2 + 1], enb3[:, kc, :], ALU.mult,
                                 ALU.mult, r=['kt', 'enb', 'c_hmask'], w=[('kem', h2)])
                    self.ts('pool', qem[h2], qe, self.hmask[:, h2:h2 + 1], None, ALU.mult, None,
                            r=['qe', 'c_hmask'], w=[('qem', h2)])
                self.copy('dve', bsb, pbm[:, 0:256], r=[('ps', bb_)], w=['bsb'])
                self.tt('dve', bsb, pbm[:, 256:512], bsb, ALU.subtract, r=[('ps', bb_), 'bsb'], w=['bsb'])
                self.act(bsb, bsb, AF.Exp, r=['bsb'], w=['bsb'])
                self.tt('dve', kl, ktm, bsb, ALU.mult, r=['ktm', 'bsb'], w=['kl'])
                for c2 in range(2):
                    self.ts('pool', klm[c2], kl, self.hmask[:, c2:c2 + 1], None, ALU.mult, None,
                            r=['kl', 'c_hmask'], w=[('klm', c2)])
                bo, po = self.bank()
                for c2 in range(2):
                    cs = slice(64 * c2, 64 * c2 + 64)
                    ba, pa = self.bank()
                    for hd in range(4):
                        kc, h2 = hd // 2, hd % 2
                        hs = slice(64 * h2, 64 * h2 + 64)
                        self.mm(pa[cs, hd * 64:(hd + 1) * 64], kem3[h2][:, kc, cs], qe3[:, kc, cs], True, True,
                                r=[('kem', h2), 'qe'], w=[('ps', ba)])
                    AT3 = f3(ATs[c2], 4)
                    self.tt('dve', AT3[cs, :, :], f3(pa[cs, 0:256], 4),
                            self.mask64[cs, :].unsqueeze(1).broadcast_to([64, 4, 64]), ALU.mult,
                            r=[('ps', ba), 'c_mask64'], w=[('AT', c2)])
                    for hd in range(4):
                        kc, h2 = hd // 2, hd % 2
                        hs = slice(64 * h2, 64 * h2 + 64)
                        o_ = po[:, hd * 128 + 64 * c2:hd * 128 + 64 * c2 + 64]
                        self.mm(o_, vtm[:, hd * 128:(hd + 1) * 128], AT3[:, hd, :], True, False,
                                r=['vtm', ('AT', c2)], w=[('ps', bo)])
                        self.mm(o_, Sb[kc], qem3[h2][:, kc, cs], False, True, r=[('Sb', kc), ('qem', h2)],
                                w=[('ps', bo)])
                    bs_, pS = self.bank()
                    for kc in range(2):
                        for h2 in range(2):
                            hd = 2 * kc + h2
                            self.mm(pS[64 * h2:64 * h2 + 64, kc * 128:(kc + 1) * 128],
                                    klm[c2][:, hd * 64:(hd + 1) * 64],
                                    vtm[:, hd * 128:(hd + 1) * 128], True, True, r=[('klm', c2), 'vtm'],
                                    w=[('ps', bs_)])
                        self.stt(Sf[kc], Sf[kc], eb3[:, kc, 64 * c2 + 63:64 * c2 + 64], pS[:, kc * 128:(kc + 1) * 128],
                                 ALU.mult, ALU.add, r=[('Sf', kc), 'eb', ('ps', bs_)], w=[('Sf', kc)])
                        self.copy('act', Sb[kc], Sf[kc], r=[('Sf', kc)], w=[('Sb', kc)])
                self.act(sq, po, AF.Square, r=[('ps', bo)], w=['sq'])
                bn, pn = self.bank()
                self.mm(pn, self.ones_b, sq, True, True, r=['ones', 'sq'], w=[('ps', bn)])
                self.act(sd, pn, AF.Sqrt, r=[('ps', bn)], w=['sd'], bias=EPS, scale=1.0 / 128)
                self.A('dve', lambda e: e.reciprocal(out=rstd, in_=sd), r=['sd'], w=['rstd'])
                self.stt(yg, po, gnw[:, 0:1], rstd, ALU.mult, ALU.mult, r=[('ps', bo), 'gnw', 'rstd'], w=['yg'])
                self.tt('dve', yb3[:, :, tk], f3(yg, 4), go3[:, :, tk], ALU.mult, r=['yg', 'got'], w=['ybuf'])
                yield
            self.dma(self.Y[1, :, :, t0:t0 + TT].rearrange("k p n -> p k n"), yb3, r=['ybuf'])

    def stage_s5(self, l):
        m = self.mark()
        P = self.prm
        PI = float(np.pi)
        f3 = lambda a, k: a.rearrange("p (k n) -> p k n", k=k)
        NS = 32
        nat = {n: self.alloc(64) for n in ["lr", "li", "x1", "th", "mag", "c", "s", "cc", "ss", "cs", "ar", "ai",
                                           "den", "nr", "t1", "t2"]}
        stc = self.alloc(1)
        ff = self.alloc(128)
        N = lambda n: nat[n][:NS, :]
        self.dma(N("lr"), P["s5_lam_re"][l], w=['n_lr'])
        self.dma(N("li"), P["s5_lam_im"][l], w=['n_li'])
        self.dma(stc[:NS, :], P["s5_log_step"][l].rearrange("(g o) -> g o", o=1), w=['stc'], slow=True)
        self.act(stc[:NS, :], stc[:NS, :], AF.Exp, r=['stc'], w=['stc'])
        self.ts('dve', N("lr"), N("lr"), -1e-4, None, ALU.min, None, r=['n_lr'], w=['n_lr'])
        self.ts('dve', N("x1"), N("lr"), stc[:NS, 0:1], None, ALU.mult, None, r=['n_lr', 'stc'], w=['n_x1'])
        self.ts('dve', N("th"), N("li"), stc[:NS, 0:1], None, ALU.mult, None, r=['n_li', 'stc'], w=['n_th'])
        self.act(N("mag"), N("x1"), AF.Exp, r=['n_x1'], w=['n_mag'])
        hp = self.alloc(1)
        self.A('pool', lambda e: e.memset(hp, PI / 2), w=['hp'])
        self.act(N("s"), N("th"), AF.Sin, r=['n_th'], w=['n_s'], scale=1.0 / 16)
        self.act(N("c"), N("th"), AF.Sin, r=['n_th', 'hp'], w=['n_c'], scale=1.0 / 16, bias=hp[:NS, 0:1])
        for it in range(4):
            self.tt('dve', N("cc"), N("c"), N("c"), ALU.mult, r=['n_c'], w=['n_cc'])
            self.tt('dve', N("ss"), N("s"), N("s"), ALU.mult, r=['n_s'], w=['n_ss'])
            self.tt('dve', N("cs"), N("c"), N("s"), ALU.mult, r=['n_c', 'n_s'], w=['n_cs'])
            self.tt('dve', N("c"), N("cc"), N("ss"), ALU.subtract, r=['n_cc', 'n_ss'], w=['n_c'])
            self.ts('dve', N("s"), N("cs"), 2.0, None, ALU.mult, None, r=['n_cs'], w=['n_s'])
        self.tt('dve', N("ar"), N("mag"), N("c"), ALU.mult, r=['n_mag', 'n_c'], w=['n_ar'])
        self.tt('dve', N("ai"), N("mag"), N("s"), ALU.mult, r=['n_mag', 'n_s'], w=['n_ai'])
        self.tt('dve', N("t1"), N("lr"), N("lr"), ALU.mult, r=['n_lr'], w=['n_t1'])
        self.tt('dve', N("t2"), N("li"), N("li"), ALU.mult, r=['n_li'], w=['n_t2'])
        self.tt('dve', N("den"), N("t1"), N("t2"), ALU.add, r=['n_t1', 'n_t2'], w=['n_den'])
        self.A('dve', lambda e: e.reciprocal(out=N("den"), in_=N("den")), r=['n_den'], w=['n_den'])
        self.ts('dve', N("nr"), N("ar"), -1.0, None, ALU.add, None, r=['n_ar'], w=['n_nr'])
        self.tt('dve', N("t1"), N("nr"), N("lr"), ALU.mult, r=['n_nr', 'n_lr'], w=['n_t1'])
        self.tt('dve', N("t2"), N("ai"), N("li"), ALU.mult, r=['n_ai', 'n_li'], w=['n_t2'])
        self.tt('dve', N("t1"), N("t1"), N("t2"), ALU.add, r=['n_t1', 'n_t2'], w=['n_t1'])
        self.tt('dve', ff[:NS, 0:64], N("t1"), N("den"), ALU.mult, r=['n_t1', 'n_den'], w=['ff'])
        self.tt('dve', N("t1"), N("ai"), N("lr"), ALU.mult, r=['n_ai', 'n_lr'], w=['n_t1'])
        self.tt('dve', N("t2"), N("nr"), N("li"), ALU.mult, r=['n_nr', 'n_li'], w=['n_t2'])
        self.tt('dve', N("t1"), N("t1"), N("t2"), ALU.subtract, r=['n_t1', 'n_t2'], w=['n_t1'])
        self.tt('dve', ff[:NS, 64:128], N("t1"), N("den"), ALU.mult, r=['n_t1', 'n_den'], w=['ff'])
        self.dma(self.S5T[0], N("mag"), r=['n_mag'], w=['S5T0'])
        self.dma(self.S5T[1], N("th"), r=['n_th'], w=['S5T1'])
        rho_c = self.alloc(16)
        th_c = self.alloc(16)
        for t2 in range(2):
            self.dma(rho_c[t2 * 64:(t2 + 1) * 64, :], self.S5T[0, t2::2, :].rearrange("j p -> p j"), r=['S5T0'],
                     w=['rho_c'], slow=True)
            self.dma(th_c[t2 * 64:(t2 + 1) * 64, :], self.S5T[1, t2::2, :].rearrange("j p -> p j"), r=['S5T1'],
                     w=['th_c'], slow=True)
        selb = self.alloc(512)
        self.dma(selb[:NS, :], self.cst["c_selb"], w=['selb'])
        bnat = [self.alloc(512) for _ in range(2)]
        self.dma(f3(bnat[0][:64, :], 32), P["s5_b_re"][l].rearrange("g p h -> p g h"), w=[('bnat', 0)])
        self.dma(f3(bnat[1][:64, :], 32), P["s5_b_im"][l].rearrange("g p h -> p g h"), w=[('bnat', 1)])
        BbT = self.alloc(16 * 2 * 128, BF16)
        BbT4 = BbT.rearrange("p (j c n) -> p j c n", j=16, c=2)
        fq = self.alloc(128)
        bq = [self.alloc(64) for _ in range(2)]
        bbq = [self.alloc(64) for _ in range(2)]
        tq = [self.alloc(64) for _ in range(2)]
        for q in range(4):
            b, pb = self.bank()
            self.mm(pb[:, 0:128], selb[:NS, q * 128:(q + 1) * 128], ff[:NS, :], True, True, r=['selb', 'ff'],
                    w=[('ps', b)])
            self.copy('act', fq, pb[:, 0:128], r=[('ps', b)], w=['fq'])
            b2, pb2 = self.bank()
            for c in range(2):
                self.A('pe', lambda e, pb2=pb2, c=c, q=q: e.transpose(out=pb2[:, c * 64:(c + 1) * 64],
                                                                      in_=bnat[c][:64, q * 128:(q + 1) * 128],
                                                                      identity=self.ident[:64, :64]),
                       r=[('bnat', c), 'ident'], w=[('ps', b2)])
                self.copy('act', bq[c], pb2[:, c * 64:(c + 1) * 64], r=[('ps', b2)], w=[('bq', c)])
            self.tt('dve', tq[0], fq[:, 0:64], bq[0], ALU.mult, r=['fq', ('bq', 0)], w=[('tq', 0)])
            self.tt('dve', tq[1], fq[:, 64:128], bq[1], ALU.mult, r=['fq', ('bq', 1)], w=[('tq', 1)])
            self.tt('dve', bbq[0], tq[0], tq[1], ALU.subtract, r=[('tq', 0), ('tq', 1)], w=[('bbq', 0)])
            self.tt('dve', tq[0], fq[:, 0:64], bq[1], ALU.mult, r=['fq', ('bq', 1)], w=[('tq', 0)])
            self.tt('dve', tq[1], fq[:, 64:128], bq[0], ALU.mult, r=['fq', ('bq', 0)], w=[('tq', 1)])
            self.tt('dve', bbq[1], tq[0], tq[1], ALU.add, r=[('tq', 0), ('tq', 1)], w=[('bbq', 1)])
            for gl in range(8):
                g = q * 8 + gl
                j, t2 = g // 2, g % 2
                for c in range(2):
                    self.ts('dve', BbT4[:, j, c, t2 * 64:(t2 + 1) * 64], bbq[c], self.gmask[:, gl:gl + 1], None,
                            ALU.mult, None, r=[('bbq', c), 'c_gmask'], w=['BbT'])
        CT = self.alloc(16 * 2 * 128, BF16)
        CT4 = CT.rearrange("p (j c n) -> p j c n", j=16, c=2)
        self.A('pool', lambda e: e.memset(CT, 0.0), w=['CT'])
        cdup = self.alloc(128)
        cT = self.alloc(128)
        for c, nm in enumerate(["s5_c_re", "s5_c_im"]):
            src = P[nm][l].rearrange("g h p -> (g h) p")
            for q in range(4):
                self.dma(cdup[:, 0:64], src[q * 128:(q + 1) * 128, :], w=['cdup'])
                self.dma(cdup[:, 64:128], src[q * 128:(q + 1) * 128, :], w=['cdup'])
                b, pb = self.bank()
                self.A('pe', lambda e, pb=pb: e.transpose(out=pb[:, 0:128], in_=cdup, identity=self.ident),
                       r=['cdup', 'ident'], w=[('ps', b)])
                self.act(cT, pb[:, 0:128], AF.Copy, r=[('ps', b)], w=['cT'], scale=(1.0 if c == 0 else -1.0))
                for gl in range(8):
                    g = q * 8 + gl
                    j, t2 = g // 2, g % 2
                    rs = slice(t2 * 64, (t2 + 1) * 64)
                    self.copy('dve' if gl % 2 else 'pool', CT4[rs, j, c, gl * 16:(gl + 1) * 16],
                              cT[rs, gl * 16:(gl + 1) * 16], r=['cT'], w=['CT'])
        cs_t = self.XNf[:, 0:8192]
        sn_t = self.XNf[:, 8192:16384]
        cs3, sn3 = f3(cs_t, 16), f3(sn_t, 16)
        iota = self.alloc(512)
        self.dma(iota, self.cst["c_iota"], w=['iota'])
        ph = self.alloc(512)
        kq = self.alloc(512)
        ki = self.alloc(512).bitcast(mybir.dt.int32)
        ta = self.alloc(512)
        tb = self.alloc(512)
        for j in range(16):
            self.ts('dve', ph, iota, th_c[:, j:j + 1], None, ALU.mult, None, r=['iota', 'th_c'], w=['ph'])
            self.ts('dve', kq, ph, 1.0 / (2 * PI), None, ALU.mult, None, r=['ph'], w=['kq'])
            self.copy('dve', ki, kq, r=['kq'], w=['ki'])
            self.copy('dve', kq, ki, r=['ki'], w=['kq'])
            self.stt(ph, kq, -2 * PI, ph, ALU.mult, ALU.add, r=['kq', 'ph'], w=['ph'])
            self.act(sn3[:, j, :], ph, AF.Sin, r=['ph'], w=[('sn', j)], scale=0.25)
            self.act(cs3[:, j, :], ph, AF.Sin, r=['ph', 'hp'], w=[('cs', j)], scale=0.25, bias=hp[:, 0:1])
            for it in range(2):
                self.tt('dve', ta, cs3[:, j, :], cs3[:, j, :], ALU.mult, r=[('cs', j)], w=['ta'])
                self.tt('pool', tb, sn3[:, j, :], sn3[:, j, :], ALU.mult, r=[('sn', j)], w=['tb'])
                self.tt('pool', kq, cs3[:, j, :], sn3[:, j, :], ALU.mult, r=[('cs', j), ('sn', j)], w=['kq'])
                self.tt('dve', cs3[:, j, :], ta, tb, ALU.subtract, r=['ta', 'tb'], w=[('cs', j)])
                self.ts('pool', sn3[:, j, :], kq, 2.0, None, ALU.mult, None, r=['kq'], w=[('sn', j)])
        e5r = self.alloc(16)
        e5i = self.alloc(16)
        t16a = self.alloc(16)
        t16b = self.alloc(16)
        allk = [('cs', j) for j in range(16)] + [('sn', j) for j in range(16)]
        self.tt('dve', t16a, cs3[:, :, 511], cs3[:, :, 1], ALU.mult, r=allk, w=['t16a'])
        self.tt('dve', t16b, sn3[:, :, 511], sn3[:, :, 1], ALU.mult, r=allk, w=['t16b'])
        self.tt('dve', e5r, t16a, t16b, ALU.subtract, r=['t16a', 't16b'], w=['e5r'])
        self.tt('dve', t16a, cs3[:, :, 511], sn3[:, :, 1], ALU.mult, r=allk, w=['t16a'])
        self.tt('dve', t16b, sn3[:, :, 511], cs3[:, :, 1], ALU.mult, r=allk, w=['t16b'])
        self.tt('dve', e5i, t16a, t16b, ALU.add, r=['t16a', 't16b'], w=['e5i'])
        dcol = self.alloc(4)
        self.dma(dcol, P["s5_d"][l].rearrange("(c p) -> p c", p=128), w=['dcol'], slow=True)
        stg = self.alloc(4 * 512)
        wgl = self.alloc(4 * 512, BF16)
        self.dma(f3(stg, 4), P["s5_glu"][l].rearrange("(k p) n -> p k n", p=128), w=['stg'])
        self.copy('pool', wgl, stg, r=['stg'], w=['wgl'])
        wgl3 = f3(wgl, 4)
        ut = self.alloc(4 * 512, BF16)
        ut3 = f3(ut, 4)
        p1, p2, p3, p4 = [self.alloc(512) for _ in range(4)]
        wr, wi = self.alloc(512), self.alloc(512)
        gr, gi = self.alloc(512), self.alloc(512)
        Hr, Hi = self.alloc(512, BF16), self.alloc(512, BF16)
        glr_, gli_ = self.alloc(16), self.alloc(16)
        inr, ini = self.alloc(16), self.alloc(16)
        yq = self.alloc(512)
        x2 = self.alloc(512)
        gq = self.alloc(4 * 512, BF16)
        gq3 = f3(gq, 4)
        sgl = self.alloc(512)
        ybuf = self.alloc(4 * 512, BF16)
        yb3 = f3(ybuf, 4)
        self.A('pool', lambda e: e.memset(inr, 0.0), w=['inr'])
        self.A('pool', lambda e: e.memset(ini, 0.0), w=['ini'])
        xbk = 0
        for t in range(NT):
            t0 = t * TT
            self.dma(ut3, self.US5[:, :, t0:t0 + TT].rearrange("k p n -> p k n"), w=['ut'])
            if t > 0:
                self.tt('dve', t16a, e5r, glr_, ALU.mult, r=['e5r', 'glr_'], w=['t16a'])
                self.tt('dve', t16b, e5i, gli_, ALU.mult, r=['e5i', 'gli_'], w=['t16b'])
                self.tt('dve', inr, t16a, t16b, ALU.subtract, r=['t16a', 't16b'], w=['inr'])
                self.tt('dve', t16a, e5r, gli_, ALU.mult, r=['e5r', 'gli_'], w=['t16a'])
                self.tt('dve', t16b, e5i, glr_, ALU.mult, r=['e5i', 'glr_'], w=['t16b'])
                self.tt('dve', ini, t16a, t16b, ALU.add, r=['t16a', 't16b'], w=['ini'])
            for j in range(16):
                q = j // 4
                bA = xbk % 6
                bB = (xbk + 1) % 6
                xbk += 2
                pA = self.ps[:, bA * 512:(bA + 1) * 512]
                pB = self.ps[:, bB * 512:(bB + 1) * 512]
                self.mm(pA, BbT4[:, j, 0, :], ut3[:, q, :], True, True, r=['BbT', 'ut'], w=[('ps', bA)])
                self.mm(pB, BbT4[:, j, 1, :], ut3[:, q, :], True, True, r=['BbT', 'ut'], w=[('ps', bB)])
                cj, sj = cs3[:, j, :], sn3[:, j, :]
                tabk = [('cs', j), ('sn', j)]
                self.tt('dve', p1, pA, cj, ALU.mult, r=[('ps', bA)] + tabk, w=['p1'])
                self.tt('dve', p2, pB, sj, ALU.mult, r=[('ps', bB)] + tabk, w=['p2'])
                self.tt('dve', p3, pB, cj, ALU.mult, r=[('ps', bB)] + tabk, w=['p3'])
                self.tt('dve', p4, pA, sj, ALU.mult, r=[('ps', bA)] + tabk, w=['p4'])
                self.tt('pool', wr, p1, p2, ALU.add, r=['p1', 'p2'], w=['wr'])
                self.tt('pool', wi, p3, p4, ALU.subtract, r=['p3', 'p4'], w=['wi'])
                rb = rho_c[:, j:j + 1].broadcast_to([128, 512])
                self.A('dve', lambda e, rb=rb, j=j: e.tensor_tensor_scan(out=gr, data0=rb, data1=wr,
                                                                          initial=inr[:, j:j + 1], op0=ALU.mult,
                                                                          op1=ALU.add),
                       r=['wr', 'rho_c', 'inr'], w=['gr'])
                self.A('dve', lambda e, rb=rb, j=j: e.tensor_tensor_scan(out=gi, data0=rb, data1=wi,
                                                                          initial=ini[:, j:j + 1], op0=ALU.mult,
                                                                          op1=ALU.add),
                       r=['wi', 'rho_c', 'ini'], w=['gi'])
                self.copy('act', glr_[:, j:j + 1], gr[:, 511:512], r=['gr'], w=['glr_'])
                self.copy('act', gli_[:, j:j + 1], gi[:, 511:512], r=['gi'], w=['gli_'])
                self.tt('pool', p1, gr, cj, ALU.mult, r=['gr'] + tabk, w=['p1'])
                self.tt('pool', p2, gi, sj, ALU.mult, r=['gi'] + tabk, w=['p2'])
                self.tt('pool', p3, gi, cj, ALU.mult, r=['gi'] + tabk, w=['p3'])
                self.tt('pool', p4, gr, sj, ALU.mult, r=['gr'] + tabk, w=['p4'])
                self.tt('dve', Hr, p1, p2, ALU.subtract, r=['p1', 'p2'], w=['Hr'])
                self.tt('dve', Hi, p3, p4, ALU.add, r=['p3', 'p4'], w=['Hi'])
                bY = 6 + q % 2
                pY = self.ps[:, bY * 512:(bY + 1) * 512]
                first = (j % 4 == 0)
                last = (j % 4 == 3)
                self.mm(pY, CT4[:, j, 0, :], Hr, first, False, r=['CT', 'Hr'], w=[('ps', bY)])
                self.mm(pY, CT4[:, j, 1, :], Hi, False, last, r=['CT', 'Hi'], w=[('ps', bY)])
                if last:
                    self.stt(yq, ut3[:, q, :], dcol[:, q:q + 1], pY, ALU.mult, ALU.add, r=['ut', 'dcol', ('ps', bY)],
                             w=['yq'])
                    self.tt('dve', x2, yq, yq, ALU.mult, r=['yq'], w=['x2'])
                    self.ts('dve', x2, x2, 0.044715, 1.0, ALU.mult, ALU.add, r=['x2'], w=['x2'])
                    self.tt('dve', x2, x2, yq, ALU.mult, r=['x2', 'yq'], w=['x2'])
                    self.act(x2, x2, AF.Tanh, r=['x2'], w=['x2'], scale=0.7978845608028654)
                    self.stt(x2, x2, 1.0, yq, ALU.add, ALU.mult, r=['x2', 'yq'], w=['x2'])
                    self.ts('dve', gq3[:, q, :], x2, 0.5, None, ALU.mult, None, r=['x2'], w=[('gq', q)])
            for mo in range(4):
                b, pb = self.bank()
                for q in range(4):
                    self.mm(pb, wgl3[:, q, mo * 128:(mo + 1) * 128], gq3[:, q, :], q == 0, q == 3,
                            r=['wgl', ('gq', q)], w=[('ps', b)])
                self.act(sgl, pb, AF.Sigmoid, r=[('ps', b)], w=['sgl'])
                self.tt('dve', yb3[:, mo, :], gq3[:, mo, :], sgl, ALU.mult, r=[('gq', mo), 'sgl'], w=['ybuf'])
            self.dma(self.Y[0, :, :, t0:t0 + TT].rearrange("k p n -> p k n"), yb3, r=['ybuf'])
        self.release(m)

    def stage_merge(self, l):
        m = self.mark()
        P = self.prm
        f3 = lambda a, k: a.rearrange("p (k n) -> p k n", k=k)
        stg = self.alloc(8 * 512)
        wbr = [self.alloc(4 * 1024, BF16) for _ in range(3)]
        wo = self.alloc(8 * 1024, BF16)
        for bi, nm in enumerate(["w_br_s5", "w_br_gla", "w_br_ssd"]):
            st = f3(stg[:, :4096], 4)
            self.dma(st, P[nm][l].rearrange("(k p) n -> p k n", p=128), w=['stg'])
            self.copy('pool', f3(wbr[bi], 4), st, r=['stg'], w=[('wbr', bi)])
        wo3 = f3(wo, 8)
        for half in range(2):
            st = f3(stg, 8)
            self.dma(st, P["w_out"][l][:, half * 512:(half + 1) * 512].rearrange("(k p) n -> p k n", p=128), w=['stg'])
            self.copy('pool', wo3[:, :, half * 512:(half + 1) * 512], st, r=['stg'], w=['wo'])
        yt = [self.alloc(4 * 512, BF16) for _ in range(3)]
        sg = self.alloc(24 * 512, BF16)
        hb = self.alloc(8 * 512)
        mg = self.alloc(512)
        tmpm = self.alloc(512)
        mgb = self.alloc(8 * 512, BF16)
        tmp = (self.alloc(8 * 512, BF16), self.alloc(512), self.alloc(512))
        wcol = self.alloc(8)
        kw = self.load_cols(wcol, P["ffn2_norm"][l], 8)
        sg3 = f3(sg, 24)
        for t in range(NT):
            t0 = t * TT
            for bi in range(3):
                self.dma(f3(yt[bi], 4), self.Y[bi, :, :, t0:t0 + TT].rearrange("k p n -> p k n"), w=[('yt', bi)])
            self.dma(sg3, self.SIG[:, :, t0:t0 + TT].rearrange("k p n -> p k n"), w=['sg'])
            hkeys = [('h', 0, k) for k in range(8)]
            self.dma(f3(hb, 8), self.H[:, :, t0:t0 + TT].rearrange("k p n -> p k n"), w=hkeys)
            for mo in range(8):
                for bi in range(3):
                    b, pb = self.bank()
                    y3 = f3(yt[bi], 4)
                    w3 = f3(wbr[bi], 4)
                    for kc in range(4):
                        self.mm(pb, w3[:, kc, mo * 128:(mo + 1) * 128], y3[:, kc, :], kc == 0, kc == 3,
                                r=[('wbr', bi), ('yt', bi)], w=[('ps', b)])
                    if bi == 0:
                        self.tt('dve', mg, pb, sg3[:, bi * 8 + mo, :], ALU.mult, r=[('ps', b), 'sg'], w=['mg'])
                    else:
                        self.tt('dve', tmpm, pb, sg3[:, bi * 8 + mo, :], ALU.mult, r=[('ps', b), 'sg'], w=['tmpm'])
                        self.tt('pool', mg, mg, tmpm, ALU.add, r=['mg', 'tmpm'], w=['mg'])
                self.copy('act', mgb[:, mo * 512:(mo + 1) * 512], mg, r=['mg'], w=[('mgb', mo)])
            for mo2 in range(8):
                b, pb = self.bank()
                for mo in range(8):
                    self.mm(pb, wo3[:, mo, mo2 * 128:(mo2 + 1) * 128], mgb[:, mo * 512:(mo + 1) * 512], mo == 0,
                            mo == 7, r=['wo', ('mgb', mo)], w=[('ps', b)])
                hk = hb[:, mo2 * 512:(mo2 + 1) * 512]
                self.tt('dve', hk, hk, pb, ALU.add, r=[hkeys[mo2], ('ps', b)], w=[hkeys[mo2]])
            self.dma(self.H[:, :, t0:t0 + TT].rearrange("k p n -> p k n"), f3(hb, 8), r=hkeys)
            self.rmsnorm_tile(hb, hkeys, wcol, kw, lambda k, t=t: self.xn_ap(k, t),
                              [('xn', k, t) for k in range(8)], tmp)
        self.release(m)

    def renorm(self, vec):
        m = self.mark()
        hb = [self.alloc(8 * 512) for _ in range(2)]
        tmp = (self.alloc(8 * 512, BF16), self.alloc(512), self.alloc(512))
        wcol = self.alloc(8)
        kw = self.load_cols(wcol, vec, 8)
        for t in range(NT):
            h = hb[t % 2]
            hkeys = [('h', t % 2, k) for k in range(8)]
            self.dma(h.rearrange("p (k n) -> p k n", k=8),
                     self.H[:, :, t * TT:(t + 1) * TT].rearrange("k p n -> p k n"), w=hkeys)
            self.rmsnorm_tile(h, hkeys, wcol, kw, lambda k, t=t: self.xn_ap(k, t),
                              [('xn', k, t) for k in range(8)], tmp)
        self.release(m)

    def stage_ple(self, l):
        m = self.mark()
        Wpg, Wpp = self.prm["ple_gate"][l], self.prm["ple_proj"][l]
        stg = [self.alloc(8 * 512)] * 2
        wg = self.alloc(8 * 1024, BF16)
        wp = self.alloc(2 * 1024, BF16)
        wg3 = wg.rearrange("p (k n) -> p k n", k=8)
        wp3 = wp.rearrange("p (k n) -> p k n", k=2)
        for half in range(2):
            st = stg[half][:, :8 * 512].rearrange("p (k n) -> p k n", k=8)
            self.dma(st, Wpg[:, half * 512:(half + 1) * 512].rearrange("(k p) n -> p k n", p=128),
                     w=[('stg', 0)])
            self.copy('pool', wg3[:, :, half * 512:(half + 1) * 512], st, r=[('stg', 0)], w=['wg'])
        st = stg[0][:, :2048].rearrange("p (k n) -> p k n", k=2)
        self.dma(st, Wpp.rearrange("(k p) n -> p k n", p=128), w=[('stg', 0)])
        self.copy('pool', wp3, st, r=[('stg', 0)], w=['wp'])
        pt = [[self.alloc(256) for _ in range(4)] for _ in range(2)]
        pf = [self.alloc(2 * 512, BF16) for _ in range(2)]
        hb = [self.alloc(8 * 512) for _ in range(2)]
        sg = [self.alloc(512) for _ in range(2)]
        tmp = (self.alloc(8 * 512, BF16), self.alloc(512), self.alloc(512))
        wcol = self.alloc(8)
        last = (l == DEPTH - 1)
        nxt = self.prm["final_norm"] if last else self.prm["ffn1_norm"][l + 1]
        kw = self.load_cols(wcol, nxt, 8)
        if last:
            yb = [self.alloc(8 * 512)] * 2
            ot = [self.alloc(1024) for _ in range(2)]
        it = 0
        for t in range(NT):
            h = hb[t % 2]
            hkeys = [('h', t % 2, k) for k in range(8)]
            self.dma(h.rearrange("p (k n) -> p k n", k=8),
                     self.H[:, :, t * TT:(t + 1) * TT].rearrange("k p n -> p k n"), w=hkeys)
            pb_ = pt[t % 2]
            for sub in range(4):
                r0 = t * TT + sub * 128
                self.dma(pb_[sub], self.p[l, r0:r0 + 128, :], w=[('pt', t % 2, sub)])
            pfb = pf[t % 2]
            for kc in range(2):
                b, pb = self.bank()
                for sub in range(4):
                    self.A('pe', lambda e, pb=pb, sub=sub, kc=kc, pb_=pb_: e.transpose(
                        out=pb[:, sub * 128:(sub + 1) * 128], in_=pb_[sub][:, kc * 128:(kc + 1) * 128],
                        identity=self.ident), r=[('pt', t % 2, sub), 'ident'], w=[('ps', b)])
                self.copy('act', pfb[:, kc * 512:(kc + 1) * 512], pb, r=[('ps', b)], w=[('pf', t % 2, kc)])
            for mo in range(8):
                bg, pg = self.bank()
                for k in range(8):
                    self.mm(pg, wg3[:, k, mo * 128:(mo + 1) * 128], self.xn_ap(k, t), k == 0, k == 7,
                            r=['wg', ('xn', k, t)], w=[('ps', bg)])
                bp, pp = self.bank()
                for kc in range(2):
                    self.mm(pp, wp3[:, kc, mo * 128:(mo + 1) * 128], pfb[:, kc * 512:(kc + 1) * 512],
                            kc == 0, kc == 1, r=['wp', ('pf', t % 2, kc)], w=[('ps', bp)])
                s_ = sg[it % 2]
                self.act(s_, pg, AF.Sigmoid, r=[('ps', bg)], w=[('sg', it % 2)])
                self.tt('dve', s_, s_, pp, ALU.mult, r=[('sg', it % 2), ('ps', bp)], w=[('sg', it % 2)])
                hk = h[:, mo * 512:(mo + 1) * 512]
                self.tt('pool', hk, hk, s_, ALU.add, r=[('sg', it % 2), hkeys[mo]], w=[hkeys[mo]])
                it += 1
            if not last:
                self.dma(self.H[:, :, t * TT:(t + 1) * TT].rearrange("k p n -> p k n"),
                         h.rearrange("p (k n) -> p k n", k=8), r=hkeys)
                self.rmsnorm_tile(h, hkeys, wcol, kw, lambda k, t=t: self.xn_ap(k, t),
                                  [('xn', k, t) for k in range(8)], tmp)
            else:
                y = yb[t % 2]
                ykeys = [('y', 0, k) for k in range(8)]
                self.rmsnorm_tile(h, hkeys, wcol, kw, lambda k, y=y: y[:, k * 512:(k + 1) * 512], ykeys, tmp)
                for sub in range(4):
                    ob = ot[sub % 2]
                    for half in range(2):
                        b, pb = self.bank()
                        for kk in range(4):
                            k = half * 4 + kk
                            self.A('pe', lambda e, pb=pb, kk=kk, k=k, sub=sub, y=y: e.transpose(
                                out=pb[:, kk * 128:(kk + 1) * 128],
                                in_=y[:, k * 512 + sub * 128:k * 512 + (sub + 1) * 128], identity=self.ident),
                                r=[ykeys[k], 'ident'], w=[('ps', b)])
                        self.copy('act' if half == 0 else 'dve', ob[:, half * 512:(half + 1) * 512], pb,
                                  r=[('ps', b)], w=[('ot', sub % 2, half)])
                    r0 = t * TT + sub * 128
                    self.dma(self.out[r0:r0 + 128, :], ob, r=[('ot', sub % 2, 0), ('ot', sub % 2, 1)])
        self.release(m)

    def stage_final(self):
        pass


_NC_CACHE = {}


def kernel(**inputs):
    if "nc" not in _NC_CACHE:
        nc = bass.Bass("TRN2", target_bir_lowering=False)
        kb = KB(nc)
        kb.build()
        _NC_CACHE["nc"] = nc
    nc = _NC_CACHE["nc"]
    consts = host_consts()
    x = np.ascontiguousarray(inputs["x"], dtype=np.float32)
    p = np.ascontiguousarray(inputs["p"], dtype=np.float32)
    in_maps = []
    for c in range(8):
        m = {"x": x[c], "p": np.ascontiguousarray(p[:, c])}
        for n in PARAM_NAMES:
            m[n] = np.ascontiguousarray(inputs[n], dtype=np.float32)
        m.update(consts)
        in_maps.append(m)
    res = run_bass_kernel_spmd(nc, in_maps, core_ids=list(range(8)))
    return np.stack([np.asarray(res.results[c]["out"]) for c in range(8)], axis=0).astype(np.float32)
```

```python
import contextlib
import numpy as np
import concourse.bass as bass
import concourse.mybir as mybir
from concourse.alu_op_type import AluOpType as ALU
from concourse.bass_utils import run_bass_kernel_spmd

F32 = mybir.dt.float32
BF16 = mybir.dt.bfloat16
AF = mybir.ActivationFunctionType

S = 4096
D = 1024
DFF = 2752
DEPTH = 2
TT = 512
NT = S // TT
IN_TOTAL = 6680
EPS = 1e-6

STREAMS = ['pe', 'act', 'dve', 'pool', 'sp']
NCH = 8
SEM_ROLL = 12000


class Sched:
    def __init__(self):
        self.ops = []
        self.ns = None

    @staticmethod
    def _nk(k, ns):
        if isinstance(k, tuple) and k[0] == 'ps':
            return k
        if isinstance(k, str) and (k.startswith('c_') or k in ('ident', 'ones')):
            return k
        return (ns, k)

    def add(self, eng, fn, reads=(), writes=(), dma=False):
        if self.ns is not None:
            reads = tuple(self._nk(k, self.ns) for k in reads)
            writes = tuple(self._nk(k, self.ns) for k in writes)
        self.ops.append(dict(eng=eng, fn=fn, reads=tuple(reads), writes=tuple(writes),
                             dma=dma, barrier=False))

    def barrier(self):
        self.ops.append(dict(barrier=True))

    def analyze(self):
        last_w = {}
        readers = {}
        last_on_stream = {}
        last_on_chan = {}
        pending = {s: set() for s in STREAMS}
        ch_rr = 0
        for i, op in enumerate(self.ops):
            if op['barrier']:
                allp = set(last_on_stream.values()) | set(last_on_chan.values())
                for s in STREAMS:
                    pending[s] |= allp
                last_w = {}
                readers = {}
                continue
            deps = {}
            eng = op['eng']
            for r in op['reads']:
                j = last_w.get(r)
                if j is not None:
                    deps[j] = 'RAW'
            for w in op['writes']:
                j = last_w.get(w)
                if j is not None and j not in deps:
                    deps[j] = 'WAW'
                for j in readers.get(w, ()):
                    if j not in deps:
                        deps[j] = 'WAR'
            for j in pending[eng]:
                if j not in deps:
                    deps[j] = 'BAR'
            pending[eng] = set()
            if op['dma']:
                op['chan'] = ch_rr
                ch_rr = (ch_rr + 1) % NCH
                j = last_on_chan.get(op['chan'])
                if j is not None:
                    deps[j] = 'BAR'
                last_on_chan[op['chan']] = i
            else:
                last_on_stream[eng] = i
            fdeps = []
            for j, kind in deps.items():
                pj = self.ops[j]
                if (not pj['dma']) and (not op['dma']) and pj['eng'] == eng:
                    if eng == 'pe' or kind != 'RAW':
                        continue
                fdeps.append(j)
            op['deps'] = fdeps
            for j in fdeps:
                self.ops[j]['needs_inc'] = True
            for r in op['reads']:
                readers.setdefault(r, []).append(i)
            for w in op['writes']:
                last_w[w] = i
                readers[w] = []
        self.last_on_chan = last_on_chan

    def emit(self, nc):
        self.analyze()
        with contextlib.ExitStack() as es:
            def newsem(name):
                return es.enter_context(nc.semaphore(name))

            cur = {s: [newsem(f"s_{s}_0"), 0, 0] for s in ['pe', 'act', 'dve', 'pool']}
            chs = [[newsem(f"s_ch{c}_0"), 0, 0] for c in range(NCH)]
            for op in self.ops:
                if op['barrier']:
                    continue
                if op['dma']:
                    st = chs[op['chan']]
                    nm = f"s_ch{op['chan']}"
                    inc = 16
                elif op.get('needs_inc'):
                    st = cur[op['eng']]
                    nm = f"s_{op['eng']}"
                    inc = 1
                else:
                    continue
                if st[1] + inc > SEM_ROLL:
                    st[2] += 1
                    st[0] = newsem(f"{nm}_{st[2]}")
                    st[1] = 0
                st[1] += inc
                op['sem'], op['val'], op['inc'] = st[0], st[1], inc
            lists = {s: [] for s in STREAMS}
            waited = {s: {} for s in STREAMS}
            for op in self.ops:
                if op['barrier']:
                    continue
                s = op['eng']
                for j in op['deps']:
                    pj = self.ops[j]
                    key = id(pj['sem'])
                    if waited[s].get(key, 0) >= pj['val']:
                        continue
                    waited[s][key] = pj['val']
                    lists[s].append(('wait', pj['sem'], pj['val']))
                lists[s].append(('op', op))
            for c, j in self.last_on_chan.items():
                pj = self.ops[j]
                lists['sp'].append(('wait', pj['sem'], pj['val']))

            def run(e, items):
                for it in items:
                    if it[0] == 'wait':
                        e.wait_ge(it[1], it[2])
                    else:
                        op = it[1]
                        ins = op['fn'](e)
                        if 'sem' in op:
                            ins.then_inc(op['sem'], op['inc'])

            with nc.Block() as block:
                @block.tensor
                def _(e):
                    run(e, lists['pe'])

                @block.scalar
                def _(e):
                    run(e, lists['act'])

                @block.vector
                def _(e):
                    run(e, lists['dve'])

                @block.gpsimd
                def _(e):
                    run(e, lists['pool'])

                @block.sync
                def _(e):
                    run(e, lists['sp'])


PARAM_NAMES = ["ffn1_norm", "ffn1_gate", "ffn1_up", "ffn1_down", "mix_norm", "w_in",
               "s5_lam_re", "s5_lam_im", "s5_log_step", "s5_b_re", "s5_b_im", "s5_c_re", "s5_c_im",
               "s5_d", "s5_glu", "gla_gate_w2", "gla_gate_b2", "gla_norm", "ssd_conv_w", "ssd_conv_b",
               "ssd_dt_bias", "ssd_a_log", "ssd_d", "ssd_norm", "w_br_s5", "w_br_gla", "w_br_ssd",
               "w_out", "ffn2_norm", "ffn2_gate", "ffn2_up", "ffn2_down", "ple_norm", "ple_gate",
               "ple_proj", "final_norm"]
PARAM_SHAPES = {
    "ffn1_norm": (2, 1024), "ffn1_gate": (2, 1024, 2752), "ffn1_up": (2, 1024, 2752),
    "ffn1_down": (2, 2752, 1024), "mix_norm": (2, 1024), "w_in": (2, 1024, 6680),
    "s5_lam_re": (2, 32, 64), "s5_lam_im": (2, 32, 64), "s5_log_step": (2, 32),
    "s5_b_re": (2, 32, 64, 16), "s5_b_im": (2, 32, 64, 16), "s5_c_re": (2, 32, 16, 64),
    "s5_c_im": (2, 32, 16, 64), "s5_d": (2, 512), "s5_glu": (2, 512, 512),
    "gla_gate_w2": (2, 16, 256), "gla_gate_b2": (2, 256), "gla_norm": (2, 128),
    "ssd_conv_w": (2, 4, 1024), "ssd_conv_b": (2, 1024), "ssd_dt_bias": (2, 8), "ssd_a_log": (2, 8),
    "ssd_d": (2, 8), "ssd_norm": (2, 512), "w_br_s5": (2, 512, 1024), "w_br_gla": (2, 512, 1024),
    "w_br_ssd": (2, 512, 1024), "w_out": (2, 1024, 1024), "ffn2_norm": (2, 1024),
    "ffn2_gate": (2, 1024, 2752), "ffn2_up": (2, 1024, 2752), "ffn2_down": (2, 2752, 1024),
    "ple_norm": (2, 1024), "ple_gate": (2, 1024, 1024), "ple_proj": (2, 256, 1024),
    "final_norm": (1024,),
}


def host_consts():
    c = {}
    c["c_ident"] = np.eye(128, dtype=np.float32)
    i = np.arange(128)
    c["c_tri128"] = (i[:, None] <= i[None, :]).astype(np.float32)
    same = (i[:, None] // 64) == (i[None, :] // 64)
    c["c_tri64"] = ((i[:, None] <= i[None, :]) & same).astype(np.float32)
    c["c_blk64"] = same.astype(np.float32)
    c["c_mask64"] = ((i[:, None] % 64) <= np.arange(64)[None, :]).astype(np.float32)
    c["c_ones"] = np.ones((128, 128), np.float32)
    g = np.arange(512) // 16
    c["c_selb"] = (np.arange(32)[:, None] == g[None, :]).astype(np.float32)
    c["c_gmask"] = ((i[:, None] // 16) == np.arange(8)[None, :]).astype(np.float32)
    c["c_hmask"] = ((i[:, None] // 64) == np.arange(2)[None, :]).astype(np.float32)
    c["c_iota"] = np.tile(np.arange(512, dtype=np.float32)[None, :], (128, 1))
    return c


class KB:
    def __init__(self, nc, debug=False, stages=None):
        self.nc = nc
        self.sc = Sched()
        self.debug = debug
        self.stages = stages
        self.pbank = 0
        self.uid = 0

    def alloc(self, n, dt=F32):
        if dt == BF16:
            m = (n + 1) // 2
            a = self.arena[:, self.off:self.off + m].bitcast(BF16)
        else:
            m = n
            a = self.arena[:, self.off:self.off + m]
        self.off += m
        assert self.off <= self.arena_n, f"arena overflow {self.off}"
        return a

    def mark(self):
        return self.off

    def release(self, m):
        self.off = m
        self.sc.barrier()

    def bank(self):
        b = self.pbank % 8
        self.pbank += 1
        return b, self.ps[:, b * 512:(b + 1) * 512]

    def key(self, name):
        self.uid += 1
        return (name, self.uid)

    def dram(self, name, shape, dt):
        kind = "ExternalOutput" if (self.debug and name in self.debug) else "Internal"
        return self.nc.dram_tensor(name, list(shape), dt, kind=kind).ap()

    def A(self, eng, fn, r=(), w=()):
        self.sc.add(eng, fn, r, w)

    def dma(self, out, in_, r=(), w=(), slow=False):
        if slow:
            self.sc.add('sp', lambda e: e.dma_start(out=out, in_=in_, allow_slow_non_contiguous=True), r, w, dma=True)
        else:
            self.sc.add('sp', lambda e: e.dma_start(out=out, in_=in_), r, w, dma=True)

    def mm(self, out, lhsT, rhs, start, stop, r, w):
        self.sc.add('pe', lambda e: e.matmul(out, lhsT=lhsT, rhs=rhs, start=start, stop=stop), r, w)

    def act(self, out, in_, func, r, w, bias=None, scale=None):
        kw = {}
        if bias is not None:
            kw['bias'] = bias
        if scale is not None:
            kw['scale'] = scale
        self.sc.add('act', lambda e: e.activation(out=out, in_=in_, func=func, **kw), r, w)

    def tt(self, eng, out, in0, in1, op, r, w):
        self.sc.add(eng, lambda e: e.tensor_tensor(out=out, in0=in0, in1=in1, op=op), r, w)

    def ts(self, eng, out, in0, s1, s2, op0, op1, r, w):
        if op1 is None:
            self.sc.add(eng, lambda e: e.tensor_scalar(out=out, in0=in0, scalar1=s1, scalar2=None, op0=op0), r, w)
        else:
            self.sc.add(eng, lambda e: e.tensor_scalar(out=out, in0=in0, scalar1=s1, scalar2=s2, op0=op0, op1=op1), r, w)

    def stt(self, out, in0, scalar, in1, op0, op1, r, w):
        self.sc.add('dve', lambda e: e.scalar_tensor_tensor(out=out, in0=in0, scalar=scalar, in1=in1,
                                                           op0=op0, op1=op1), r, w)

    def copy(self, eng, out, in_, r, w):
        if eng == 'act':
            self.sc.add('act', lambda e: e.activation(out=out, in_=in_, func=AF.Copy), r, w)
        else:
            self.sc.add(eng, lambda e: e.tensor_copy(out=out, in_=in_), r, w)

    def load_cols(self, dst, vec_ap, nk):
        k = self.key('col')
        self.dma(dst, vec_ap.rearrange("(k p) -> p k", p=128), w=[k], slow=True)
        return k

    def load_weight(self, w_ap, kc_sizes, c0, ncols, stg, wb, kstg, kwb, cast_eng='pool'):
        KC = len(kc_sizes)
        full = [i for i, s in enumerate(kc_sizes) if s == 128]
        nf = len(full)
        stg3 = stg[:, :KC * ncols].rearrange("p (k n) -> p k n", k=KC)
        wb3 = wb[:, :KC * ncols].rearrange("p (k n) -> p k n", k=KC)
        if nf > 0:
            self.dma(stg3[:, :nf, :], w_ap[0:nf * 128, c0:c0 + ncols].rearrange("(k p) n -> p k n", p=128),
                     w=[kstg])
            self.copy(cast_eng, wb3[:, :nf, :], stg3[:, :nf, :], r=[kstg], w=[kwb])
        if nf < KC:
            rem = kc_sizes[-1]
            self.dma(stg3[:rem, nf, :], w_ap[nf * 128:nf * 128 + rem, c0:c0 + ncols], w=[kstg])
            self.copy(cast_eng, wb3[:rem, nf, :], stg3[:rem, nf, :], r=[kstg], w=[kwb])
        return wb3

    def rmsnorm_tile(self, h, hkeys, wcol, kw, xn_out, xnkeys, tmp):
        sq, sd, rstd = tmp
        ksq = [self.key('sq') for _ in range(8)]
        for k in range(8):
            self.act(sq[:, k * 512:(k + 1) * 512], h[:, k * 512:(k + 1) * 512], AF.Square,
                     r=[hkeys[k]], w=[ksq[k]])
        b, pb = self.bank()
        for k in range(8):
            self.mm(pb, self.ones_b, sq[:, k * 512:(k + 1) * 512], k == 0, k == 7,
                    r=[ksq[k], 'ones'], w=[('ps', b)])
        ksd = self.key('sd')
        self.act(sd, pb, AF.Sqrt, r=[('ps', b)], w=[ksd], bias=EPS, scale=1.0 / D)
        krs = self.key('rstd')
        self.A('dve', lambda e: e.reciprocal(out=rstd, in_=sd), r=[ksd], w=[krs])
        for k in range(8):
            eng = 'dve'
            self.stt(xn_out(k), h[:, k * 512:(k + 1) * 512], wcol[:, k:k + 1], rstd, ALU.mult, ALU.mult,
                     r=[hkeys[k], krs, kw], w=[xnkeys[k]])

    def build(self):
        nc = self.nc
        self.x = nc.dram_tensor("x", [S, D], F32, kind="ExternalInput").ap()
        self.p = nc.dram_tensor("p", [DEPTH, S, 256], F32, kind="ExternalInput").ap()
        self.prm = {}
        for n in PARAM_NAMES:
            self.prm[n] = nc.dram_tensor(n, list(PARAM_SHAPES[n]), F32, kind="ExternalInput").ap()
        self.cst = {}
        for n, v in host_consts().items():
            self.cst[n] = nc.dram_tensor(n, list(v.shape), F32, kind="ExternalInput").ap()
        self.out = nc.dram_tensor("out", [S, D], F32, kind="ExternalOutput").ap()
        self.H = self.dram("H", [8, 128, S], F32)
        self.HID = self.dram("HID", [22, 128, S], BF16)
        self.US5 = self.dram("US5", [4, 128, S], BF16)
        self.Q = self.dram("Q", [2, 128, S], BF16)
        self.Kf = self.dram("Kf", [2, 128, S], BF16)
        self.KT = self.dram("KT", [S, 256], BF16)
        self.VT = self.dram("VT", [S, 512], BF16)
        self.GO = self.dram("GO", [4, 128, S], BF16)
        self.GLR = self.dram("GLR", [1, 128, S], F32)
        self.Z = self.dram("Z", [4, 128, S], BF16)
        self.XBC = self.dram("XBC", [8, 128, S], F32)
        self.DTT = self.dram("DTT", [S, 8], F32)
        self.SIG = self.dram("SIG", [24, 128, S], BF16)
        self.Y = self.dram("Y", [3, 4, 128, S], BF16)
        self.S5T = self.dram("S5T", [2, 32, 64], F32)
        self.arena_n = 52000
        with nc.sbuf_tensor("arena", [128, self.arena_n], F32) as arena, \
                nc.psum_tensor("ps", [128, 4096], F32) as ps:
            self.arena = arena
            self.ps = ps
            self.off = 0
            self.ident = self.alloc(128)
            self.ones_b = self.alloc(128, BF16)
            self.eps_col = self.alloc(1)
            self.dma(self.ident, self.cst["c_ident"], w=['ident'])
            self.A('pool', lambda e: e.memset(self.ones_b, 1.0), w=['ones'])
            self.A('pool', lambda e: e.memset(self.eps_col, EPS), w=['eps'])
            self.alloc_consts()
            self.XNf = self.arena[:, self.off:self.off + 4 * S]
            self.XN = self.alloc(8 * S, BF16)
            self.base = self.mark()
            self.sc.barrier()

            self.stage_input()
            if self.debug:
                self.stage_ffn(0, 1)
                self.stage_mixer(0)
            else:
                for l in range(DEPTH):
                    self.stage_ffn(l, 1)
                    self.stage_mixer(l)
                    self.stage_ffn(l, 2)
                    self.stage_ple(l)
            self.sc.emit(nc)
        return nc

    def xn_ap(self, k, t):
        return self.XN[:, k * S + t * TT:k * S + (t + 1) * TT]

    def stage_input(self):
        m = self.mark()
        xt = [[self.alloc(1024) for _ in range(4)] for _ in range(2)]
        hb = [self.alloc(8 * 512) for _ in range(2)]
        tmp = (self.alloc(8 * 512, BF16), self.alloc(512), self.alloc(512))
        wcol = self.alloc(8)
        kw = self.load_cols(wcol, self.prm["ffn1_norm"][0], 8)
        for t in range(NT):
            xb = xt[t % 2]
            h = hb[t % 2]
            for sub in range(4):
                r0 = t * TT + sub * 128
                self.dma(xb[sub], self.x[r0:r0 + 128, :], w=[('xt', t % 2, sub)])
            hkeys = [('h', t % 2, k) for k in range(8)]
            for k in range(8):
                b, pb = self.bank()
                for sub in range(4):
                    self.A('pe', lambda e, pb=pb, sub=sub, k=k, xb=xb: e.transpose(
                        out=pb[:, sub * 128:(sub + 1) * 128], in_=xb[sub][:, k * 128:(k + 1) * 128],
                        identity=self.ident), r=[('xt', t % 2, sub), 'ident'], w=[('ps', b)])
                self.copy('act' if k % 2 == 0 else 'dve', h[:, k * 512:(k + 1) * 512], pb,
                          r=[('ps', b)], w=[hkeys[k]])
            self.dma(self.H[:, :, t * TT:(t + 1) * TT].rearrange("k p n -> p k n"),
                     h.rearrange("p (k n) -> p k n", k=8), r=hkeys)
            self.rmsnorm_tile(h, hkeys, wcol, kw, lambda k, t=t: self.xn_ap(k, t),
                              [('xn', k, t) for k in range(8)], tmp)
        self.release(m)

    def stage_ffn(self, l, which):
        pre = f"ffn{which}_"
        Wg, Wu, Wd = self.prm[pre + "gate"][l], self.prm[pre + "up"][l], self.prm[pre + "down"][l]
        m0 = self.mark()
        kc_sizes = [128] * 21 + [64]
        wd = self.alloc(22 * 1024, BF16)
        wd3 = wd.rearrange("p (k n) -> p k n", k=22)
        m = self.mark()
        stgd = self.alloc(8 * 512)
        stg = [self.alloc(8 * 512) for _ in range(2)]
        wgb = [self.alloc(8 * 512, BF16) for _ in range(2)]
        wub = [self.alloc(8 * 512, BF16) for _ in range(2)]
        sil = [self.alloc(512) for _ in range(2)]
        hid = [self.alloc(512, BF16) for _ in range(4)]
        groups = [(c0, min(512, DFF - c0)) for c0 in range(0, DFF, 512)]
        loaded = {}

        def issue_load(gi):
            c0, ncols = groups[gi]
            s_ = gi % 2
            g3 = self.load_weight(Wg, [128] * 8, c0, ncols, stg[0], wgb[s_], ('stg', 0), ('wgb', s_))
            u3 = self.load_weight(Wu, [128] * 8, c0, ncols, stg[1], wub[s_], ('stg', 1), ('wub', s_))
            loaded[gi] = (g3, u3)

        def issue_wd(q):
            k0 = q * 4
            nk = min(4, 22 - k0)
            st = stgd[:, :nk * 1024].rearrange("p (k n) -> p k n", k=nk)
            for kk in range(nk):
                rows = kc_sizes[k0 + kk]
                self.dma(st[:rows, kk, :], Wd[(k0 + kk) * 128:(k0 + kk) * 128 + rows, :], w=['stgd'])
            if k0 + nk == 22:
                self.copy('pool', wd3[:, k0:k0 + nk - 1, :], st[:, :nk - 1, :], r=['stgd'], w=['wd'])
                self.copy('pool', wd3[:64, 21, :], st[:64, nk - 1, :], r=['stgd'], w=['wd'])
            else:
                self.copy('pool', wd3[:, k0:k0 + nk, :], st, r=['stgd'], w=['wd'])

        issue_load(0)
        it = 0
        hi = 0
        for gi, (c0, ncols) in enumerate(groups):
            s = gi % 2
            if gi + 1 < len(groups):
                issue_load(gi + 1)
            issue_wd(gi)
            g3, u3 = loaded[gi]
            for t in range(NT):
                for mc in range((ncols + 127) // 128):
                    mw = min(128, ncols - mc * 128)
                    j = (c0 // 128) + mc
                    bg, pg = self.bank()
                    for k in range(8):
                        self.mm(pg[:mw, :], g3[:, k, mc * 128:mc * 128 + mw], self.xn_ap(k, t), k == 0, k == 7,
                                r=[('wgb', s), ('xn', k, t)], w=[('ps', bg)])
                    bu, pu = self.bank()
                    for k in range(8):
                        self.mm(pu[:mw, :], u3[:, k, mc * 128:mc * 128 + mw], self.xn_ap(k, t), k == 0, k == 7,
                                r=[('wub', s), ('xn', k, t)], w=[('ps', bu)])
                    sl = sil[it % 2]
                    hd = hid[hi % 4]
                    self.act(sl[:mw, :], pg[:mw, :], AF.Silu, r=[('ps', bg)], w=[('sil', it % 2)])
                    self.tt('dve', hd[:mw, :], sl[:mw, :], pu[:mw, :], ALU.mult,
                            r=[('sil', it % 2), ('ps', bu)], w=[('hid', hi % 4)])
                    self.dma(self.HID[j, :mw, t * TT:(t + 1) * TT], hd[:mw, :], r=[('hid', hi % 4)],
                             w=[('HID', j, t)])
                    it += 1
                    hi += 1
        assert len(groups) == 6
        self.release(m)
        hidt = [self.alloc(22 * 512, BF16) for _ in range(2)]
        hb = [self.alloc(8 * 512) for _ in range(2)]
        tmp = (self.alloc(8 * 512, BF16), self.alloc(512), self.alloc(512))
        wcol = self.alloc(8)
        nxt = self.prm["mix_norm"][l] if which == 1 else self.prm["ple_norm"][l]
        kw = self.load_cols(wcol, nxt, 8)

        def issue_tile_loads(t):
            ht3 = hidt[t % 2].rearrange("p (k n) -> p k n", k=22)
            self.dma(ht3[:, :21, :], self.HID[0:21, :, t * TT:(t + 1) * TT].rearrange("k p n -> p k n"),
                     w=[('hidt', t % 2)])
            self.dma(ht3[:64, 21, :], self.HID[21, :64, t * TT:(t + 1) * TT], w=[('hidt', t % 2)])
            self.dma(hb[t % 2].rearrange("p (k n) -> p k n", k=8),
                     self.H[:, :, t * TT:(t + 1) * TT].rearrange("k p n -> p k n"),
                     w=[('h', t % 2, k) for k in range(8)])

        issue_tile_loads(0)
        for t in range(NT):
            if t + 1 < NT:
                issue_tile_loads(t + 1)
            ht3 = hidt[t % 2].rearrange("p (k n) -> p k n", k=22)
            h = hb[t % 2]
            hkeys = [('h', t % 2, k) for k in range(8)]
            for mo in range(8):
                b, pb = self.bank()
                for j in range(22):
                    rows = kc_sizes[j]
                    self.mm(pb, wd3[:rows, j, mo * 128:(mo + 1) * 128], ht3[:rows, j, :], j == 0, j == 21,
                            r=['wd', ('hidt', t % 2)], w=[('ps', b)])
                hk = h[:, mo * 512:(mo + 1) * 512]
                self.stt(hk, pb, 0.5, hk, ALU.mult, ALU.add, r=[('ps', b), hkeys[mo]], w=[hkeys[mo]])
            self.dma(self.H[:, :, t * TT:(t + 1) * TT].rearrange("k p n -> p k n"),
                     h.rearrange("p (k n) -> p k n", k=8), r=hkeys)
            self.rmsnorm_tile(h, hkeys, wcol, kw, lambda k, t=t: self.xn_ap(k, t),
                              [('xn', k, t) for k in range(8)], tmp)
        self.release(m0)

    def stage_mixer(self, l):
        st = self.stages
        self.stage_proj(l)
        def run_gens(gens):
            m_ = self.mark()
            while gens:
                for it_ in list(gens):
                    self.sc.ns = it_[0]
                    try:
                        next(it_[1])
                    except StopIteration:
                        gens.remove(it_)
            self.sc.ns = None
            self.release(m_)

        if st is None or 'ssd' in st:
            run_gens([('ssd', self.stage_ssd(l))])
        m2 = self.mark()
        gens = []
        if st is None or 's5' in st:
            C_ = self.s5_pre(l)
            gens.append(('s5', self.s5_main(l, C_)))
        if st is None or 'gla' in st:
            gens.append(('gla', self.stage_gla(l)))
        run_gens(gens)
        self.release(m2)
        if st is None or 'merge' in st:
            self.stage_merge(l)
        else:
            self.renorm(self.prm["ffn2_norm"][l])

    def load_consts(self):
        return

    def alloc_consts(self):
        def ld(name, n, rows=128):
            a = self.alloc(n)
            self.dma(a[:rows, :], self.cst[name], w=[name])
            return a
        self.tri128 = ld("c_tri128", 128)
        self.tri64 = ld("c_tri64", 128)
        self.blk64 = ld("c_blk64", 128)
        self.mask64 = ld("c_mask64", 64)
        self.onesf = ld("c_ones", 128)
        self.gmask = ld("c_gmask", 8)
        self.hmask = ld("c_hmask", 2)

    def stage_proj(self, l):
        W = self.prm["w_in"][l]
        m = self.mark()
        stg = [self.alloc(8 * 512) for _ in range(2)]
        wbb = [self.alloc(8 * 512, BF16) for _ in range(2)]
        of = [self.alloc(512) for _ in range(3)]
        ob = [self.alloc(512, BF16) for _ in range(3)]
        segs = [(0, 512, AF.Copy, 1.0, self.US5, BF16), (512, 256, AF.Copy, 0.125, self.Q, BF16),
                (768, 256, AF.Copy, 1.0, self.Kf, BF16), (1536, 512, AF.Silu, 1.0, self.GO, BF16),
                (2048, 16, AF.Copy, 1.0, self.GLR, F32), (2064, 512, AF.Silu, 1.0, self.Z, BF16),
                (2576, 1024, AF.Copy, 1.0, self.XBC, F32), (3608, 3072, AF.Sigmoid, 1.0, self.SIG, BF16)]
        work = []
        for (c0s, n, func, scale, dest, dt) in segs:
            for g0 in range(0, n, 512):
                work.append(('fm', c0s + g0, min(512, n - g0), func, scale, dest, dt, g0))
        for (c0s, n, dest, dt) in [(768, 256, self.KT, BF16), (1024, 512, self.VT, BF16), (3600, 8, self.DTT, F32)]:
            work.append(('tm', c0s, n, None, None, dest, dt, 0))
        loaded = {}

        def issue_load(gi):
            kind, c0, ncols = work[gi][0], work[gi][1], work[gi][2]
            sidx = gi % 2
            loaded[gi] = self.load_weight(W, [128] * 8, c0, ncols, stg[sidx], wbb[sidx], ('stg', sidx),
                                          ('wbb', sidx))

        issue_load(0)
        oi = 0
        for gi, (kind, c0, ncols, func, scale, dest, dt, g0) in enumerate(work):
            sidx = gi % 2
            if gi + 1 < len(work):
                issue_load(gi + 1)
            w3 = loaded[gi]
            if kind == 'fm':
                for t in range(NT):
                    for mc in range((ncols + 127) // 128):
                        mw = min(128, ncols - mc * 128)
                        j = g0 // 128 + mc
                        b, pb = self.bank()
                        for k in range(8):
                            self.mm(pb[:mw, :], w3[:, k, mc * 128:mc * 128 + mw], self.xn_ap(k, t), k == 0, k == 7,
                                    r=[('wbb', sidx), ('xn', k, t)], w=[('ps', b)])
                        o = (of if dt == F32 else ob)[oi % 3]
                        okey = ('of' if dt == F32 else 'ob', oi % 3)
                        oi += 1
                        self.act(o[:mw, :], pb[:mw, :], func, r=[('ps', b)], w=[okey], scale=scale)
                        self.dma(dest[j, :mw, t * TT:(t + 1) * TT], o[:mw, :], r=[okey])
            else:
                n = ncols
                for blk in range(S // 128):
                    t, sub = blk // 4, blk % 4
                    b, pb = self.bank()
                    for k in range(8):
                        xs_ = self.XN[:, k * S + blk * 128:k * S + (blk + 1) * 128]
                        self.mm(pb[:, :n], xs_, w3[:, k, :n], k == 0, k == 7, r=[('wbb', sidx), ('xn', k, t)],
                                w=[('ps', b)])
                    o = (of if dt == F32 else ob)[oi % 3]
                    okey = ('of' if dt == F32 else 'ob', oi % 3)
                    oi += 1
                    self.copy('act' if blk % 2 == 0 else 'dve', o[:, :n], pb[:, :n], r=[('ps', b)], w=[okey])
                    self.dma(dest[blk * 128:(blk + 1) * 128, :], o[:, :n], r=[okey])
        self.release(m)

    def stage_ssd(self, l):
        P = self.prm
        f3 = lambda a, k: a.rearrange("p (k n) -> p k n", k=k)
        cw = self.alloc(32)
        cb = self.alloc(8)
        for k in range(4):
            self.dma(cw[:, k * 8:(k + 1) * 8], P["ssd_conv_w"][l][k].rearrange("(c p) -> p c", p=128), w=['cw'],
                     slow=True)
        self.dma(cb, P["ssd_conv_b"][l].rearrange("(c p) -> p c", p=128), w=['cb'], slow=True)
        dtb = self.alloc(8)
        abc = self.alloc(8)
        dbc = self.alloc(8)
        self.dma(dtb, P["ssd_dt_bias"][l].partition_broadcast(128), w=['dtb'], slow=True)
        self.dma(abc, P["ssd_a_log"][l].partition_broadcast(128), w=['abc'], slow=True)
        self.dma(dbc, P["ssd_d"][l].partition_broadcast(128), w=['dbc'], slow=True)
        self.act(abc, abc, AF.Exp, r=['abc'], w=['abc'])
        self.ts('dve', abc, abc, -1.0, None, ALU.mult, None, r=['abc'], w=['abc'])
        dcol = self.alloc(4)
        for i in range(4):
            self.copy('dve', dcol[0:64, i:i + 1], dbc[0:64, 2 * i:2 * i + 1], r=['dbc'], w=['dcol'])
            self.copy('dve', dcol[64:128, i:i + 1], dbc[64:128, 2 * i + 1:2 * i + 2], r=['dbc'], w=['dcol'])
        nw = self.alloc(4)
        self.dma(nw, P["ssd_norm"][l].rearrange("(c p) -> p c", p=128), w=['nw'], slow=True)
        hT = [self.alloc(256) for _ in range(2)]
        hTb = [self.alloc(256, BF16) for _ in range(2)]
        for g in range(2):
            self.A('pool', lambda e, g=g: e.memset(hT[g], 0.0), w=[('hT', g)])
            self.A('pool', lambda e, g=g: e.memset(hTb[g], 0.0), w=[('hTb', g)])
        xin = [self.alloc(8 * 520) for _ in range(1)]
        acc = self.alloc(512)
        xc = self.alloc(8 * 512)
        xcb = self.alloc(4 * 512, BF16)
        zt = self.alloc(4 * 512, BF16)
        ybuf = self.alloc(4 * 512, BF16)
        dtr = self.alloc(8)
        dt = self.alloc(8)
        la = self.alloc(8)
        cum = self.alloc(8)
        latri = self.alloc(1024)
        dec = self.alloc(1024)
        ecr = self.alloc(1024)
        ecl = self.alloc(8)
        ds = self.alloc(8)
        Cp = self.alloc(1024, BF16)
        scm = self.alloc(256)
        SdT = self.alloc(1024, BF16)
        xdt = self.alloc(512, BF16)
        xdtd = self.alloc(512, BF16)
        Bt = self.alloc(256, BF16)
        yv = self.alloc(512)
        sq = self.alloc(512, BF16)
        sd = self.alloc(256)
        rstd = self.alloc(256)
        xc3 = f3(xc, 8)
        xcb3 = f3(xcb, 4)
        for t in range(NT):
            t0 = t * TT
            xi3 = xin[0].rearrange("p (k n) -> p k n", k=8)
            if t == 0:
                self.A('pool', lambda e: e.memset(xi3[:, :, 0:3], 0.0), w=['xin'])
                self.dma(xi3[:, :, 3:515], self.XBC[:, :, 0:TT].rearrange("k p n -> p k n"), w=['xin'])
            else:
                self.dma(xi3[:, :, 0:515], self.XBC[:, :, t0 - 3:t0 + TT].rearrange("k p n -> p k n"), w=['xin'])
            self.dma(f3(zt, 4), self.Z[:, :, t0:t0 + TT].rearrange("k p n -> p k n"), w=['zt'])
            for c in range(8):
                self.ts('dve', acc, xi3[:, c, 3:515], cw[:, 24 + c:25 + c], None, ALU.mult, None,
                        r=['xin', 'cw'], w=['acc'])
                for k in (2, 1, 0):
                    self.stt(acc, xi3[:, c, k:k + 512], cw[:, k * 8 + c:k * 8 + c + 1], acc, ALU.mult, ALU.add,
                             r=['xin', 'cw', 'acc'], w=['acc'])
                self.act(xc3[:, c, :], acc, AF.Silu, r=['acc', 'cb'], w=[('xc', c)], bias=cb[:, c:c + 1])
                if c >= 4:
                    self.copy('act', xcb3[:, c - 4, :], xc3[:, c, :], r=[('xc', c)], w=[('xcb', c)])
            for sub in range(4):
                r0 = t0 + sub * 128
                tk = slice(sub * 128, (sub + 1) * 128)
                self.dma(dtr, self.DTT[r0:r0 + 128, :], w=['dtr'])
                self.tt('dve', dt, dtr, dtb, ALU.add, r=['dtr', 'dtb'], w=['dt'])
                self.act(dt, dt, AF.Exp, r=['dt'], w=['dt'])
                self.act(dt, dt, AF.Ln, r=['dt'], w=['dt'], bias=1.0)
                self.tt('dve', la, dt, abc, ALU.mult, r=['dt', 'abc'], w=['la'])
                bc_, pc = self.bank()
                self.mm(pc[:, 0:8], self.tri128, la, True, True, r=['c_tri128', 'la'], w=[('ps', bc_)])
                self.copy('dve', cum, pc[:, 0:8], r=[('ps', bc_)], w=['cum'])
                lt3 = f3(latri, 8)
                for r in range(8):
                    if r % 2:
                        self.act(lt3[:, r, :], self.tri128, AF.Copy, r=['c_tri128', 'la'], w=[('latri', r)],
                                 scale=la[:, r:r + 1])
                    else:
                        self.ts('dve', lt3[:, r, :], self.tri128, la[:, r:r + 1], None, ALU.mult, None,
                                r=['c_tri128', 'la'], w=[('latri', r)])
                b1, p1 = self.bank()
                b2, p2 = self.bank()
                self.mm(p1, self.onesf, latri[:, 0:512], True, True, r=['c_ones'] + [('latri', r) for r in range(4)],
                        w=[('ps', b1)])
                self.mm(p2, self.onesf, latri[:, 512:1024], True, True,
                        r=['c_ones'] + [('latri', r) for r in range(4, 8)], w=[('ps', b2)])
                pr = [f3(p1, 4), f3(p2, 4)]
                d3 = f3(dec, 8)
                e3 = f3(ecr, 8)
                for r in range(8):
                    self.ts('dve', d3[:, r, :], pr[r // 4][:, r % 4, :], cum[:, r:r + 1], 0.0, ALU.subtract, ALU.min,
                            r=[('ps', b1 if r < 4 else b2), 'cum'], w=[('dec', r // 4)])
                for hh in range(2):
                    self.act(dec[:, hh * 512:(hh + 1) * 512], dec[:, hh * 512:(hh + 1) * 512], AF.Exp,
                             r=[('dec', hh)], w=[('dec', hh)])
                    self.act(ecr[:, hh * 512:(hh + 1) * 512], [p1, p2][hh], AF.Exp, r=[('ps', [b1, b2][hh])],
                             w=[('ecr', hh)])
                    self.copy('dve', ecl[:, hh * 4:(hh + 1) * 4], e3[:, hh * 4:(hh + 1) * 4, 127], r=[('ecr', hh)],
                              w=['ecl'])
                    self.tt('dve', ds[:, hh * 4:(hh + 1) * 4], pr[hh][:, :, 127], cum[:, hh * 4:(hh + 1) * 4],
                            ALU.subtract, r=[('ps', [b1, b2][hh]), 'cum'], w=['ds'])
                self.act(ds, ds, AF.Exp, r=['ds'], w=['ds'])
                C3 = f3(Cp, 8)
                for g in range(2):
                    cin = xc3[:, 6 + g, tk].unsqueeze(1).broadcast_to([128, 4, 128])
                    self.tt('pool', C3[:, 4 * g:4 * g + 4, :], e3[:, 4 * g:4 * g + 4, :], cin, ALU.mult,
                            r=[('ecr', g), ('xc', 6 + g)], w=[('Cp', g)])
                bs, psc = self.bank()
                for g in range(2):
                    self.mm(psc[:, g * 128:(g + 1) * 128], xcb3[:, g, tk], xcb3[:, 2 + g, tk], True, True,
                            r=[('xcb', 4 + g), ('xcb', 6 + g)], w=[('ps', bs)])
                s3 = f3(scm, 2)
                self.tt('dve', s3, f3(psc[:, 0:256], 2), self.tri128.unsqueeze(1).broadcast_to([128, 2, 128]), ALU.mult,
                        r=[('ps', bs), 'c_tri128'], w=['scm'])
                S3 = f3(SdT, 8)
                for g in range(2):
                    self.tt('pool' if g else 'dve', S3[:, 4 * g:4 * g + 4, :], d3[:, 4 * g:4 * g + 4, :],
                            s3[:, g, :].unsqueeze(1).broadcast_to([128, 4, 128]), ALU.mult,
                            r=[('dec', g), 'scm'], w=[('SdT', g)])
                bx, px = self.bank()
                for c in range(4):
                    self.A('pe', lambda e, px=px, c=c, tk=tk: e.transpose(out=px[:, c * 128:(c + 1) * 128],
                                                                          in_=xc3[:, c, tk], identity=self.ident),
                           r=[('xc', c), 'ident'], w=[('ps', bx)])
                bb, pbt = self.bank()
                for c in range(2):
                    self.A('pe', lambda e, pbt=pbt, c=c, tk=tk: e.transpose(out=pbt[:, c * 128:(c + 1) * 128],
                                                                            in_=xc3[:, 4 + c, tk], identity=self.ident),
                           r=[('xc', 4 + c), 'ident'], w=[('ps', bb)])
                x3 = f3(xdt, 8)
                xd3 = f3(xdtd, 8)
                self.tt('dve', x3, f3(px, 8), dt.unsqueeze(2).broadcast_to([128, 8, 64]), ALU.mult,
                        r=[('ps', bx), 'dt'], w=['xdt'])
                self.tt('pool', xd3, x3, ds.unsqueeze(2).broadcast_to([128, 8, 64]), ALU.mult, r=['xdt', 'ds'],
                        w=['xdtd'])
                self.copy('act', Bt, pbt[:, 0:256], r=[('ps', bb)], w=['Bt'])
                by, py = self.bank()
                for i in range(4):
                    for r2 in range(2):
                        r = 2 * i + r2
                        g = r // 4
                        o_ = py[64 * r2:64 * r2 + 64, i * 128:(i + 1) * 128]
                        self.mm(o_, x3[:, r, :], S3[:, r, :], True, False, r=['xdt', ('SdT', g)], w=[('ps', by)])
                        self.mm(o_, hTb[g][:, (r % 4) * 64:(r % 4) * 64 + 64], C3[:, r, :], False, True,
                                r=[('hTb', g), ('Cp', g)], w=[('ps', by)])
                for g in range(2):
                    bst, pst = self.bank()
                    self.mm(pst[:, 0:256], Bt[:, g * 128:(g + 1) * 128], xdtd[:, g * 256:(g + 1) * 256], True, True,
                            r=['Bt', 'xdtd'], w=[('ps', bst)])
                    h3 = f3(hT[g], 4)
                    self.tt('dve', h3, h3, ecl[:, 4 * g:4 * g + 4].unsqueeze(2).broadcast_to([128, 4, 64]), ALU.mult,
                            r=[('hT', g), 'ecl'], w=[('hT', g)])
                    self.tt('dve', hT[g], hT[g], pst[:, 0:256], ALU.add, r=[('hT', g), ('ps', bst)], w=[('hT', g)])
                    self.copy('act', hTb[g], hT[g], r=[('hT', g)], w=[('hTb', g)])
                y3 = f3(yv, 4)
                z3 = f3(zt, 4)
                for i in range(4):
                    self.stt(y3[:, i, :], xc3[:, i, tk], dcol[:, i:i + 1], py[:, i * 128:(i + 1) * 128], ALU.mult,
                             ALU.add, r=[('xc', i), 'dcol', ('ps', by)], w=['yv'])
                self.tt('dve', y3, y3, z3[:, :, tk], ALU.mult, r=['yv', 'zt'], w=['yv'])
                self.act(sq, yv, AF.Square, r=['yv'], w=['sq'])
                bn, pn = self.bank()
                for g in range(2):
                    for j in range(2):
                        c = 2 * g + j
                        self.mm(pn[:, g * 128:(g + 1) * 128], self.ones_b, sq[:, c * 128:(c + 1) * 128], j == 0, j == 1,
                                r=['ones', 'sq'], w=[('ps', bn)])
                self.act(sd, pn[:, 0:256], AF.Sqrt, r=[('ps', bn)], w=['sd'], bias=EPS, scale=1.0 / 256)
                self.A('dve', lambda e: e.reciprocal(out=rstd, in_=sd), r=['sd'], w=['rstd'])
                yb3 = f3(ybuf, 4)
                for i in range(4):
                    self.stt(yb3[:, i, tk], y3[:, i, :], nw[:, i:i + 1], rstd[:, (i // 2) * 128:(i // 2 + 1) * 128],
                             ALU.mult, ALU.mult, r=['yv', 'nw', 'rstd'], w=['ybuf'])
                yield
            self.dma(self.Y[2, :, :, t0:t0 + TT].rearrange("k p n -> p k n"), f3(ybuf, 4), r=['ybuf'])

    def stage_gla(self, l):
        P = self.prm
        f3 = lambda a, k: a.rearrange("p (k n) -> p k n", k=k)
        w2 = self.alloc(256)
        b2b = self.alloc(256)
        gnw = self.alloc(1)
        self.A('pool', lambda e: e.memset(w2, 0.0), w=['w2'])
        self.dma(w2[0:16, :], P["gla_gate_w2"][l], w=['w2'])
        self.dma(b2b, P["gla_gate_b2"][l].partition_broadcast(128), w=['b2b'], slow=True)
        self.dma(gnw, P["gla_norm"][l].rearrange("(p o) -> p o", o=1), w=['gnw'], slow=True)
        Sf = [self.alloc(128) for _ in range(2)]
        Sb = [self.alloc(128, BF16) for _ in range(2)]
        for kc in range(2):
            self.A('pool', lambda e, kc=kc: e.memset(Sf[kc], 0.0), w=[('Sf', kc)])
            self.A('pool', lambda e, kc=kc: e.memset(Sb[kc], 0.0), w=[('Sb', kc)])
        qt = self.alloc(2 * 512, BF16)
        kt_ = self.alloc(2 * 512, BF16)
        got = self.alloc(4 * 512, BF16)
        glr = self.alloc(512)
        self.A('pool', lambda e: e.memset(glr, 0.0), w=['glr'])
        ybuf = self.alloc(4 * 512, BF16)
        ktm = self.alloc(256, BF16)
        vtm = self.alloc(512, BF16)
        xb = self.alloc(256)
        loga = self.alloc(256)
        bsb = self.alloc(256)
        eb = self.alloc(256)
        enb = self.alloc(256)
        qe = self.alloc(256, BF16)
        qem = [self.alloc(256, BF16) for _ in range(2)]
        kem = [self.alloc(256, BF16) for _ in range(2)]
        kl = self.alloc(256)
        klm = [self.alloc(256, BF16) for _ in range(2)]
        ATs = [self.alloc(256, BF16) for _ in range(2)]
        for c2 in range(2):
            self.A('pool', lambda e, c2=c2: e.memset(ATs[c2], 0.0), w=[('AT', c2)])
        sq = self.alloc(512, BF16)
        sd = self.alloc(512)
        rstd = self.alloc(512)
        yg = self.alloc(512)
        q3, k3, go3, yb3 = f3(qt, 2), f3(kt_, 2), f3(got, 4), f3(ybuf, 4)
        eb3, enb3, qe3 = f3(eb, 2), f3(enb, 2), f3(qe, 2)
        qem3 = [f3(a, 2) for a in qem]
        kem3 = [f3(a, 2) for a in kem]
        for t in range(NT):
            t0 = t * TT
            self.dma(q3, self.Q[:, :, t0:t0 + TT].rearrange("k p n -> p k n"), w=['qt'])
            self.dma(k3, self.Kf[:, :, t0:t0 + TT].rearrange("k p n -> p k n"), w=['kt'])
            self.dma(go3, self.GO[:, :, t0:t0 + TT].rearrange("k p n -> p k n"), w=['got'])
            self.dma(glr[0:16, :], self.GLR[0, 0:16, t0:t0 + TT], w=['glr'])
            for sub in range(4):
                r0 = t0 + sub * 128
                tk = slice(sub * 128, (sub + 1) * 128)
                self.dma(ktm, self.KT[r0:r0 + 128, :], w=['ktm'])
                self.dma(vtm, self.VT[r0:r0 + 128, :], w=['vtm'])
                bx, px = self.bank()
                self.mm(px[:, 0:256], glr[:, tk], w2, True, True, r=['glr', 'w2'], w=[('ps', bx)])
                self.tt('dve', xb, px[:, 0:256], b2b, ALU.add, r=[('ps', bx), 'b2b'], w=['xb'])
                self.act(xb, xb, AF.Exp, r=['xb'], w=['xb'], scale=-1.0)
                self.act(xb, xb, AF.Ln, r=['xb'], w=['xb'], bias=1.0)
                self.ts('dve', loga, xb, -1.0 / 16.0, None, ALU.mult, None, r=['xb'], w=['loga'])
                bb_, pbm = self.bank()
                self.mm(pbm[:, 0:256], self.tri64, loga, True, True, r=['c_tri64', 'loga'], w=[('ps', bb_)])
                self.mm(pbm[:, 256:512], self.blk64, loga, True, True, r=['c_blk64', 'loga'], w=[('ps', bb_)])
                bf_, pbf = self.bank()
                for kc in range(2):
                    self.mm(pbf[:, kc * 128:(kc + 1) * 128], loga[:, kc * 128:(kc + 1) * 128], self.tri64, True, True,
                            r=['loga', 'c_tri64'], w=[('ps', bf_)])
                self.act(eb, pbf[:, 0:256], AF.Exp, r=[('ps', bf_)], w=['eb'])
                self.act(enb, pbf[:, 0:256], AF.Exp, r=[('ps', bf_)], w=['enb'], scale=-1.0)
                self.tt('dve', qe3, q3[:, :, tk], eb3, ALU.mult, r=['qt', 'eb'], w=['qe'])
                for h2 in range(2):
                    for kc in range(2):
                        self.stt(kem3[h2][:, kc, :], k3[:, kc, tk], self.hmask[:, h2:h2 + 1], enb3[:, kc, :], ALU.mult,
                                 ALU.mult, r=['kt', 'enb', 'c_hmask'], w=[('kem', h2)])
                    self.act(qem[h2], qe, AF.Copy, r=['qe', 'c_hmask'], w=[('qem', h2)],
                             scale=self.hmask[:, h2:h2 + 1])
                self.copy('dve', bsb, pbm[:, 0:256], r=[('ps', bb_)], w=['bsb'])
                self.tt('dve', bsb, pbm[:, 256:512], bsb, ALU.subtract, r=[('ps', bb_), 'bsb'], w=['bsb'])
                self.act(bsb, bsb, AF.Exp, r=['bsb'], w=['bsb'])
                self.tt('dve', kl, ktm, bsb, ALU.mult, r=['ktm', 'bsb'], w=['kl'])
                for c2 in range(2):
                    self.act(klm[c2], kl, AF.Copy, r=['kl', 'c_hmask'], w=[('klm', c2)],
                             scale=self.hmask[:, c2:c2 + 1])
                bo, po = self.bank()
                for c2 in range(2):
                    cs = slice(64 * c2, 64 * c2 + 64)
                    ba, pa = self.bank()
                    for hd in range(4):
                        kc, h2 = hd // 2, hd % 2
                        hs = slice(64 * h2, 64 * h2 + 64)
                        self.mm(pa[cs, hd * 64:(hd + 1) * 64], kem3[h2][:, kc, cs], qe3[:, kc, cs], True, True,
                                r=[('kem', h2), 'qe'], w=[('ps', ba)])
                    AT3 = f3(ATs[c2], 4)
                    self.tt('dve', AT3[cs, :, :], f3(pa[cs, 0:256], 4),
                            self.mask64[cs, :].unsqueeze(1).broadcast_to([64, 4, 64]), ALU.mult,
                            r=[('ps', ba), 'c_mask64'], w=[('AT', c2)])
                    for hd in range(4):
                        kc, h2 = hd // 2, hd % 2
                        hs = slice(64 * h2, 64 * h2 + 64)
                        o_ = po[:, hd * 128 + 64 * c2:hd * 128 + 64 * c2 + 64]
                        self.mm(o_, vtm[:, hd * 128:(hd + 1) * 128], AT3[:, hd, :], True, False,
                                r=['vtm', ('AT', c2)], w=[('ps', bo)])
                        self.mm(o_, Sb[kc], qem3[h2][:, kc, cs], False, True, r=[('Sb', kc), ('qem', h2)],
                                w=[('ps', bo)])
                    bs_, pS = self.bank()
                    for kc in range(2):
                        for h2 in range(2):
                            hd = 2 * kc + h2
                            self.mm(pS[64 * h2:64 * h2 + 64, kc * 128:(kc + 1) * 128],
                                    klm[c2][:, hd * 64:(hd + 1) * 64],
                                    vtm[:, hd * 128:(hd + 1) * 128], True, True, r=[('klm', c2), 'vtm'],
                                    w=[('ps', bs_)])
                        self.stt(Sf[kc], Sf[kc], eb3[:, kc, 64 * c2 + 63:64 * c2 + 64], pS[:, kc * 128:(kc + 1) * 128],
                                 ALU.mult, ALU.add, r=[('Sf', kc), 'eb', ('ps', bs_)], w=[('Sf', kc)])
                        self.copy('act', Sb[kc], Sf[kc], r=[('Sf', kc)], w=[('Sb', kc)])
                self.act(sq, po, AF.Square, r=[('ps', bo)], w=['sq'])
                bn, pn = self.bank()
                self.mm(pn, self.ones_b, sq, True, True, r=['ones', 'sq'], w=[('ps', bn)])
                self.act(sd, pn, AF.Sqrt, r=[('ps', bn)], w=['sd'], bias=EPS, scale=1.0 / 128)
                self.A('dve', lambda e: e.reciprocal(out=rstd, in_=sd), r=['sd'], w=['rstd'])
                self.stt(yg, po, gnw[:, 0:1], rstd, ALU.mult, ALU.mult, r=[('ps', bo), 'gnw', 'rstd'], w=['yg'])
                self.tt('dve', yb3[:, :, tk], f3(yg, 4), go3[:, :, tk], ALU.mult, r=['yg', 'got'], w=['ybuf'])
                yield
            self.dma(self.Y[1, :, :, t0:t0 + TT].rearrange("k p n -> p k n"), yb3, r=['ybuf'])

    def s5_pre(self, l):
        P = self.prm
        PI = float(np.pi)
        f3 = lambda a, k: a.rearrange("p (k n) -> p k n", k=k)
        NS = 32
        hp = self.alloc(1)
        rho_c = self.alloc(16)
        th_c = self.alloc(16)
        BbT = self.alloc(16 * 2 * 128, BF16)
        CT = self.alloc(16 * 2 * 128, BF16)
        e5r = self.alloc(16)
        e5i = self.alloc(16)
        t16a = self.alloc(16)
        t16b = self.alloc(16)
        dcol = self.alloc(4)
        wgl = self.alloc(4 * 512, BF16)
        mt = self.mark()
        nat = {n: self.alloc(64) for n in ["lr", "li", "x1", "th", "mag", "c", "s", "cc", "ss", "cs", "ar", "ai",
                                           "den", "nr", "t1", "t2"]}
        stc = self.alloc(1)
        ff = self.alloc(128)
        N = lambda n: nat[n][:NS, :]
        self.dma(N("lr"), P["s5_lam_re"][l], w=['n_lr'])
        self.dma(N("li"), P["s5_lam_im"][l], w=['n_li'])
        self.dma(stc[:NS, :], P["s5_log_step"][l].rearrange("(g o) -> g o", o=1), w=['stc'], slow=True)
        self.act(stc[:NS, :], stc[:NS, :], AF.Exp, r=['stc'], w=['stc'])
        self.ts('dve', N("lr"), N("lr"), -1e-4, None, ALU.min, None, r=['n_lr'], w=['n_lr'])
        self.ts('dve', N("x1"), N("lr"), stc[:NS, 0:1], None, ALU.mult, None, r=['n_lr', 'stc'], w=['n_x1'])
        self.ts('dve', N("th"), N("li"), stc[:NS, 0:1], None, ALU.mult, None, r=['n_li', 'stc'], w=['n_th'])
        self.act(N("mag"), N("x1"), AF.Exp, r=['n_x1'], w=['n_mag'])
        self.A('pool', lambda e: e.memset(hp, PI / 2), w=['hp'])
        self.act(N("s"), N("th"), AF.Sin, r=['n_th'], w=['n_s'], scale=1.0 / 16)
        self.act(N("c"), N("th"), AF.Sin, r=['n_th', 'hp'], w=['n_c'], scale=1.0 / 16, bias=hp[:NS, 0:1])
        for it in range(4):
            self.tt('dve', N("cc"), N("c"), N("c"), ALU.mult, r=['n_c'], w=['n_cc'])
            self.tt('dve', N("ss"), N("s"), N("s"), ALU.mult, r=['n_s'], w=['n_ss'])
            self.tt('dve', N("cs"), N("c"), N("s"), ALU.mult, r=['n_c', 'n_s'], w=['n_cs'])
            self.tt('dve', N("c"), N("cc"), N("ss"), ALU.subtract, r=['n_cc', 'n_ss'], w=['n_c'])
            self.ts('dve', N("s"), N("cs"), 2.0, None, ALU.mult, None, r=['n_cs'], w=['n_s'])
        self.tt('dve', N("ar"), N("mag"), N("c"), ALU.mult, r=['n_mag', 'n_c'], w=['n_ar'])
        self.tt('dve', N("ai"), N("mag"), N("s"), ALU.mult, r=['n_mag', 'n_s'], w=['n_ai'])
        self.tt('dve', N("t1"), N("lr"), N("lr"), ALU.mult, r=['n_lr'], w=['n_t1'])
        self.tt('dve', N("t2"), N("li"), N("li"), ALU.mult, r=['n_li'], w=['n_t2'])
        self.tt('dve', N("den"), N("t1"), N("t2"), ALU.add, r=['n_t1', 'n_t2'], w=['n_den'])
        self.A('dve', lambda e: e.reciprocal(out=N("den"), in_=N("den")), r=['n_den'], w=['n_den'])
        self.ts('dve', N("nr"), N("ar"), -1.0, None, ALU.add, None, r=['n_ar'], w=['n_nr'])
        self.tt('dve', N("t1"), N("nr"), N("lr"), ALU.mult, r=['n_nr', 'n_lr'], w=['n_t1'])
        self.tt('dve', N("t2"), N("ai"), N("li"), ALU.mult, r=['n_ai', 'n_li'], w=['n_t2'])
        self.tt('dve', N("t1"), N("t1"), N("t2"), ALU.add, r=['n_t1', 'n_t2'], w=['n_t1'])
        self.tt('dve', ff[:NS, 0:64], N("t1"), N("den"), ALU.mult, r=['n_t1', 'n_den'], w=['ff'])
        self.tt('dve', N("t1"), N("ai"), N("lr"), ALU.mult, r=['n_ai', 'n_lr'], w=['n_t1'])
        self.tt('dve', N("t2"), N("nr"), N("li"), ALU.mult, r=['n_nr', 'n_li'], w=['n_t2'])
        self.tt('dve', N("t1"), N("t1"), N("t2"), ALU.subtract, r=['n_t1', 'n_t2'], w=['n_t1'])
        self.tt('dve', ff[:NS, 64:128], N("t1"), N("den"), ALU.mult, r=['n_t1', 'n_den'], w=['ff'])
        self.dma(self.S5T[0], N("mag"), r=['n_mag'], w=['S5T0'])
        self.dma(self.S5T[1], N("th"), r=['n_th'], w=['S5T1'])
        for t2 in range(2):
            self.dma(rho_c[t2 * 64:(t2 + 1) * 64, :], self.S5T[0, t2::2, :].rearrange("j p -> p j"), r=['S5T0'],
                     w=['rho_c'], slow=True)
            self.dma(th_c[t2 * 64:(t2 + 1) * 64, :], self.S5T[1, t2::2, :].rearrange("j p -> p j"), r=['S5T1'],
                     w=['th_c'], slow=True)
        selb = self.alloc(512)
        self.dma(selb[:NS, :], self.cst["c_selb"], w=['selb'])
        bnat = [self.alloc(512) for _ in range(2)]
        self.dma(f3(bnat[0][:64, :], 32), P["s5_b_re"][l].rearrange("g p h -> p g h"), w=[('bnat', 0)])
        self.dma(f3(bnat[1][:64, :], 32), P["s5_b_im"][l].rearrange("g p h -> p g h"), w=[('bnat', 1)])
        BbT4 = BbT.rearrange("p (j c n) -> p j c n", j=16, c=2)
        fq = self.alloc(128)
        bq = [self.alloc(64) for _ in range(2)]
        bbq = [self.alloc(64) for _ in range(2)]
        tq = [self.alloc(64) for _ in range(2)]
        for q in range(4):
            b, pb = self.bank()
            self.mm(pb[:, 0:128], selb[:NS, q * 128:(q + 1) * 128], ff[:NS, :], True, True, r=['selb', 'ff'],
                    w=[('ps', b)])
            self.copy('act', fq, pb[:, 0:128], r=[('ps', b)], w=['fq'])
            b2, pb2 = self.bank()
            for c in range(2):
                self.A('pe', lambda e, pb2=pb2, c=c, q=q: e.transpose(out=pb2[:, c * 64:(c + 1) * 64],
                                                                      in_=bnat[c][:64, q * 128:(q + 1) * 128],
                                                                      identity=self.ident[:64, :64]),
                       r=[('bnat', c), 'ident'], w=[('ps', b2)])
                self.copy('act', bq[c], pb2[:, c * 64:(c + 1) * 64], r=[('ps', b2)], w=[('bq', c)])
            self.tt('dve', tq[0], fq[:, 0:64], bq[0], ALU.mult, r=['fq', ('bq', 0)], w=[('tq', 0)])
            self.tt('dve', tq[1], fq[:, 64:128], bq[1], ALU.mult, r=['fq', ('bq', 1)], w=[('tq', 1)])
            self.tt('dve', bbq[0], tq[0], tq[1], ALU.subtract, r=[('tq', 0), ('tq', 1)], w=[('bbq', 0)])
            self.tt('dve', tq[0], fq[:, 0:64], bq[1], ALU.mult, r=['fq', ('bq', 1)], w=[('tq', 0)])
            self.tt('dve', tq[1], fq[:, 64:128], bq[0], ALU.mult, r=['fq', ('bq', 0)], w=[('tq', 1)])
            self.tt('dve', bbq[1], tq[0], tq[1], ALU.add, r=[('tq', 0), ('tq', 1)], w=[('bbq', 1)])
            for gl in range(8):
                g = q * 8 + gl
                j, t2 = g // 2, g % 2
                for c in range(2):
                    self.ts('dve', BbT4[:, j, c, t2 * 64:(t2 + 1) * 64], bbq[c], self.gmask[:, gl:gl + 1], None,
                            ALU.mult, None, r=[('bbq', c), 'c_gmask'], w=['BbT'])
        CT4 = CT.rearrange("p (j c n) -> p j c n", j=16, c=2)
        self.A('pool', lambda e: e.memset(CT, 0.0), w=['CT'])
        cdup = self.alloc(128)
        cT = self.alloc(128)
        for c, nm in enumerate(["s5_c_re", "s5_c_im"]):
            src = P[nm][l].rearrange("g h p -> (g h) p")
            for q in range(4):
                self.dma(cdup[:, 0:64], src[q * 128:(q + 1) * 128, :], w=['cdup'])
                self.dma(cdup[:, 64:128], src[q * 128:(q + 1) * 128, :], w=['cdup'])
                b, pb = self.bank()
                self.A('pe', lambda e, pb=pb: e.transpose(out=pb[:, 0:128], in_=cdup, identity=self.ident),
                       r=['cdup', 'ident'], w=[('ps', b)])
                self.act(cT, pb[:, 0:128], AF.Copy, r=[('ps', b)], w=['cT'], scale=(1.0 if c == 0 else -1.0))
                for gl in range(8):
                    g = q * 8 + gl
                    j, t2 = g // 2, g % 2
                    rs = slice(t2 * 64, (t2 + 1) * 64)
                    self.copy('dve' if gl % 2 else 'pool', CT4[rs, j, c, gl * 16:(gl + 1) * 16],
                              cT[rs, gl * 16:(gl + 1) * 16], r=['cT'], w=['CT'])
        cs_t = self.XNf[:, 0:8192]
        sn_t = self.XNf[:, 8192:16384]
        cs3, sn3 = f3(cs_t, 16), f3(sn_t, 16)
        iota = self.alloc(512)
        self.dma(iota, self.cst["c_iota"], w=['iota'])
        ph = self.alloc(512)
        kq = self.alloc(512)
        ki = self.alloc(512).bitcast(mybir.dt.int32)
        ta = self.alloc(512)
        tb = self.alloc(512)
        for j in range(16):
            self.ts('dve', ph, iota, th_c[:, j:j + 1], None, ALU.mult, None, r=['iota', 'th_c'], w=['ph'])
            self.ts('dve', kq, ph, 1.0 / (2 * PI), None, ALU.mult, None, r=['ph'], w=['kq'])
            self.copy('dve', ki, kq, r=['kq'], w=['ki'])
            self.copy('dve', kq, ki, r=['ki'], w=['kq'])
            self.stt(ph, kq, -2 * PI, ph, ALU.mult, ALU.add, r=['kq', 'ph'], w=['ph'])
            self.act(sn3[:, j, :], ph, AF.Sin, r=['ph'], w=[('sn', j)], scale=0.25)
            self.act(cs3[:, j, :], ph, AF.Sin, r=['ph', 'hp'], w=[('cs', j)], scale=0.25, bias=hp[:, 0:1])
            for it in range(2):
                self.tt('dve', ta, cs3[:, j, :], cs3[:, j, :], ALU.mult, r=[('cs', j)], w=['ta'])
                self.tt('pool', tb, sn3[:, j, :], sn3[:, j, :], ALU.mult, r=[('sn', j)], w=['tb'])
                self.tt('pool', kq, cs3[:, j, :], sn3[:, j, :], ALU.mult, r=[('cs', j), ('sn', j)], w=['kq'])
                self.tt('dve', cs3[:, j, :], ta, tb, ALU.subtract, r=['ta', 'tb'], w=[('cs', j)])
                self.ts('pool', sn3[:, j, :], kq, 2.0, None, ALU.mult, None, r=['kq'], w=[('sn', j)])
        allk = [('cs', j) for j in range(16)] + [('sn', j) for j in range(16)]
        self.tt('dve', t16a, cs3[:, :, 511], cs3[:, :, 1], ALU.mult, r=allk, w=['t16a'])
        self.tt('dve', t16b, sn3[:, :, 511], sn3[:, :, 1], ALU.mult, r=allk, w=['t16b'])
        self.tt('dve', e5r, t16a, t16b, ALU.subtract, r=['t16a', 't16b'], w=['e5r'])
        self.tt('dve', t16a, cs3[:, :, 511], sn3[:, :, 1], ALU.mult, r=allk, w=['t16a'])
        self.tt('dve', t16b, sn3[:, :, 511], cs3[:, :, 1], ALU.mult, r=allk, w=['t16b'])
        self.tt('dve', e5i, t16a, t16b, ALU.add, r=['t16a', 't16b'], w=['e5i'])
        self.dma(dcol, P["s5_d"][l].rearrange("(c p) -> p c", p=128), w=['dcol'], slow=True)
        stg = self.alloc(4 * 512)
        self.dma(f3(stg, 4), P["s5_glu"][l].rearrange("(k p) n -> p k n", p=128), w=['stg'])
        self.copy('pool', wgl, stg, r=['stg'], w=['wgl'])
        self.release(mt)
        return dict(BbT=BbT, CT=CT, rho_c=rho_c, e5r=e5r, e5i=e5i, t16a=t16a, t16b=t16b, dcol=dcol, wgl=wgl)

    def s5_main(self, l, C_):
        f3 = lambda a, k: a.rearrange("p (k n) -> p k n", k=k)
        BbT, CT, rho_c, e5r, e5i, t16a, t16b, dcol, wgl = (C_[k] for k in
                                                             ["BbT", "CT", "rho_c", "e5r", "e5i", "t16a", "t16b",
                                                              "dcol", "wgl"])
        BbT4 = BbT.rearrange("p (j c n) -> p j c n", j=16, c=2)
        CT4 = CT.rearrange("p (j c n) -> p j c n", j=16, c=2)
        cs3, sn3 = f3(self.XNf[:, 0:8192], 16), f3(self.XNf[:, 8192:16384], 16)
        wgl3 = f3(wgl, 4)
        ut = self.alloc(4 * 512, BF16)
        ut3 = f3(ut, 4)
        p1, p2, p3, p4 = [self.alloc(512) for _ in range(4)]
        wr, wi = self.alloc(512), self.alloc(512)
        gr, gi = self.alloc(512), self.alloc(512)
        Hr, Hi = self.alloc(512, BF16), self.alloc(512, BF16)
        glr_, gli_ = self.alloc(16), self.alloc(16)
        inr, ini = self.alloc(16), self.alloc(16)
        yq = self.alloc(512)
        x2 = self.alloc(512)
        gq = self.alloc(4 * 512, BF16)
        gq3 = f3(gq, 4)
        sgl = self.alloc(512)
        ybuf = self.alloc(4 * 512, BF16)
        yb3 = f3(ybuf, 4)
        self.A('pool', lambda e: e.memset(inr, 0.0), w=['inr'])
        self.A('pool', lambda e: e.memset(ini, 0.0), w=['ini'])
        xbk = 0
        for t in range(NT):
            t0 = t * TT
            self.dma(ut3, self.US5[:, :, t0:t0 + TT].rearrange("k p n -> p k n"), w=['ut'])
            if t > 0:
                self.tt('dve', t16a, e5r, glr_, ALU.mult, r=['e5r', 'glr_'], w=['t16a'])
                self.tt('dve', t16b, e5i, gli_, ALU.mult, r=['e5i', 'gli_'], w=['t16b'])
                self.tt('dve', inr, t16a, t16b, ALU.subtract, r=['t16a', 't16b'], w=['inr'])
                self.tt('dve', t16a, e5r, gli_, ALU.mult, r=['e5r', 'gli_'], w=['t16a'])
                self.tt('dve', t16b, e5i, glr_, ALU.mult, r=['e5i', 'glr_'], w=['t16b'])
                self.tt('dve', ini, t16a, t16b, ALU.add, r=['t16a', 't16b'], w=['ini'])
            for j in range(16):
                q = j // 4
                bA = xbk % 6
                bB = (xbk + 1) % 6
                xbk += 2
                pA = self.ps[:, bA * 512:(bA + 1) * 512]
                pB = self.ps[:, bB * 512:(bB + 1) * 512]
                self.mm(pA, BbT4[:, j, 0, :], ut3[:, q, :], True, True, r=['BbT', 'ut'], w=[('ps', bA)])
                self.mm(pB, BbT4[:, j, 1, :], ut3[:, q, :], True, True, r=['BbT', 'ut'], w=[('ps', bB)])
                cj, sj = cs3[:, j, :], sn3[:, j, :]
                tabk = [('cs', j), ('sn', j)]
                self.tt('dve', p1, pA, cj, ALU.mult, r=[('ps', bA)] + tabk, w=['p1'])
                self.tt('dve', p2, pB, sj, ALU.mult, r=[('ps', bB)] + tabk, w=['p2'])
                self.tt('dve', p3, pB, cj, ALU.mult, r=[('ps', bB)] + tabk, w=['p3'])
                self.tt('dve', p4, pA, sj, ALU.mult, r=[('ps', bA)] + tabk, w=['p4'])
                self.tt('pool', wr, p1, p2, ALU.add, r=['p1', 'p2'], w=['wr'])
                self.tt('pool', wi, p3, p4, ALU.subtract, r=['p3', 'p4'], w=['wi'])
                rb = rho_c[:, j:j + 1].broadcast_to([128, 512])
                self.A('dve', lambda e, rb=rb, j=j: e.tensor_tensor_scan(out=gr, data0=rb, data1=wr,
                                                                          initial=inr[:, j:j + 1], op0=ALU.mult,
                                                                          op1=ALU.add),
                       r=['wr', 'rho_c', 'inr'], w=['gr'])
                self.A('dve', lambda e, rb=rb, j=j: e.tensor_tensor_scan(out=gi, data0=rb, data1=wi,
                                                                          initial=ini[:, j:j + 1], op0=ALU.mult,
                                                                          op1=ALU.add),
                       r=['wi', 'rho_c', 'ini'], w=['gi'])
                self.copy('act', glr_[:, j:j + 1], gr[:, 511:512], r=['gr'], w=['glr_'])
                self.copy('act', gli_[:, j:j + 1], gi[:, 511:512], r=['gi'], w=['gli_'])
                self.tt('pool', p1, gr, cj, ALU.mult, r=['gr'] + tabk, w=['p1'])
                self.tt('pool', p2, gi, sj, ALU.mult, r=['gi'] + tabk, w=['p2'])
                self.tt('pool', p3, gi, cj, ALU.mult, r=['gi'] + tabk, w=['p3'])
                self.tt('pool', p4, gr, sj, ALU.mult, r=['gr'] + tabk, w=['p4'])
                self.tt('dve', Hr, p1, p2, ALU.subtract, r=['p1', 'p2'], w=['Hr'])
                self.tt('dve', Hi, p3, p4, ALU.add, r=['p3', 'p4'], w=['Hi'])
                bY = 6 + q % 2
                pY = self.ps[:, bY * 512:(bY + 1) * 512]
                first = (j % 4 == 0)
                last = (j % 4 == 3)
                self.mm(pY, CT4[:, j, 0, :], Hr, first, False, r=['CT', 'Hr'], w=[('ps', bY)])
                self.mm(pY, CT4[:, j, 1, :], Hi, False, last, r=['CT', 'Hi'], w=[('ps', bY)])
                if last:
                    self.stt(yq, ut3[:, q, :], dcol[:, q:q + 1], pY, ALU.mult, ALU.add, r=['ut', 'dcol', ('ps', bY)],
                             w=['yq'])
                    self.tt('dve', x2, yq, yq, ALU.mult, r=['yq'], w=['x2'])
                    self.ts('dve', x2, x2, 0.044715, 1.0, ALU.mult, ALU.add, r=['x2'], w=['x2'])
                    self.tt('dve', x2, x2, yq, ALU.mult, r=['x2', 'yq'], w=['x2'])
                    self.act(x2, x2, AF.Tanh, r=['x2'], w=['x2'], scale=0.7978845608028654)
                    self.stt(x2, x2, 1.0, yq, ALU.add, ALU.mult, r=['x2', 'yq'], w=['x2'])
                    self.ts('dve', gq3[:, q, :], x2, 0.5, None, ALU.mult, None, r=['x2'], w=[('gq', q)])
                    yield
            for mo in range(4):
                b, pb = self.bank()
                for q in range(4):
                    self.mm(pb, wgl3[:, q, mo * 128:(mo + 1) * 128], gq3[:, q, :], q == 0, q == 3,
                            r=['wgl', ('gq', q)], w=[('ps', b)])
                self.act(sgl, pb, AF.Sigmoid, r=[('ps', b)], w=['sgl'])
                self.tt('dve', yb3[:, mo, :], gq3[:, mo, :], sgl, ALU.mult, r=[('gq', mo), 'sgl'], w=['ybuf'])
            self.dma(self.Y[0, :, :, t0:t0 + TT].rearrange("k p n -> p k n"), yb3, r=['ybuf'])

    def stage_merge(self, l):
        m = self.mark()
        P = self.prm
        f3 = lambda a, k: a.rearrange("p (k n) -> p k n", k=k)
        stg = self.alloc(8 * 512)
        wbr = [self.alloc(4 * 1024, BF16) for _ in range(3)]
        wo = self.alloc(8 * 1024, BF16)
        for bi, nm in enumerate(["w_br_s5", "w_br_gla", "w_br_ssd"]):
            st = f3(stg[:, :4096], 4)
            self.dma(st, P[nm][l].rearrange("(k p) n -> p k n", p=128), w=['stg'])
            self.copy('pool', f3(wbr[bi], 4), st, r=['stg'], w=[('wbr', bi)])
        wo3 = f3(wo, 8)
        for half in range(2):
            st = f3(stg, 8)
            self.dma(st, P["w_out"][l][:, half * 512:(half + 1) * 512].rearrange("(k p) n -> p k n", p=128), w=['stg'])
            self.copy('pool', wo3[:, :, half * 512:(half + 1) * 512], st, r=['stg'], w=['wo'])
        yt = [self.alloc(4 * 512, BF16) for _ in range(3)]
        sg = self.alloc(24 * 512, BF16)
        hb = self.alloc(8 * 512)
        mg = self.alloc(512)
        tmpm = self.alloc(512)
        mgb = self.alloc(8 * 512, BF16)
        tmp = (self.alloc(8 * 512, BF16), self.alloc(512), self.alloc(512))
        wcol = self.alloc(8)
        kw = self.load_cols(wcol, P["ffn2_norm"][l], 8)
        sg3 = f3(sg, 24)
        for t in range(NT):
            t0 = t * TT
            for bi in range(3):
                self.dma(f3(yt[bi], 4), self.Y[bi, :, :, t0:t0 + TT].rearrange("k p n -> p k n"), w=[('yt', bi)])
            self.dma(sg3, self.SIG[:, :, t0:t0 + TT].rearrange("k p n -> p k n"), w=['sg'])
            hkeys = [('h', 0, k) for k in range(8)]
            self.dma(f3(hb, 8), self.H[:, :, t0:t0 + TT].rearrange("k p n -> p k n"), w=hkeys)
            for mo in range(8):
                for bi in range(3):
                    b, pb = self.bank()
                    y3 = f3(yt[bi], 4)
                    w3 = f3(wbr[bi], 4)
                    for kc in range(4):
                        self.mm(pb, w3[:, kc, mo * 128:(mo + 1) * 128], y3[:, kc, :], kc == 0, kc == 3,
                                r=[('wbr', bi), ('yt', bi)], w=[('ps', b)])
                    if bi == 0:
                        self.tt('dve', mg, pb, sg3[:, bi * 8 + mo, :], ALU.mult, r=[('ps', b), 'sg'], w=['mg'])
                    else:
                        self.tt('dve', tmpm, pb, sg3[:, bi * 8 + mo, :], ALU.mult, r=[('ps', b), 'sg'], w=['tmpm'])
                        self.tt('pool', mg, mg, tmpm, ALU.add, r=['mg', 'tmpm'], w=['mg'])
                self.copy('act', mgb[:, mo * 512:(mo + 1) * 512], mg, r=['mg'], w=[('mgb', mo)])
            for mo2 in range(8):
                b, pb = self.bank()
                for mo in range(8):
                    self.mm(pb, wo3[:, mo, mo2 * 128:(mo2 + 1) * 128], mgb[:, mo * 512:(mo + 1) * 512], mo == 0,
                            mo == 7, r=['wo', ('mgb', mo)], w=[('ps', b)])
                hk = hb[:, mo2 * 512:(mo2 + 1) * 512]
                self.tt('dve', hk, hk, pb, ALU.add, r=[hkeys[mo2], ('ps', b)], w=[hkeys[mo2]])
            self.dma(self.H[:, :, t0:t0 + TT].rearrange("k p n -> p k n"), f3(hb, 8), r=hkeys)
            self.rmsnorm_tile(hb, hkeys, wcol, kw, lambda k, t=t: self.xn_ap(k, t),
                              [('xn', k, t) for k in range(8)], tmp)
        self.release(m)

    def renorm(self, vec):
        m = self.mark()
        hb = [self.alloc(8 * 512) for _ in range(2)]
        tmp = (self.alloc(8 * 512, BF16), self.alloc(512), self.alloc(512))
        wcol = self.alloc(8)
        kw = self.load_cols(wcol, vec, 8)
        for t in range(NT):
            h = hb[t % 2]
            hkeys = [('h', t % 2, k) for k in range(8)]
            self.dma(h.rearrange("p (k n) -> p k n", k=8),
                     self.H[:, :, t * TT:(t + 1) * TT].rearrange("k p n -> p k n"), w=hkeys)
            self.rmsnorm_tile(h, hkeys, wcol, kw, lambda k, t=t: self.xn_ap(k, t),
                              [('xn', k, t) for k in range(8)], tmp)
        self.release(m)

    def stage_ple(self, l):
        m = self.mark()
        Wpg, Wpp = self.prm["ple_gate"][l], self.prm["ple_proj"][l]
        stg = [self.alloc(8 * 512)] * 2
        wg = self.alloc(8 * 1024, BF16)
        wp = self.alloc(2 * 1024, BF16)
        wg3 = wg.rearrange("p (k n) -> p k n", k=8)
        wp3 = wp.rearrange("p (k n) -> p k n", k=2)
        for half in range(2):
            st = stg[half][:, :8 * 512].rearrange("p (k n) -> p k n", k=8)
            self.dma(st, Wpg[:, half * 512:(half + 1) * 512].rearrange("(k p) n -> p k n", p=128),
                     w=[('stg', 0)])
            self.copy('pool', wg3[:, :, half * 512:(half + 1) * 512], st, r=[('stg', 0)], w=['wg'])
        st = stg[0][:, :2048].rearrange("p (k n) -> p k n", k=2)
        self.dma(st, Wpp.rearrange("(k p) n -> p k n", p=128), w=[('stg', 0)])
        self.copy('pool', wp3, st, r=[('stg', 0)], w=['wp'])
        pt = [[self.alloc(256) for _ in range(4)] for _ in range(2)]
        pf = [self.alloc(2 * 512, BF16) for _ in range(2)]
        hb = [self.alloc(8 * 512) for _ in range(2)]
        sg = [self.alloc(512) for _ in range(2)]
        tmp = (self.alloc(8 * 512, BF16), self.alloc(512), self.alloc(512))
        wcol = self.alloc(8)
        last = (l == DEPTH - 1)
        nxt = self.prm["final_norm"] if last else self.prm["ffn1_norm"][l + 1]
        kw = self.load_cols(wcol, nxt, 8)
        if last:
            yb = [self.alloc(8 * 512)] * 2
            ot = [self.alloc(1024) for _ in range(2)]
        it = 0
        for t in range(NT):
            h = hb[t % 2]
            hkeys = [('h', t % 2, k) for k in range(8)]
            self.dma(h.rearrange("p (k n) -> p k n", k=8),
                     self.H[:, :, t * TT:(t + 1) * TT].rearrange("k p n -> p k n"), w=hkeys)
            pb_ = pt[t % 2]
            for sub in range(4):
                r0 = t * TT + sub * 128
                self.dma(pb_[sub], self.p[l, r0:r0 + 128, :], w=[('pt', t % 2, sub)])
            pfb = pf[t % 2]
            for kc in range(2):
                b, pb = self.bank()
                for sub in range(4):
                    self.A('pe', lambda e, pb=pb, sub=sub, kc=kc, pb_=pb_: e.transpose(
                        out=pb[:, sub * 128:(sub + 1) * 128], in_=pb_[sub][:, kc * 128:(kc + 1) * 128],
                        identity=self.ident), r=[('pt', t % 2, sub), 'ident'], w=[('ps', b)])
                self.copy('act', pfb[:, kc * 512:(kc + 1) * 512], pb, r=[('ps', b)], w=[('pf', t % 2, kc)])
            for mo in range(8):
                bg, pg = self.bank()
                for k in range(8):
                    self.mm(pg, wg3[:, k, mo * 128:(mo + 1) * 128], self.xn_ap(k, t), k == 0, k == 7,
                            r=['wg', ('xn', k, t)], w=[('ps', bg)])
                bp, pp = self.bank()
                for kc in range(2):
                    self.mm(pp, wp3[:, kc, mo * 128:(mo + 1) * 128], pfb[:, kc * 512:(kc + 1) * 512],
                            kc == 0, kc == 1, r=['wp', ('pf', t % 2, kc)], w=[('ps', bp)])
                s_ = sg[it % 2]
                self.act(s_, pg, AF.Sigmoid, r=[('ps', bg)], w=[('sg', it % 2)])
                self.tt('dve', s_, s_, pp, ALU.mult, r=[('sg', it % 2), ('ps', bp)], w=[('sg', it % 2)])
                hk = h[:, mo * 512:(mo + 1) * 512]
                self.tt('pool', hk, hk, s_, ALU.add, r=[('sg', it % 2), hkeys[mo]], w=[hkeys[mo]])
                it += 1
            if not last:
                self.dma(self.H[:, :, t * TT:(t + 1) * TT].rearrange("k p n -> p k n"),
                         h.rearrange("p (k n) -> p k n", k=8), r=hkeys)
                self.rmsnorm_tile(h, hkeys, wcol, kw, lambda k, t=t: self.xn_ap(k, t),
                                  [('xn', k, t) for k in range(8)], tmp)
            else:
                y = yb[t % 2]
                ykeys = [('y', 0, k) for k in range(8)]
                self.rmsnorm_tile(h, hkeys, wcol, kw, lambda k, y=y: y[:, k * 512:(k + 1) * 512], ykeys, tmp)
                for sub in range(4):
                    ob = ot[sub % 2]
                    for half in range(2):
                        b, pb = self.bank()
                        for kk in range(4):
                            k = half * 4 + kk
                            self.A('pe', lambda e, pb=pb, kk=kk, k=k, sub=sub, y=y: e.transpose(
                                out=pb[:, kk * 128:(kk + 1) * 128],
                                in_=y[:, k * 512 + sub * 128:k * 512 + (sub + 1) * 128], identity=self.ident),
                                r=[ykeys[k], 'ident'], w=[('ps', b)])
                        self.copy('act' if half == 0 else 'dve', ob[:, half * 512:(half + 1) * 512], pb,
                                  r=[('ps', b)], w=[('ot', sub % 2, half)])
                    r0 = t * TT + sub * 128
                    self.dma(self.out[r0:r0 + 128, :], ob, r=[('ot', sub % 2, 0), ('ot', sub % 2, 1)])
        self.release(m)

    def stage_final(self):
        pass


_NC_CACHE = {}


def kernel(**inputs):
    if "nc" not in _NC_CACHE:
        nc = bass.Bass("TRN2", target_bir_lowering=False)
        kb = KB(nc)
        kb.build()
        _NC_CACHE["nc"] = nc
    nc = _NC_CACHE["nc"]
    consts = host_consts()
    x = np.ascontiguousarray(inputs["x"], dtype=np.float32)
    p = np.ascontiguousarray(inputs["p"], dtype=np.float32)
    in_maps = []
    for c in range(8):
        m = {"x": x[c], "p": np.ascontiguousarray(p[:, c])}
        for n in PARAM_NAMES:
            m[n] = np.ascontiguousarray(inputs[n], dtype=np.float32)
        m.update(consts)
        in_maps.append(m)
    res = run_bass_kernel_spmd(nc, in_maps, core_ids=list(range(8)))
    return np.stack([np.asarray(res.results[c]["out"]) for c in range(8)], axis=0).astype(np.float32)
```

```python
import contextlib
import numpy as np
import concourse.bass as bass
import concourse.mybir as mybir
from concourse.alu_op_type import AluOpType as ALU
from concourse.bass_utils import run_bass_kernel_spmd

F32 = mybir.dt.float32
BF16 = mybir.dt.bfloat16
AF = mybir.ActivationFunctionType

S = 4096
D = 1024
DFF = 2752
DEPTH = 2
TT = 512
NT = S // TT
IN_TOTAL = 6680
EPS = 1e-6

STREAMS = ['pe', 'act', 'dve', 'pool', 'sp']
NCH = 8
SEM_ROLL = 12000


class Sched:
    def __init__(self):
        self.ops = []
        self.ns = None

    @staticmethod
    def _nk(k, ns):
        if isinstance(k, tuple) and k[0] == 'ps':
            return k
        if isinstance(k, str) and (k.startswith('c_') or k in ('ident', 'ones')):
            return k
        return (ns, k)

    def add(self, eng, fn, reads=(), writes=(), dma=False):
        if self.ns is not None:
            reads = tuple(self._nk(k, self.ns) for k in reads)
            writes = tuple(self._nk(k, self.ns) for k in writes)
        self.ops.append(dict(eng=eng, fn=fn, reads=tuple(reads), writes=tuple(writes),
                             dma=dma, barrier=False))

    def barrier(self):
        self.ops.append(dict(barrier=True))

    def analyze(self):
        last_w = {}
        readers = {}
        last_on_stream = {}
        last_on_chan = {}
        pending = {s: set() for s in STREAMS}
        ch_rr = 0
        for i, op in enumerate(self.ops):
            if op['barrier']:
                allp = set(last_on_stream.values()) | set(last_on_chan.values())
                for s in STREAMS:
                    pending[s] |= allp
                last_w = {}
                readers = {}
                continue
            deps = {}
            eng = op['eng']
            for r in op['reads']:
                j = last_w.get(r)
                if j is not None:
                    deps[j] = 'RAW'
            for w in op['writes']:
                j = last_w.get(w)
                if j is not None and j not in deps:
                    deps[j] = 'WAW'
                for j in readers.get(w, ()):
                    if j not in deps:
                        deps[j] = 'WAR'
            for j in pending[eng]:
                if j not in deps:
                    deps[j] = 'BAR'
            pending[eng] = set()
            if op['dma']:
                op['chan'] = ch_rr
                ch_rr = (ch_rr + 1) % NCH
                j = last_on_chan.get(op['chan'])
                if j is not None:
                    deps[j] = 'BAR'
                last_on_chan[op['chan']] = i
            else:
                last_on_stream[eng] = i
            fdeps = []
            for j, kind in deps.items():
                pj = self.ops[j]
                if (not pj['dma']) and (not op['dma']) and pj['eng'] == eng:
                    if eng == 'pe' or kind != 'RAW':
                        continue
                fdeps.append(j)
            op['deps'] = fdeps
            for j in fdeps:
                self.ops[j]['needs_inc'] = True
            for r in op['reads']:
                readers.setdefault(r, []).append(i)
            for w in op['writes']:
                last_w[w] = i
                readers[w] = []
        self.last_on_chan = last_on_chan

    def emit(self, nc):
        self.analyze()
        with contextlib.ExitStack() as es:
            def newsem(name):
                return es.enter_context(nc.semaphore(name))

            cur = {s: [newsem(f"s_{s}_0"), 0, 0] for s in ['pe', 'act', 'dve', 'pool']}
            chs = [[newsem(f"s_ch{c}_0"), 0, 0] for c in range(NCH)]
            for op in self.ops:
                if op['barrier']:
                    continue
                if op['dma']:
                    st = chs[op['chan']]
                    nm = f"s_ch{op['chan']}"
                    inc = 16
                elif op.get('needs_inc'):
                    st = cur[op['eng']]
                    nm = f"s_{op['eng']}"
                    inc = 1
                else:
                    continue
                if st[1] + inc > SEM_ROLL:
                    st[2] += 1
                    st[0] = newsem(f"{nm}_{st[2]}")
                    st[1] = 0
                st[1] += inc
                op['sem'], op['val'], op['inc'] = st[0], st[1], inc
            lists = {s: [] for s in STREAMS}
            waited = {s: {} for s in STREAMS}
            for op in self.ops:
                if op['barrier']:
                    continue
                s = op['eng']
                for j in op['deps']:
                    pj = self.ops[j]
                    key = id(pj['sem'])
                    if waited[s].get(key, 0) >= pj['val']:
                        continue
                    waited[s][key] = pj['val']
                    lists[s].append(('wait', pj['sem'], pj['val']))
                lists[s].append(('op', op))
            for c, j in self.last_on_chan.items():
                pj = self.ops[j]
                lists['sp'].append(('wait', pj['sem'], pj['val']))

            def run(e, items):
                for it in items:
                    if it[0] == 'wait':
                        e.wait_ge(it[1], it[2])
                    else:
                        op = it[1]
                        ins = op['fn'](e)
                        if 'sem' in op:
                            ins.then_inc(op['sem'], op['inc'])

            with nc.Block() as block:
                @block.tensor
                def _(e):
                    run(e, lists['pe'])

                @block.scalar
                def _(e):
                    run(e, lists['act'])

                @block.vector
                def _(e):
                    run(e, lists['dve'])

                @block.gpsimd
                def _(e):
                    run(e, lists['pool'])

                @block.sync
                def _(e):
                    run(e, lists['sp'])


PARAM_NAMES = ["ffn1_norm", "ffn1_gate", "ffn1_up", "ffn1_down", "mix_norm", "w_in",
               "s5_lam_re", "s5_lam_im", "s5_log_step", "s5_b_re", "s5_b_im", "s5_c_re", "s5_c_im",
               "s5_d", "s5_glu", "gla_gate_w2", "gla_gate_b2", "gla_norm", "ssd_conv_w", "ssd_conv_b",
               "ssd_dt_bias", "ssd_a_log", "ssd_d", "ssd_norm", "w_br_s5", "w_br_gla", "w_br_ssd",
               "w_out", "ffn2_norm", "ffn2_gate", "ffn2_up", "ffn2_down", "ple_norm", "ple_gate",
               "ple_proj", "final_norm"]
PARAM_SHAPES = {
    "ffn1_norm": (2, 1024), "ffn1_gate": (2, 1024, 2752), "ffn1_up": (2, 1024, 2752),
    "ffn1_down": (2, 2752, 1024), "mix_norm": (2, 1024), "w_in": (2, 1024, 6680),
    "s5_lam_re": (2, 32, 64), "s5_lam_im": (2, 32, 64), "s5_log_step": (2, 32),
    "s5_b_re": (2, 32, 64, 16), "s5_b_im": (2, 32, 64, 16), "s5_c_re": (2, 32, 16, 64),
    "s5_c_im": (2, 32, 16, 64), "s5_d": (2, 512), "s5_glu": (2, 512, 512),
    "gla_gate_w2": (2, 16, 256), "gla_gate_b2": (2, 256), "gla_norm": (2, 128),
    "ssd_conv_w": (2, 4, 1024), "ssd_conv_b": (2, 1024), "ssd_dt_bias": (2, 8), "ssd_a_log": (2, 8),
    "ssd_d": (2, 8), "ssd_norm": (2, 512), "w_br_s5": (2, 512, 1024), "w_br_gla": (2, 512, 1024),
    "w_br_ssd": (2, 512, 1024), "w_out": (2, 1024, 1024), "ffn2_norm": (2, 1024),
    "ffn2_gate": (2, 1024, 2752), "ffn2_up": (2, 1024, 2752), "ffn2_down": (2, 2752, 1024),
    "ple_norm": (2, 1024), "ple_gate": (2, 1024, 1024), "ple_proj": (2, 256, 1024),
    "final_norm": (1024,),
}


def host_consts():
    c = {}
    c["c_ident"] = np.eye(128, dtype=np.float32)
    i = np.arange(128)
    c["c_tri128"] = (i[:, None] <= i[None, :]).astype(np.float32)
    same = (i[:, None] // 64) == (i[None, :] // 64)
    c["c_tri64"] = ((i[:, None] <= i[None, :]) & same).astype(np.float32)
    c["c_blk64"] = same.astype(np.float32)
    c["c_mask64"] = ((i[:, None] % 64) <= np.arange(64)[None, :]).astype(np.float32)
    c["c_ones"] = np.ones((128, 128), np.float32)
    g = np.arange(512) // 16
    c["c_selb"] = (np.arange(32)[:, None] == g[None, :]).astype(np.float32)
    c["c_gmask"] = ((i[:, None] // 16) == np.arange(8)[None, :]).astype(np.float32)
    c["c_hmask"] = ((i[:, None] // 64) == np.arange(2)[None, :]).astype(np.float32)
    c["c_iota"] = np.tile(np.arange(512, dtype=np.float32)[None, :], (128, 1))
    return c


class KB:
    def __init__(self, nc, debug=False, stages=None):
        self.nc = nc
        self.sc = Sched()
        self.debug = debug
        self.stages = stages
        self.pbank = 0
        self.uid = 0

    def alloc(self, n, dt=F32):
        if dt == BF16:
            m = (n + 1) // 2
            a = self.arena[:, self.off:self.off + m].bitcast(BF16)
        else:
            m = n
            a = self.arena[:, self.off:self.off + m]
        self.off += m
        assert self.off <= self.arena_n, f"arena overflow {self.off}"
        return a

    def mark(self):
        return self.off

    def release(self, m):
        self.off = m
        self.sc.barrier()

    def bank(self):
        b = self.pbank % 8
        self.pbank += 1
        return b, self.ps[:, b * 512:(b + 1) * 512]

    def key(self, name):
        self.uid += 1
        return (name, self.uid)

    def dram(self, name, shape, dt):
        kind = "ExternalOutput" if (self.debug and name in self.debug) else "Internal"
        return self.nc.dram_tensor(name, list(shape), dt, kind=kind).ap()

    def A(self, eng, fn, r=(), w=()):
        self.sc.add(eng, fn, r, w)

    def dma(self, out, in_, r=(), w=(), slow=False):
        if slow:
            self.sc.add('sp', lambda e: e.dma_start(out=out, in_=in_, allow_slow_non_contiguous=True), r, w, dma=True)
        else:
            self.sc.add('sp', lambda e: e.dma_start(out=out, in_=in_), r, w, dma=True)

    def mm(self, out, lhsT, rhs, start, stop, r, w):
        self.sc.add('pe', lambda e: e.matmul(out, lhsT=lhsT, rhs=rhs, start=start, stop=stop), r, w)

    def act(self, out, in_, func, r, w, bias=None, scale=None):
        kw = {}
        if bias is not None:
            kw['bias'] = bias
        if scale is not None:
            kw['scale'] = scale
        self.sc.add('act', lambda e: e.activation(out=out, in_=in_, func=func, **kw), r, w)

    def tt(self, eng, out, in0, in1, op, r, w):
        self.sc.add(eng, lambda e: e.tensor_tensor(out=out, in0=in0, in1=in1, op=op), r, w)

    def ts(self, eng, out, in0, s1, s2, op0, op1, r, w):
        if op1 is None:
            self.sc.add(eng, lambda e: e.tensor_scalar(out=out, in0=in0, scalar1=s1, scalar2=None, op0=op0), r, w)
        else:
            self.sc.add(eng, lambda e: e.tensor_scalar(out=out, in0=in0, scalar1=s1, scalar2=s2, op0=op0, op1=op1), r, w)

    def stt(self, out, in0, scalar, in1, op0, op1, r, w):
        self.sc.add('dve', lambda e: e.scalar_tensor_tensor(out=out, in0=in0, scalar=scalar, in1=in1,
                                                           op0=op0, op1=op1), r, w)

    def copy(self, eng, out, in_, r, w):
        if eng == 'act':
            self.sc.add('act', lambda e: e.activation(out=out, in_=in_, func=AF.Copy), r, w)
        else:
            self.sc.add(eng, lambda e: e.tensor_copy(out=out, in_=in_), r, w)

    def load_cols(self, dst, vec_ap, nk):
        k = self.key('col')
        self.dma(dst, vec_ap.rearrange("(k p) -> p k", p=128), w=[k], slow=True)
        return k

    def load_weight(self, w_ap, kc_sizes, c0, ncols, stg, wb, kstg, kwb, cast_eng='pool'):
        KC = len(kc_sizes)
        full = [i for i, s in enumerate(kc_sizes) if s == 128]
        nf = len(full)
        stg3 = stg[:, :KC * ncols].rearrange("p (k n) -> p k n", k=KC)
        wb3 = wb[:, :KC * ncols].rearrange("p (k n) -> p k n", k=KC)
        if nf > 0:
            self.dma(stg3[:, :nf, :], w_ap[0:nf * 128, c0:c0 + ncols].rearrange("(k p) n -> p k n", p=128),
                     w=[kstg])
            self.copy(cast_eng, wb3[:, :nf, :], stg3[:, :nf, :], r=[kstg], w=[kwb])
        if nf < KC:
            rem = kc_sizes[-1]
            self.dma(stg3[:rem, nf, :], w_ap[nf * 128:nf * 128 + rem, c0:c0 + ncols], w=[kstg])
            self.copy(cast_eng, wb3[:rem, nf, :], stg3[:rem, nf, :], r=[kstg], w=[kwb])
        return wb3

    def rmsnorm_tile(self, h, hkeys, wcol, kw, xn_out, xnkeys, tmp):
        sq, sd, rstd = tmp
        ksq = [self.key('sq') for _ in range(8)]
        for k in range(8):
            self.act(sq[:, k * 512:(k + 1) * 512], h[:, k * 512:(k + 1) * 512], AF.Square,
                     r=[hkeys[k]], w=[ksq[k]])
        b, pb = self.bank()
        for k in range(8):
            self.mm(pb, self.ones_b, sq[:, k * 512:(k + 1) * 512], k == 0, k == 7,
                    r=[ksq[k], 'ones'], w=[('ps', b)])
        ksd = self.key('sd')
        self.act(sd, pb, AF.Sqrt, r=[('ps', b)], w=[ksd], bias=EPS, scale=1.0 / D)
        krs = self.key('rstd')
        self.A('dve', lambda e: e.reciprocal(out=rstd, in_=sd), r=[ksd], w=[krs])
        for k in range(8):
            eng = 'dve'
            self.stt(xn_out(k), h[:, k * 512:(k + 1) * 512], wcol[:, k:k + 1], rstd, ALU.mult, ALU.mult,
                     r=[hkeys[k], krs, kw], w=[xnkeys[k]])

    def build(self):
        nc = self.nc
        self.x = nc.dram_tensor("x", [S, D], F32, kind="ExternalInput").ap()
        self.p = nc.dram_tensor("p", [DEPTH, S, 256], F32, kind="ExternalInput").ap()
        self.prm = {}
        for n in PARAM_NAMES:
            self.prm[n] = nc.dram_tensor(n, list(PARAM_SHAPES[n]), F32, kind="ExternalInput").ap()
        self.cst = {}
        for n, v in host_consts().items():
            self.cst[n] = nc.dram_tensor(n, list(v.shape), F32, kind="ExternalInput").ap()
        self.out = nc.dram_tensor("out", [S, D], F32, kind="ExternalOutput").ap()
        self.H = self.dram("H", [8, 128, S], F32)
        self.HID = self.dram("HID", [22, 128, S], BF16)
        self.US5 = self.dram("US5", [4, 128, S], BF16)
        self.Q = self.dram("Q", [2, 128, S], BF16)
        self.Kf = self.dram("Kf", [2, 128, S], BF16)
        self.KT = self.dram("KT", [S, 256], BF16)
        self.VT = self.dram("VT", [S, 512], BF16)
        self.GO = self.dram("GO", [4, 128, S], BF16)
        self.GLR = self.dram("GLR", [1, 128, S], F32)
        self.Z = self.dram("Z", [4, 128, S], BF16)
        self.XBC = self.dram("XBC", [8, 128, S], F32)
        self.DTT = self.dram("DTT", [S, 8], F32)
        self.SIG = self.dram("SIG", [24, 128, S], BF16)
        self.Y = self.dram("Y", [3, 4, 128, S], BF16)
        self.S5T = self.dram("S5T", [2, 32, 64], F32)
        self.arena_n = 52000
        with nc.sbuf_tensor("arena", [128, self.arena_n], F32) as arena, \
                nc.psum_tensor("ps", [128, 4096], F32) as ps:
            self.arena = arena
            self.ps = ps
            self.off = 0
            self.ident = self.alloc(128)
            self.ones_b = self.alloc(128, BF16)
            self.eps_col = self.alloc(1)
            self.dma(self.ident, self.cst["c_ident"], w=['ident'])
            self.A('pool', lambda e: e.memset(self.ones_b, 1.0), w=['ones'])
            self.A('pool', lambda e: e.memset(self.eps_col, EPS), w=['eps'])
            self.alloc_consts()
            self.XNf = self.arena[:, self.off:self.off + 4 * S]
            self.XN = self.alloc(8 * S, BF16)
            self.base = self.mark()
            self.sc.barrier()

            self.stage_input()
            if self.debug:
                self.stage_ffn(0, 1)
                self.stage_mixer(0)
            else:
                for l in range(DEPTH):
                    self.stage_ffn(l, 1)
                    self.stage_mixer(l)
                    self.stage_ffn(l, 2)
                    self.stage_ple(l)
            self.sc.emit(nc)
        return nc

    def xn_ap(self, k, t):
        return self.XN[:, k * S + t * TT:k * S + (t + 1) * TT]

    def stage_input(self):
        m = self.mark()
        xt = [[self.alloc(1024) for _ in range(4)] for _ in range(2)]
        hb = [self.alloc(8 * 512) for _ in range(2)]
        tmp = (self.alloc(8 * 512, BF16), self.alloc(512), self.alloc(512))
        wcol = self.alloc(8)
        kw = self.load_cols(wcol, self.prm["ffn1_norm"][0], 8)
        for t in range(NT):
            xb = xt[t % 2]
            h = hb[t % 2]
            for sub in range(4):
                r0 = t * TT + sub * 128
                self.dma(xb[sub], self.x[r0:r0 + 128, :], w=[('xt', t % 2, sub)])
            hkeys = [('h', t % 2, k) for k in range(8)]
            for k in range(8):
                b, pb = self.bank()
                for sub in range(4):
                    self.A('pe', lambda e, pb=pb, sub=sub, k=k, xb=xb: e.transpose(
                        out=pb[:, sub * 128:(sub + 1) * 128], in_=xb[sub][:, k * 128:(k + 1) * 128],
                        identity=self.ident), r=[('xt', t % 2, sub), 'ident'], w=[('ps', b)])
                self.copy('act' if k % 2 == 0 else 'dve', h[:, k * 512:(k + 1) * 512], pb,
                          r=[('ps', b)], w=[hkeys[k]])
            self.dma(self.H[:, :, t * TT:(t + 1) * TT].rearrange("k p n -> p k n"),
                     h.rearrange("p (k n) -> p k n", k=8), r=hkeys)
            self.rmsnorm_tile(h, hkeys, wcol, kw, lambda k, t=t: self.xn_ap(k, t),
                              [('xn', k, t) for k in range(8)], tmp)
        self.release(m)

    def stage_ffn(self, l, which):
        pre = f"ffn{which}_"
        Wg, Wu, Wd = self.prm[pre + "gate"][l], self.prm[pre + "up"][l], self.prm[pre + "down"][l]
        m0 = self.mark()
        kc_sizes = [128] * 21 + [64]
        wd = self.alloc(22 * 1024, BF16)
        wd3 = wd.rearrange("p (k n) -> p k n", k=22)
        m = self.mark()
        stgd = self.alloc(8 * 512)
        stg = [self.alloc(8 * 512) for _ in range(2)]
        wgb = [self.alloc(8 * 512, BF16) for _ in range(2)]
        wub = [self.alloc(8 * 512, BF16) for _ in range(2)]
        sil = [self.alloc(512) for _ in range(2)]
        hid = [self.alloc(512, BF16) for _ in range(4)]
        groups = [(c0, min(512, DFF - c0)) for c0 in range(0, DFF, 512)]
        loaded = {}

        def issue_load(gi):
            c0, ncols = groups[gi]
            s_ = gi % 2
            g3 = self.load_weight(Wg, [128] * 8, c0, ncols, stg[0], wgb[s_], ('stg', 0), ('wgb', s_))
            u3 = self.load_weight(Wu, [128] * 8, c0, ncols, stg[1], wub[s_], ('stg', 1), ('wub', s_))
            loaded[gi] = (g3, u3)

        def issue_wd(q):
            k0 = q * 4
            nk = min(4, 22 - k0)
            st = stgd[:, :nk * 1024].rearrange("p (k n) -> p k n", k=nk)
            for kk in range(nk):
                rows = kc_sizes[k0 + kk]
                self.dma(st[:rows, kk, :], Wd[(k0 + kk) * 128:(k0 + kk) * 128 + rows, :], w=['stgd'])
            if k0 + nk == 22:
                self.copy('pool', wd3[:, k0:k0 + nk - 1, :], st[:, :nk - 1, :], r=['stgd'], w=['wd'])
                self.copy('pool', wd3[:64, 21, :], st[:64, nk - 1, :], r=['stgd'], w=['wd'])
            else:
                self.copy('pool', wd3[:, k0:k0 + nk, :], st, r=['stgd'], w=['wd'])

        issue_load(0)
        it = 0
        hi = 0
        for gi, (c0, ncols) in enumerate(groups):
            s = gi % 2
            if gi + 1 < len(groups):
                issue_load(gi + 1)
            issue_wd(gi)
            g3, u3 = loaded[gi]
            for t in range(NT):
                for mc in range((ncols + 127) // 128):
                    mw = min(128, ncols - mc * 128)
                    j = (c0 // 128) + mc
                    bg, pg = self.bank()
                    for k in range(8):
                        self.mm(pg[:mw, :], g3[:, k, mc * 128:mc * 128 + mw], self.xn_ap(k, t), k == 0, k == 7,
                                r=[('wgb', s), ('xn', k, t)], w=[('ps', bg)])
                    bu, pu = self.bank()
                    for k in range(8):
                        self.mm(pu[:mw, :], u3[:, k, mc * 128:mc * 128 + mw], self.xn_ap(k, t), k == 0, k == 7,
                                r=[('wub', s), ('xn', k, t)], w=[('ps', bu)])
                    sl = sil[it % 2]
                    hd = hid[hi % 4]
                    self.act(sl[:mw, :], pg[:mw, :], AF.Silu, r=[('ps', bg)], w=[('sil', it % 2)])
                    self.tt('dve', hd[:mw, :], sl[:mw, :], pu[:mw, :], ALU.mult,
                            r=[('sil', it % 2), ('ps', bu)], w=[('hid', hi % 4)])
                    self.dma(self.HID[j, :mw, t * TT:(t + 1) * TT], hd[:mw, :], r=[('hid', hi % 4)],
                             w=[('HID', j, t)])
                    it += 1
                    hi += 1
        assert len(groups) == 6
        self.release(m)
        hidt = [self.alloc(22 * 512, BF16) for _ in range(2)]
        hb = [self.alloc(8 * 512) for _ in range(2)]
        tmp = (self.alloc(8 * 512, BF16), self.alloc(512), self.alloc(512))
        wcol = self.alloc(8)
        nxt = self.prm["mix_norm"][l] if which == 1 else self.prm["ple_norm"][l]
        kw = self.load_cols(wcol, nxt, 8)

        def issue_tile_loads(t):
            ht3 = hidt[t % 2].rearrange("p (k n) -> p k n", k=22)
            self.dma(ht3[:, :21, :], self.HID[0:21, :, t * TT:(t + 1) * TT].rearrange("k p n -> p k n"),
                     w=[('hidt', t % 2)])
            self.dma(ht3[:64, 21, :], self.HID[21, :64, t * TT:(t + 1) * TT], w=[('hidt', t % 2)])
            self.dma(hb[t % 2].rearrange("p (k n) -> p k n", k=8),
                     self.H[:, :, t * TT:(t + 1) * TT].rearrange("k p n -> p k n"),
                     w=[('h', t % 2, k) for k in range(8)])

        issue_tile_loads(0)
        for t in range(NT):
            if t + 1 < NT:
                issue_tile_loads(t + 1)
            ht3 = hidt[t % 2].rearrange("p (k n) -> p k n", k=22)
            h = hb[t % 2]
            hkeys = [('h', t % 2, k) for k in range(8)]
            for mo in range(8):
                b, pb = self.bank()
                for j in range(22):
                    rows = kc_sizes[j]
                    self.mm(pb, wd3[:rows, j, mo * 128:(mo + 1) * 128], ht3[:rows, j, :], j == 0, j == 21,
                            r=['wd', ('hidt', t % 2)], w=[('ps', b)])
                hk = h[:, mo * 512:(mo + 1) * 512]
                self.stt(hk, pb, 0.5, hk, ALU.mult, ALU.add, r=[('ps', b), hkeys[mo]], w=[hkeys[mo]])
            self.dma(self.H[:, :, t * TT:(t + 1) * TT].rearrange("k p n -> p k n"),
                     h.rearrange("p (k n) -> p k n", k=8), r=hkeys)
            self.rmsnorm_tile(h, hkeys, wcol, kw, lambda k, t=t: self.xn_ap(k, t),
                              [('xn', k, t) for k in range(8)], tmp)
        self.release(m0)

    def stage_mixer(self, l):
        st = self.stages
        self.stage_proj(l)
        def run_gens(gens):
            m_ = self.mark()
            while gens:
                for it_ in list(gens):
                    self.sc.ns = it_[0]
                    try:
                        next(it_[1])
                    except StopIteration:
                        gens.remove(it_)
            self.sc.ns = None
            self.release(m_)

        if st is None or 'ssd' in st:
            run_gens([('ssd', self.stage_ssd(l))])
        m2 = self.mark()
        gens = []
        if st is None or 's5' in st:
            C_ = self.s5_pre(l)
            gens.append(('s5', self.s5_main(l, C_)))
        if st is None or 'gla' in st:
            gens.append(('gla', self.stage_gla(l)))
        run_gens(gens)
        self.release(m2)
        if st is None or 'merge' in st:
            self.stage_merge(l)
        else:
            self.renorm(self.prm["ffn2_norm"][l])

    def load_consts(self):
        return

    def alloc_consts(self):
        def ld(name, n, rows=128):
            a = self.alloc(n)
            self.dma(a[:rows, :], self.cst[name], w=[name])
            return a
        self.tri128 = ld("c_tri128", 128)
        self.tri64 = ld("c_tri64", 128)
        self.blk64 = ld("c_blk64", 128)
        self.mask64 = ld("c_mask64", 64)
        self.onesf = ld("c_ones", 128)
        self.gmask = ld("c_gmask", 8)
        self.hmask = ld("c_hmask", 2)

    def stage_proj(self, l):
        W = self.prm["w_in"][l]
        m = self.mark()
        stg = [self.alloc(8 * 512) for _ in range(2)]
        wbb = [self.alloc(8 * 512, BF16) for _ in range(2)]
        of = [self.alloc(512) for _ in range(3)]
        ob = [self.alloc(512, BF16) for _ in range(3)]
        segs = [(0, 512, AF.Copy, 1.0, self.US5, BF16), (512, 256, AF.Copy, 0.125, self.Q, BF16),
                (768, 256, AF.Copy, 1.0, self.Kf, BF16), (1536, 512, AF.Silu, 1.0, self.GO, BF16),
                (2048, 16, AF.Copy, 1.0, self.GLR, F32), (2064, 512, AF.Silu, 1.0, self.Z, BF16),
                (2576, 1024, AF.Copy, 1.0, self.XBC, F32), (3608, 3072, AF.Sigmoid, 1.0, self.SIG, BF16)]
        work = []
        for (c0s, n, func, scale, dest, dt) in segs:
            for g0 in range(0, n, 512):
                work.append(('fm', c0s + g0, min(512, n - g0), func, scale, dest, dt, g0))
        for (c0s, n, dest, dt) in [(768, 256, self.KT, BF16), (1024, 512, self.VT, BF16), (3600, 8, self.DTT, F32)]:
            work.append(('tm', c0s, n, None, None, dest, dt, 0))
        loaded = {}

        def issue_load(gi):
            kind, c0, ncols = work[gi][0], work[gi][1], work[gi][2]
            sidx = gi % 2
            loaded[gi] = self.load_weight(W, [128] * 8, c0, ncols, stg[sidx], wbb[sidx], ('stg', sidx),
                                          ('wbb', sidx))

        issue_load(0)
        oi = 0
        for gi, (kind, c0, ncols, func, scale, dest, dt, g0) in enumerate(work):
            sidx = gi % 2
            if gi + 1 < len(work):
                issue_load(gi + 1)
            w3 = loaded[gi]
            if kind == 'fm':
                for t in range(NT):
                    for mc in range((ncols + 127) // 128):
                        mw = min(128, ncols - mc * 128)
                        j = g0 // 128 + mc
                        b, pb = self.bank()
                        for k in range(8):
                            self.mm(pb[:mw, :], w3[:, k, mc * 128:mc * 128 + mw], self.xn_ap(k, t), k == 0, k == 7,
                                    r=[('wbb', sidx), ('xn', k, t)], w=[('ps', b)])
                        o = (of if dt == F32 else ob)[oi % 3]
                        okey = ('of' if dt == F32 else 'ob', oi % 3)
                        oi += 1
                        self.act(o[:mw, :], pb[:mw, :], func, r=[('ps', b)], w=[okey], scale=scale)
                        self.dma(dest[j, :mw, t * TT:(t + 1) * TT], o[:mw, :], r=[okey])
            else:
                n = ncols
                for blk in range(S // 128):
                    t, sub = blk // 4, blk % 4
                    b, pb = self.bank()
                    for k in range(8):
                        xs_ = self.XN[:, k * S + blk * 128:k * S + (blk + 1) * 128]
                        self.mm(pb[:, :n], xs_, w3[:, k, :n], k == 0, k == 7, r=[('wbb', sidx), ('xn', k, t)],
                                w=[('ps', b)])
                    o = (of if dt == F32 else ob)[oi % 3]
                    okey = ('of' if dt == F32 else 'ob', oi % 3)
                    oi += 1
                    self.copy('act' if blk % 2 == 0 else 'dve', o[:, :n], pb[:, :n], r=[('ps', b)], w=[okey])
                    self.dma(dest[blk * 128:(blk + 1) * 128, :], o[:, :n], r=[okey])
        self.release(m)

    def stage_ssd(self, l):
        P = self.prm
        f3 = lambda a, k: a.rearrange("p (k n) -> p k n", k=k)
        cw = self.alloc(32)
        cb = self.alloc(8)
        for k in range(4):
            self.dma(cw[:, k * 8:(k + 1) * 8], P["ssd_conv_w"][l][k].rearrange("(c p) -> p c", p=128), w=['cw'],
                     slow=True)
        self.dma(cb, P["ssd_conv_b"][l].rearrange("(c p) -> p c", p=128), w=['cb'], slow=True)
        dtb = self.alloc(8)
        abc = self.alloc(8)
        dbc = self.alloc(8)
        self.dma(dtb, P["ssd_dt_bias"][l].partition_broadcast(128), w=['dtb'], slow=True)
        self.dma(abc, P["ssd_a_log"][l].partition_broadcast(128), w=['abc'], slow=True)
        self.dma(dbc, P["ssd_d"][l].partition_broadcast(128), w=['dbc'], slow=True)
        self.act(abc, abc, AF.Exp, r=['abc'], w=['abc'])
        self.ts('dve', abc, abc, -1.0, None, ALU.mult, None, r=['abc'], w=['abc'])
        dcol = self.alloc(4)
        for i in range(4):
            self.copy('dve', dcol[0:64, i:i + 1], dbc[0:64, 2 * i:2 * i + 1], r=['dbc'], w=['dcol'])
            self.copy('dve', dcol[64:128, i:i + 1], dbc[64:128, 2 * i + 1:2 * i + 2], r=['dbc'], w=['dcol'])
        nw = self.alloc(4)
        self.dma(nw, P["ssd_norm"][l].rearrange("(c p) -> p c", p=128), w=['nw'], slow=True)
        hT = [self.alloc(256) for _ in range(2)]
        hTb = [self.alloc(256, BF16) for _ in range(2)]
        for g in range(2):
            self.A('pool', lambda e, g=g: e.memset(hT[g], 0.0), w=[('hT', g)])
            self.A('pool', lambda e, g=g: e.memset(hTb[g], 0.0), w=[('hTb', g)])
        xin = [self.alloc(8 * 520) for _ in range(1)]
        acc = self.alloc(512)
        xc = self.alloc(8 * 512)
        xcb = self.alloc(4 * 512, BF16)
        zt = self.alloc(4 * 512, BF16)
        ybuf = self.alloc(4 * 512, BF16)
        dtr = self.alloc(8)
        dt = self.alloc(8)
        la = self.alloc(8)
        cum = self.alloc(8)
        latri = self.alloc(1024)
        dec = self.alloc(1024)
        ecr = self.alloc(1024)
        ecl = self.alloc(8)
        ds = self.alloc(8)
        Cp = self.alloc(1024, BF16)
        scm = self.alloc(256)
        SdT = self.alloc(1024, BF16)
        xdt = self.alloc(512, BF16)
        xdtd = self.alloc(512, BF16)
        Bt = self.alloc(256, BF16)
        yv = self.alloc(512)
        sq = self.alloc(512, BF16)
        sd = self.alloc(256)
        rstd = self.alloc(256)
        xc3 = f3(xc, 8)
        xcb3 = f3(xcb, 4)
        for t in range(NT):
            t0 = t * TT
            xi3 = xin[0].rearrange("p (k n) -> p k n", k=8)
            if t == 0:
                self.A('pool', lambda e: e.memset(xi3[:, :, 0:3], 0.0), w=['xin'])
                self.dma(xi3[:, :, 3:515], self.XBC[:, :, 0:TT].rearrange("k p n -> p k n"), w=['xin'])
            else:
                self.dma(xi3[:, :, 0:515], self.XBC[:, :, t0 - 3:t0 + TT].rearrange("k p n -> p k n"), w=['xin'])
            self.dma(f3(zt, 4), self.Z[:, :, t0:t0 + TT].rearrange("k p n -> p k n"), w=['zt'])
            for c in range(8):
                self.ts('dve', acc, xi3[:, c, 3:515], cw[:, 24 + c:25 + c], None, ALU.mult, None,
                        r=['xin', 'cw'], w=['acc'])
                for k in (2, 1, 0):
                    self.stt(acc, xi3[:, c, k:k + 512], cw[:, k * 8 + c:k * 8 + c + 1], acc, ALU.mult, ALU.add,
                             r=['xin', 'cw', 'acc'], w=['acc'])
                self.act(xc3[:, c, :], acc, AF.Silu, r=['acc', 'cb'], w=[('xc', c)], bias=cb[:, c:c + 1])
                if c >= 4:
                    self.copy('act', xcb3[:, c - 4, :], xc3[:, c, :], r=[('xc', c)], w=[('xcb', c)])
            for sub in range(4):
                r0 = t0 + sub * 128
                tk = slice(sub * 128, (sub + 1) * 128)
                self.dma(dtr, self.DTT[r0:r0 + 128, :], w=['dtr'])
                self.tt('dve', dt, dtr, dtb, ALU.add, r=['dtr', 'dtb'], w=['dt'])
                self.act(dt, dt, AF.Exp, r=['dt'], w=['dt'])
                self.act(dt, dt, AF.Ln, r=['dt'], w=['dt'], bias=1.0)
                self.tt('dve', la, dt, abc, ALU.mult, r=['dt', 'abc'], w=['la'])
                bc_, pc = self.bank()
                self.mm(pc[:, 0:8], self.tri128, la, True, True, r=['c_tri128', 'la'], w=[('ps', bc_)])
                self.copy('dve', cum, pc[:, 0:8], r=[('ps', bc_)], w=['cum'])
                lt3 = f3(latri, 8)
                for r in range(8):
                    if r % 2:
                        self.act(lt3[:, r, :], self.tri128, AF.Copy, r=['c_tri128', 'la'], w=[('latri', r)],
                                 scale=la[:, r:r + 1])
                    else:
                        self.ts('dve', lt3[:, r, :], self.tri128, la[:, r:r + 1], None, ALU.mult, None,
                                r=['c_tri128', 'la'], w=[('latri', r)])
                b1, p1 = self.bank()
                b2, p2 = self.bank()
                self.mm(p1, self.onesf, latri[:, 0:512], True, True, r=['c_ones'] + [('latri', r) for r in range(4)],
                        w=[('ps', b1)])
                self.mm(p2, self.onesf, latri[:, 512:1024], True, True,
                        r=['c_ones'] + [('latri', r) for r in range(4, 8)], w=[('ps', b2)])
                pr = [f3(p1, 4), f3(p2, 4)]
                d3 = f3(dec, 8)
                e3 = f3(ecr, 8)
                for r in range(8):
                    self.ts('dve', d3[:, r, :], pr[r // 4][:, r % 4, :], cum[:, r:r + 1], 0.0, ALU.subtract, ALU.min,
                            r=[('ps', b1 if r < 4 else b2), 'cum'], w=[('dec', r // 4)])
                for hh in range(2):
                    self.act(dec[:, hh * 512:(hh + 1) * 512], dec[:, hh * 512:(hh + 1) * 512], AF.Exp,
                             r=[('dec', hh)], w=[('dec', hh)])
                    self.act(ecr[:, hh * 512:(hh + 1) * 512], [p1, p2][hh], AF.Exp, r=[('ps', [b1, b2][hh])],
                             w=[('ecr', hh)])
                    self.copy('dve', ecl[:, hh * 4:(hh + 1) * 4], e3[:, hh * 4:(hh + 1) * 4, 127], r=[('ecr', hh)],
                              w=['ecl'])
                    self.tt('dve', ds[:, hh * 4:(hh + 1) * 4], pr[hh][:, :, 127], cum[:, hh * 4:(hh + 1) * 4],
                            ALU.subtract, r=[('ps', [b1, b2][hh]), 'cum'], w=['ds'])
                self.act(ds, ds, AF.Exp, r=['ds'], w=['ds'])
                C3 = f3(Cp, 8)
                for g in range(2):
                    cin = xc3[:, 6 + g, tk].unsqueeze(1).broadcast_to([128, 4, 128])
                    self.tt('pool', C3[:, 4 * g:4 * g + 4, :], e3[:, 4 * g:4 * g + 4, :], cin, ALU.mult,
                            r=[('ecr', g), ('xc', 6 + g)], w=[('Cp', g)])
                bs, psc = self.bank()
                for g in range(2):
                    self.mm(psc[:, g * 128:(g + 1) * 128], xcb3[:, g, tk], xcb3[:, 2 + g, tk], True, True,
                            r=[('xcb', 4 + g), ('xcb', 6 + g)], w=[('ps', bs)])
                s3 = f3(scm, 2)
                self.tt('dve', s3, f3(psc[:, 0:256], 2), self.tri128.unsqueeze(1).broadcast_to([128, 2, 128]), ALU.mult,
                        r=[('ps', bs), 'c_tri128'], w=['scm'])
                S3 = f3(SdT, 8)
                for g in range(2):
                    self.tt('pool' if g else 'dve', S3[:, 4 * g:4 * g + 4, :], d3[:, 4 * g:4 * g + 4, :],
                            s3[:, g, :].unsqueeze(1).broadcast_to([128, 4, 128]), ALU.mult,
                            r=[('dec', g), 'scm'], w=[('SdT', g)])
                bx, px = self.bank()
                for c in range(4):
                    self.A('pe', lambda e, px=px, c=c, tk=tk: e.transpose(out=px[:, c * 128:(c + 1) * 128],
                                                                          in_=xc3[:, c, tk], identity=self.ident),
                           r=[('xc', c), 'ident'], w=[('ps', bx)])
                bb, pbt = self.bank()
                for c in range(2):
                    self.A('pe', lambda e, pbt=pbt, c=c, tk=tk: e.transpose(out=pbt[:, c * 128:(c + 1) * 128],
                                                                            in_=xc3[:, 4 + c, tk], identity=self.ident),
                           r=[('xc', 4 + c), 'ident'], w=[('ps', bb)])
                x3 = f3(xdt, 8)
                xd3 = f3(xdtd, 8)
                self.tt('dve', x3, f3(px, 8), dt.unsqueeze(2).broadcast_to([128, 8, 64]), ALU.mult,
                        r=[('ps', bx), 'dt'], w=['xdt'])
                self.tt('pool', xd3, x3, ds.unsqueeze(2).broadcast_to([128, 8, 64]), ALU.mult, r=['xdt', 'ds'],
                        w=['xdtd'])
                self.copy('act', Bt, pbt[:, 0:256], r=[('ps', bb)], w=['Bt'])
                by, py = self.bank()
                for i in range(4):
                    for r2 in range(2):
                        r = 2 * i + r2
                        g = r // 4
                        o_ = py[64 * r2:64 * r2 + 64, i * 128:(i + 1) * 128]
                        self.mm(o_, x3[:, r, :], S3[:, r, :], True, False, r=['xdt', ('SdT', g)], w=[('ps', by)])
                        self.mm(o_, hTb[g][:, (r % 4) * 64:(r % 4) * 64 + 64], C3[:, r, :], False, True,
                                r=[('hTb', g), ('Cp', g)], w=[('ps', by)])
                for g in range(2):
                    bst, pst = self.bank()
                    self.mm(pst[:, 0:256], Bt[:, g * 128:(g + 1) * 128], xdtd[:, g * 256:(g + 1) * 256], True, True,
                            r=['Bt', 'xdtd'], w=[('ps', bst)])
                    h3 = f3(hT[g], 4)
                    self.tt('dve', h3, h3, ecl[:, 4 * g:4 * g + 4].unsqueeze(2).broadcast_to([128, 4, 64]), ALU.mult,
                            r=[('hT', g), 'ecl'], w=[('hT', g)])
                    self.tt('dve', hT[g], hT[g], pst[:, 0:256], ALU.add, r=[('hT', g), ('ps', bst)], w=[('hT', g)])
                    self.copy('act', hTb[g], hT[g], r=[('hT', g)], w=[('hTb', g)])
                y3 = f3(yv, 4)
                z3 = f3(zt, 4)
                for i in range(4):
                    self.stt(y3[:, i, :], xc3[:, i, tk], dcol[:, i:i + 1], py[:, i * 128:(i + 1) * 128], ALU.mult,
                             ALU.add, r=[('xc', i), 'dcol', ('ps', by)], w=['yv'])
                self.tt('dve', y3, y3, z3[:, :, tk], ALU.mult, r=['yv', 'zt'], w=['yv'])
                self.act(sq, yv, AF.Square, r=['yv'], w=['sq'])
                bn, pn = self.bank()
                for g in range(2):
                    for j in range(2):
                        c = 2 * g + j
                        self.mm(pn[:, g * 128:(g + 1) * 128], self.ones_b, sq[:, c * 128:(c + 1) * 128], j == 0, j == 1,
                                r=['ones', 'sq'], w=[('ps', bn)])
                self.act(sd, pn[:, 0:256], AF.Sqrt, r=[('ps', bn)], w=['sd'], bias=EPS, scale=1.0 / 256)
                self.A('dve', lambda e: e.reciprocal(out=rstd, in_=sd), r=['sd'], w=['rstd'])
                yb3 = f3(ybuf, 4)
                for i in range(4):
                    self.stt(yb3[:, i, tk], y3[:, i, :], nw[:, i:i + 1], rstd[:, (i // 2) * 128:(i // 2 + 1) * 128],
                             ALU.mult, ALU.mult, r=['yv', 'nw', 'rstd'], w=['ybuf'])
                yield
            self.dma(self.Y[2, :, :, t0:t0 + TT].rearrange("k p n -> p k n"), f3(ybuf, 4), r=['ybuf'])

    def stage_gla(self, l):
        P = self.prm
        f3 = lambda a, k: a.rearrange("p (k n) -> p k n", k=k)
        w2 = self.alloc(256)
        b2b = self.alloc(256)
        gnw = self.alloc(1)
        self.A('pool', lambda e: e.memset(w2, 0.0), w=['w2'])
        self.dma(w2[0:16, :], P["gla_gate_w2"][l], w=['w2'])
        self.dma(b2b, P["gla_gate_b2"][l].partition_broadcast(128), w=['b2b'], slow=True)
        self.dma(gnw, P["gla_norm"][l].rearrange("(p o) -> p o", o=1), w=['gnw'], slow=True)
        Sf = [self.alloc(128) for _ in range(2)]
        Sb = [self.alloc(128, BF16) for _ in range(2)]
        for kc in range(2):
            self.A('pool', lambda e, kc=kc: e.memset(Sf[kc], 0.0), w=[('Sf', kc)])
            self.A('pool', lambda e, kc=kc: e.memset(Sb[kc], 0.0), w=[('Sb', kc)])
        qt = self.alloc(2 * 512, BF16)
        kt_ = self.alloc(2 * 512, BF16)
        got = self.alloc(4 * 512, BF16)
        glr = self.alloc(512)
        self.A('pool', lambda e: e.memset(glr, 0.0), w=['glr'])
        ybuf = self.alloc(4 * 512, BF16)
        ktmB = [self.alloc(256, BF16) for _ in range(2)]
        vtmB = [self.alloc(512, BF16) for _ in range(2)]

        def kv_load(bi):
            self.dma(ktmB[bi % 2], self.KT[bi * 128:(bi + 1) * 128, :], w=[('ktm', bi % 2)])
            self.dma(vtmB[bi % 2], self.VT[bi * 128:(bi + 1) * 128, :], w=[('vtm', bi % 2)])

        kv_load(0)
        xb = self.alloc(256)
        loga = self.alloc(256)
        bsb = self.alloc(256)
        eb = self.alloc(256)
        enb = self.alloc(256)
        qe = self.alloc(256, BF16)
        qem = [self.alloc(256, BF16) for _ in range(2)]
        kem = [self.alloc(256, BF16) for _ in range(2)]
        kl = self.alloc(256)
        klm = [self.alloc(256, BF16) for _ in range(2)]
        ATs = [self.alloc(256, BF16) for _ in range(2)]
        for c2 in range(2):
            self.A('pool', lambda e, c2=c2: e.memset(ATs[c2], 0.0), w=[('AT', c2)])
        sq = self.alloc(512, BF16)
        sd = self.alloc(512)
        rstd = self.alloc(512)
        yg = self.alloc(512)
        q3, k3, go3, yb3 = f3(qt, 2), f3(kt_, 2), f3(got, 4), f3(ybuf, 4)
        eb3, enb3, qe3 = f3(eb, 2), f3(enb, 2), f3(qe, 2)
        qem3 = [f3(a, 2) for a in qem]
        kem3 = [f3(a, 2) for a in kem]
        for t in range(NT):
            t0 = t * TT
            self.dma(q3, self.Q[:, :, t0:t0 + TT].rearrange("k p n -> p k n"), w=['qt'])
            self.dma(k3, self.Kf[:, :, t0:t0 + TT].rearrange("k p n -> p k n"), w=['kt'])
            self.dma(go3, self.GO[:, :, t0:t0 + TT].rearrange("k p n -> p k n"), w=['got'])
            self.dma(glr[0:16, :], self.GLR[0, 0:16, t0:t0 + TT], w=['glr'])
            for sub in range(4):
                r0 = t0 + sub * 128
                tk = slice(sub * 128, (sub + 1) * 128)
                bi_ = t * 4 + sub
                if bi_ + 1 < S // 128:
                    kv_load(bi_ + 1)
                ktm, vtm = ktmB[bi_ % 2], vtmB[bi_ % 2]
                kkt, kvt = ('ktm', bi_ % 2), ('vtm', bi_ % 2)
                bx, px = self.bank()
                self.mm(px[:, 0:256], glr[:, tk], w2, True, True, r=['glr', 'w2'], w=[('ps', bx)])
                self.tt('dve', xb, px[:, 0:256], b2b, ALU.add, r=[('ps', bx), 'b2b'], w=['xb'])
                self.act(xb, xb, AF.Exp, r=['xb'], w=['xb'], scale=-1.0)
                self.act(xb, xb, AF.Ln, r=['xb'], w=['xb'], bias=1.0)
                self.ts('dve', loga, xb, -1.0 / 16.0, None, ALU.mult, None, r=['xb'], w=['loga'])
                bb_, pbm = self.bank()
                self.mm(pbm[:, 0:256], self.tri64, loga, True, True, r=['c_tri64', 'loga'], w=[('ps', bb_)])
                self.mm(pbm[:, 256:512], self.blk64, loga, True, True, r=['c_blk64', 'loga'], w=[('ps', bb_)])
                bf_, pbf = self.bank()
                for kc in range(2):
                    self.mm(pbf[:, kc * 128:(kc + 1) * 128], loga[:, kc * 128:(kc + 1) * 128], self.tri64, True, True,
                            r=['loga', 'c_tri64'], w=[('ps', bf_)])
                self.act(eb, pbf[:, 0:256], AF.Exp, r=[('ps', bf_)], w=['eb'])
                self.act(enb, pbf[:, 0:256], AF.Exp, r=[('ps', bf_)], w=['enb'], scale=-1.0)
                self.tt('dve', qe3, q3[:, :, tk], eb3, ALU.mult, r=['qt', 'eb'], w=['qe'])
                for h2 in range(2):
                    for kc in range(2):
                        self.stt(kem3[h2][:, kc, :], k3[:, kc, tk], self.hmask[:, h2:h2 + 1], enb3[:, kc, :], ALU.mult,
                                 ALU.mult, r=['kt', 'enb', 'c_hmask'], w=[('kem', h2)])
                    self.act(qem[h2], qe, AF.Copy, r=['qe', 'c_hmask'], w=[('qem', h2)],
                             scale=self.hmask[:, h2:h2 + 1])
                self.copy('dve', bsb, pbm[:, 0:256], r=[('ps', bb_)], w=['bsb'])
                self.tt('dve', bsb, pbm[:, 256:512], bsb, ALU.subtract, r=[('ps', bb_), 'bsb'], w=['bsb'])
                self.act(bsb, bsb, AF.Exp, r=['bsb'], w=['bsb'])
                self.tt('dve', kl, ktm, bsb, ALU.mult, r=[kkt, 'bsb'], w=['kl'])
                for c2 in range(2):
                    self.act(klm[c2], kl, AF.Copy, r=['kl', 'c_hmask'], w=[('klm', c2)],
                             scale=self.hmask[:, c2:c2 + 1])
                bo, po = self.bank()
                for c2 in range(2):
                    cs = slice(64 * c2, 64 * c2 + 64)
                    ba, pa = self.bank()
                    for hd in range(4):
                        kc, h2 = hd // 2, hd % 2
                        hs = slice(64 * h2, 64 * h2 + 64)
                        self.mm(pa[cs, hd * 64:(hd + 1) * 64], kem3[h2][:, kc, cs], qe3[:, kc, cs], True, True,
                                r=[('kem', h2), 'qe'], w=[('ps', ba)])
                    AT3 = f3(ATs[c2], 4)
                    self.tt('dve', AT3[cs, :, :], f3(pa[cs, 0:256], 4),
                            self.mask64[cs, :].unsqueeze(1).broadcast_to([64, 4, 64]), ALU.mult,
                            r=[('ps', ba), 'c_mask64'], w=[('AT', c2)])
                    for hd in range(4):
                        kc, h2 = hd // 2, hd % 2
                        hs = slice(64 * h2, 64 * h2 + 64)
                        o_ = po[:, hd * 128 + 64 * c2:hd * 128 + 64 * c2 + 64]
                        self.mm(o_, vtm[:, hd * 128:(hd + 1) * 128], AT3[:, hd, :], True, False,
                                r=[kvt, ('AT', c2)], w=[('ps', bo)])
                        self.mm(o_, Sb[kc], qem3[h2][:, kc, cs], False, True, r=[('Sb', kc), ('qem', h2)],
                                w=[('ps', bo)])
                    bs_, pS = self.bank()
                    for kc in range(2):
                        for h2 in range(2):
                            hd = 2 * kc + h2
                            self.mm(pS[64 * h2:64 * h2 + 64, kc * 128:(kc + 1) * 128],
                                    klm[c2][:, hd * 64:(hd + 1) * 64],
                                    vtm[:, hd * 128:(hd + 1) * 128], True, True, r=[('klm', c2), kvt],
                                    w=[('ps', bs_)])
                        self.stt(Sf[kc], Sf[kc], eb3[:, kc, 64 * c2 + 63:64 * c2 + 64], pS[:, kc * 128:(kc + 1) * 128],
                                 ALU.mult, ALU.add, r=[('Sf', kc), 'eb', ('ps', bs_)], w=[('Sf', kc)])
                        self.copy('act', Sb[kc], Sf[kc], r=[('Sf', kc)], w=[('Sb', kc)])
                self.act(sq, po, AF.Square, r=[('ps', bo)], w=['sq'])
                bn, pn = self.bank()
                self.mm(pn, self.ones_b, sq, True, True, r=['ones', 'sq'], w=[('ps', bn)])
                self.act(sd, pn, AF.Sqrt, r=[('ps', bn)], w=['sd'], bias=EPS, scale=1.0 / 128)
                self.A('dve', lambda e: e.reciprocal(out=rstd, in_=sd), r=['sd'], w=['rstd'])
                self.stt(yg, po, gnw[:, 0:1], rstd, ALU.mult, ALU.mult, r=[('ps', bo), 'gnw', 'rstd'], w=['yg'])
                self.tt('dve', yb3[:, :, tk], f3(yg, 4), go3[:, :, tk], ALU.mult, r=['yg', 'got'], w=['ybuf'])
                yield
            self.dma(self.Y[1, :, :, t0:t0 + TT].rearrange("k p n -> p k n"), yb3, r=['ybuf'])

    def s5_pre(self, l):
        P = self.prm
        PI = float(np.pi)
        f3 = lambda a, k: a.rearrange("p (k n) -> p k n", k=k)
        NS = 32
        hp = self.alloc(1)
        rho_c = self.alloc(16)
        th_c = self.alloc(16)
        BbT = self.alloc(16 * 2 * 128, BF16)
        CT = self.alloc(16 * 2 * 128, BF16)
        e5r = self.alloc(16)
        e5i = self.alloc(16)
        t16a = self.alloc(16)
        t16b = self.alloc(16)
        dcol = self.alloc(4)
        wgl = self.alloc(4 * 512, BF16)
        mt = self.mark()
        nat = {n: self.alloc(64) for n in ["lr", "li", "x1", "th", "mag", "c", "s", "cc", "ss", "cs", "ar", "ai",
                                           "den", "nr", "t1", "t2"]}
        stc = self.alloc(1)
        ff = self.alloc(128)
        N = lambda n: nat[n][:NS, :]
        self.dma(N("lr"), P["s5_lam_re"][l], w=['n_lr'])
        self.dma(N("li"), P["s5_lam_im"][l], w=['n_li'])
        self.dma(stc[:NS, :], P["s5_log_step"][l].rearrange("(g o) -> g o", o=1), w=['stc'], slow=True)
        self.act(stc[:NS, :], stc[:NS, :], AF.Exp, r=['stc'], w=['stc'])
        self.ts('dve', N("lr"), N("lr"), -1e-4, None, ALU.min, None, r=['n_lr'], w=['n_lr'])
        self.ts('dve', N("x1"), N("lr"), stc[:NS, 0:1], None, ALU.mult, None, r=['n_lr', 'stc'], w=['n_x1'])
        self.ts('dve', N("th"), N("li"), stc[:NS, 0:1], None, ALU.mult, None, r=['n_li', 'stc'], w=['n_th'])
        self.act(N("mag"), N("x1"), AF.Exp, r=['n_x1'], w=['n_mag'])
        self.A('pool', lambda e: e.memset(hp, PI / 2), w=['hp'])
        self.act(N("s"), N("th"), AF.Sin, r=['n_th'], w=['n_s'], scale=1.0 / 16)
        self.act(N("c"), N("th"), AF.Sin, r=['n_th', 'hp'], w=['n_c'], scale=1.0 / 16, bias=hp[:NS, 0:1])
        for it in range(4):
            self.tt('dve', N("cc"), N("c"), N("c"), ALU.mult, r=['n_c'], w=['n_cc'])
            self.tt('dve', N("ss"), N("s"), N("s"), ALU.mult, r=['n_s'], w=['n_ss'])
            self.tt('dve', N("cs"), N("c"), N("s"), ALU.mult, r=['n_c', 'n_s'], w=['n_cs'])
            self.tt('dve', N("c"), N("cc"), N("ss"), ALU.subtract, r=['n_cc', 'n_ss'], w=['n_c'])
            self.ts('dve', N("s"), N("cs"), 2.0, None, ALU.mult, None, r=['n_cs'], w=['n_s'])
        self.tt('dve', N("ar"), N("mag"), N("c"), ALU.mult, r=['n_mag', 'n_c'], w=['n_ar'])
        self.tt('dve', N("ai"), N("mag"), N("s"), ALU.mult, r=['n_mag', 'n_s'], w=['n_ai'])
        self.tt('dve', N("t1"), N("lr"), N("lr"), ALU.mult, r=['n_lr'], w=['n_t1'])
        self.tt('dve', N("t2"), N("li"), N("li"), ALU.mult, r=['n_li'], w=['n_t2'])
        self.tt('dve', N("den"), N("t1"), N("t2"), ALU.add, r=['n_t1', 'n_t2'], w=['n_den'])
        self.A('dve', lambda e: e.reciprocal(out=N("den"), in_=N("den")), r=['n_den'], w=['n_den'])
        self.ts('dve', N("nr"), N("ar"), -1.0, None, ALU.add, None, r=['n_ar'], w=['n_nr'])
        self.tt('dve', N("t1"), N("nr"), N("lr"), ALU.mult, r=['n_nr', 'n_lr'], w=['n_t1'])
        self.tt('dve', N("t2"), N("ai"), N("li"), ALU.mult, r=['n_ai', 'n_li'], w=['n_t2'])
        self.tt('dve', N("t1"), N("t1"), N("t2"), ALU.add, r=['n_t1', 'n_t2'], w=['n_t1'])
        self.tt('dve', ff[:NS, 0:64], N("t1"), N("den"), ALU.mult, r=['n_t1', 'n_den'], w=['ff'])
        self.tt('dve', N("t1"), N("ai"), N("lr"), ALU.mult, r=['n_ai', 'n_lr'], w=['n_t1'])
        self.tt('dve', N("t2"), N("nr"), N("li"), ALU.mult, r=['n_nr', 'n_li'], w=['n_t2'])
        self.tt('dve', N("t1"), N("t1"), N("t2"), ALU.subtract, r=['n_t1', 'n_t2'], w=['n_t1'])
        self.tt('dve', ff[:NS, 64:128], N("t1"), N("den"), ALU.mult, r=['n_t1', 'n_den'], w=['ff'])
        self.dma(self.S5T[0], N("mag"), r=['n_mag'], w=['S5T0'])
        self.dma(self.S5T[1], N("th"), r=['n_th'], w=['S5T1'])
        for t2 in range(2):
            self.dma(rho_c[t2 * 64:(t2 + 1) * 64, :], self.S5T[0, t2::2, :].rearrange("j p -> p j"), r=['S5T0'],
                     w=['rho_c'], slow=True)
            self.dma(th_c[t2 * 64:(t2 + 1) * 64, :], self.S5T[1, t2::2, :].rearrange("j p -> p j"), r=['S5T1'],
                     w=['th_c'], slow=True)
        selb = self.alloc(512)
        self.dma(selb[:NS, :], self.cst["c_selb"], w=['selb'])
        bnat = [self.alloc(512) for _ in range(2)]
        self.dma(f3(bnat[0][:64, :], 32), P["s5_b_re"][l].rearrange("g p h -> p g h"), w=[('bnat', 0)])
        self.dma(f3(bnat[1][:64, :], 32), P["s5_b_im"][l].rearrange("g p h -> p g h"), w=[('bnat', 1)])
        BbT4 = BbT.rearrange("p (j c n) -> p j c n", j=16, c=2)
        fq = self.alloc(128)
        bq = [self.alloc(64) for _ in range(2)]
        bbq = [self.alloc(64) for _ in range(2)]
        tq = [self.alloc(64) for _ in range(2)]
        for q in range(4):
            b, pb = self.bank()
            self.mm(pb[:, 0:128], selb[:NS, q * 128:(q + 1) * 128], ff[:NS, :], True, True, r=['selb', 'ff'],
                    w=[('ps', b)])
            self.copy('act', fq, pb[:, 0:128], r=[('ps', b)], w=['fq'])
            b2, pb2 = self.bank()
            for c in range(2):
                self.A('pe', lambda e, pb2=pb2, c=c, q=q: e.transpose(out=pb2[:, c * 64:(c + 1) * 64],
                                                                      in_=bnat[c][:64, q * 128:(q + 1) * 128],
                                                                      identity=self.ident[:64, :64]),
                       r=[('bnat', c), 'ident'], w=[('ps', b2)])
                self.copy('act', bq[c], pb2[:, c * 64:(c + 1) * 64], r=[('ps', b2)], w=[('bq', c)])
            self.tt('dve', tq[0], fq[:, 0:64], bq[0], ALU.mult, r=['fq', ('bq', 0)], w=[('tq', 0)])
            self.tt('dve', tq[1], fq[:, 64:128], bq[1], ALU.mult, r=['fq', ('bq', 1)], w=[('tq', 1)])
            self.tt('dve', bbq[0], tq[0], tq[1], ALU.subtract, r=[('tq', 0), ('tq', 1)], w=[('bbq', 0)])
            self.tt('dve', tq[0], fq[:, 0:64], bq[1], ALU.mult, r=['fq', ('bq', 1)], w=[('tq', 0)])
            self.tt('dve', tq[1], fq[:, 64:128], bq[0], ALU.mult, r=['fq', ('bq', 0)], w=[('tq', 1)])
            self.tt('dve', bbq[1], tq[0], tq[1], ALU.add, r=[('tq', 0), ('tq', 1)], w=[('bbq', 1)])
            for gl in range(8):
                g = q * 8 + gl
                j, t2 = g // 2, g % 2
                for c in range(2):
                    self.ts('dve', BbT4[:, j, c, t2 * 64:(t2 + 1) * 64], bbq[c], self.gmask[:, gl:gl + 1], None,
                            ALU.mult, None, r=[('bbq', c), 'c_gmask'], w=['BbT'])
        CT4 = CT.rearrange("p (j c n) -> p j c n", j=16, c=2)
        self.A('pool', lambda e: e.memset(CT, 0.0), w=['CT'])
        cdup = self.alloc(128)
        cT = self.alloc(128)
        for c, nm in enumerate(["s5_c_re", "s5_c_im"]):
            src = P[nm][l].rearrange("g h p -> (g h) p")
            for q in range(4):
                self.dma(cdup[:, 0:64], src[q * 128:(q + 1) * 128, :], w=['cdup'])
                self.dma(cdup[:, 64:128], src[q * 128:(q + 1) * 128, :], w=['cdup'])
                b, pb = self.bank()
                self.A('pe', lambda e, pb=pb: e.transpose(out=pb[:, 0:128], in_=cdup, identity=self.ident),
                       r=['cdup', 'ident'], w=[('ps', b)])
                self.act(cT, pb[:, 0:128], AF.Copy, r=[('ps', b)], w=['cT'], scale=(1.0 if c == 0 else -1.0))
                for gl in range(8):
                    g = q * 8 + gl
                    j, t2 = g // 2, g % 2
                    rs = slice(t2 * 64, (t2 + 1) * 64)
                    self.copy('dve' if gl % 2 else 'pool', CT4[rs, j, c, gl * 16:(gl + 1) * 16],
                              cT[rs, gl * 16:(gl + 1) * 16], r=['cT'], w=['CT'])
        cs_t = self.XNf[:, 0:8192]
        sn_t = self.XNf[:, 8192:16384]
        cs3, sn3 = f3(cs_t, 16), f3(sn_t, 16)
        iota = self.alloc(512)
        self.dma(iota, self.cst["c_iota"], w=['iota'])
        ph = self.alloc(512)
        kq = self.alloc(512)
        ki = self.alloc(512).bitcast(mybir.dt.int32)
        ta = self.alloc(512)
        tb = self.alloc(512)
        for j in range(16):
            self.ts('dve', ph, iota, th_c[:, j:j + 1], None, ALU.mult, None, r=['iota', 'th_c'], w=['ph'])
            self.ts('dve', kq, ph, 1.0 / (2 * PI), None, ALU.mult, None, r=['ph'], w=['kq'])
            self.copy('dve', ki, kq, r=['kq'], w=['ki'])
            self.copy('dve', kq, ki, r=['ki'], w=['kq'])
            self.stt(ph, kq, -2 * PI, ph, ALU.mult, ALU.add, r=['kq', 'ph'], w=['ph'])
            self.act(sn3[:, j, :], ph, AF.Sin, r=['ph'], w=[('sn', j)], scale=0.25)
            self.act(cs3[:, j, :], ph, AF.Sin, r=['ph', 'hp'], w=[('cs', j)], scale=0.25, bias=hp[:, 0:1])
            for it in range(2):
                self.tt('dve', ta, cs3[:, j, :], cs3[:, j, :], ALU.mult, r=[('cs', j)], w=['ta'])
                self.tt('pool', tb, sn3[:, j, :], sn3[:, j, :], ALU.mult, r=[('sn', j)], w=['tb'])
                self.tt('pool', kq, cs3[:, j, :], sn3[:, j, :], ALU.mult, r=[('cs', j), ('sn', j)], w=['kq'])
                self.tt('dve', cs3[:, j, :], ta, tb, ALU.subtract, r=['ta', 'tb'], w=[('cs', j)])
                self.ts('pool', sn3[:, j, :], kq, 2.0, None, ALU.mult, None, r=['kq'], w=[('sn', j)])
        allk = [('cs', j) for j in range(16)] + [('sn', j) for j in range(16)]
        self.tt('dve', t16a, cs3[:, :, 511], cs3[:, :, 1], ALU.mult, r=allk, w=['t16a'])
        self.tt('dve', t16b, sn3[:, :, 511], sn3[:, :, 1], ALU.mult, r=allk, w=['t16b'])
        self.tt('dve', e5r, t16a, t16b, ALU.subtract, r=['t16a', 't16b'], w=['e5r'])
        self.tt('dve', t16a, cs3[:, :, 511], sn3[:, :, 1], ALU.mult, r=allk, w=['t16a'])
        self.tt('dve', t16b, sn3[:, :, 511], cs3[:, :, 1], ALU.mult, r=allk, w=['t16b'])
        self.tt('dve', e5i, t16a, t16b, ALU.add, r=['t16a', 't16b'], w=['e5i'])
        self.dma(dcol, P["s5_d"][l].rearrange("(c p) -> p c", p=128), w=['dcol'], slow=True)
        stg = self.alloc(4 * 512)
        self.dma(f3(stg, 4), P["s5_glu"][l].rearrange("(k p) n -> p k n", p=128), w=['stg'])
        self.copy('pool', wgl, stg, r=['stg'], w=['wgl'])
        self.release(mt)
        return dict(BbT=BbT, CT=CT, rho_c=rho_c, e5r=e5r, e5i=e5i, t16a=t16a, t16b=t16b, dcol=dcol, wgl=wgl)

    def s5_main(self, l, C_):
        f3 = lambda a, k: a.rearrange("p (k n) -> p k n", k=k)
        BbT, CT, rho_c, e5r, e5i, t16a, t16b, dcol, wgl = (C_[k] for k in
                                                             ["BbT", "CT", "rho_c", "e5r", "e5i", "t16a", "t16b",
                                                              "dcol", "wgl"])
        BbT4 = BbT.rearrange("p (j c n) -> p j c n", j=16, c=2)
        CT4 = CT.rearrange("p (j c n) -> p j c n", j=16, c=2)
        cs3, sn3 = f3(self.XNf[:, 0:8192], 16), f3(self.XNf[:, 8192:16384], 16)
        wgl3 = f3(wgl, 4)
        ut = self.alloc(4 * 512, BF16)
        ut3 = f3(ut, 4)
        PB = [[self.alloc(512) for _ in range(4)] for _ in range(2)]
        WB = [[self.alloc(512), self.alloc(512)] for _ in range(2)]
        GB = [[self.alloc(512), self.alloc(512)] for _ in range(2)]
        HB = [[self.alloc(512, BF16), self.alloc(512, BF16)] for _ in range(2)]
        glr_, gli_ = self.alloc(16), self.alloc(16)
        inr, ini = self.alloc(16), self.alloc(16)
        yq = self.alloc(512)
        x2 = self.alloc(512)
        gq = self.alloc(4 * 512, BF16)
        gq3 = f3(gq, 4)
        sgl = self.alloc(512)
        ybuf = self.alloc(4 * 512, BF16)
        yb3 = f3(ybuf, 4)
        self.A('pool', lambda e: e.memset(inr, 0.0), w=['inr'])
        self.A('pool', lambda e: e.memset(ini, 0.0), w=['ini'])
        xbk = 0
        for t in range(NT):
            t0 = t * TT
            self.dma(ut3, self.US5[:, :, t0:t0 + TT].rearrange("k p n -> p k n"), w=['ut'])
            if t > 0:
                self.tt('dve', t16a, e5r, glr_, ALU.mult, r=['e5r', 'glr_'], w=['t16a'])
                self.tt('dve', t16b, e5i, gli_, ALU.mult, r=['e5i', 'gli_'], w=['t16b'])
                self.tt('dve', inr, t16a, t16b, ALU.subtract, r=['t16a', 't16b'], w=['inr'])
                self.tt('dve', t16a, e5r, gli_, ALU.mult, r=['e5r', 'gli_'], w=['t16a'])
                self.tt('dve', t16b, e5i, glr_, ALU.mult, r=['e5i', 'glr_'], w=['t16b'])
                self.tt('dve', ini, t16a, t16b, ALU.add, r=['t16a', 't16b'], w=['ini'])
            for j in range(16):
                q = j // 4
                bA = xbk % 6
                bB = (xbk + 1) % 6
                xbk += 2
                pA = self.ps[:, bA * 512:(bA + 1) * 512]
                pB = self.ps[:, bB * 512:(bB + 1) * 512]
                self.mm(pA, BbT4[:, j, 0, :], ut3[:, q, :], True, True, r=['BbT', 'ut'], w=[('ps', bA)])
                self.mm(pB, BbT4[:, j, 1, :], ut3[:, q, :], True, True, r=['BbT', 'ut'], w=[('ps', bB)])
                cj, sj = cs3[:, j, :], sn3[:, j, :]
                tabk = [('cs', j), ('sn', j)]
                par = j % 2
                p1, p2, p3, p4 = PB[par]
                wr, wi = WB[par]
                gr, gi = GB[par]
                Hr, Hi = HB[par]
                kp = [('p%d' % n_, par) for n_ in range(1, 5)]
                kwr, kwi, kgr, kgi, kHr, kHi = [(n_, par) for n_ in ['wr', 'wi', 'gr', 'gi', 'Hr', 'Hi']]
                self.tt('dve', p1, pA, cj, ALU.mult, r=[('ps', bA)] + tabk, w=[kp[0]])
                self.tt('dve', p2, pB, sj, ALU.mult, r=[('ps', bB)] + tabk, w=[kp[1]])
                self.tt('dve', p3, pB, cj, ALU.mult, r=[('ps', bB)] + tabk, w=[kp[2]])
                self.tt('dve', p4, pA, sj, ALU.mult, r=[('ps', bA)] + tabk, w=[kp[3]])
                self.tt('pool', wr, p1, p2, ALU.add, r=[kp[0], kp[1]], w=[kwr])
                self.tt('pool', wi, p3, p4, ALU.subtract, r=[kp[2], kp[3]], w=[kwi])
                rb = rho_c[:, j:j + 1].broadcast_to([128, 512])
                self.A('dve', lambda e, rb=rb, j=j, gr=gr, wr=wr: e.tensor_tensor_scan(
                    out=gr, data0=rb, data1=wr, initial=inr[:, j:j + 1], op0=ALU.mult, op1=ALU.add),
                    r=[kwr, 'rho_c', 'inr'], w=[kgr])
                self.A('dve', lambda e, rb=rb, j=j, gi=gi, wi=wi: e.tensor_tensor_scan(
                    out=gi, data0=rb, data1=wi, initial=ini[:, j:j + 1], op0=ALU.mult, op1=ALU.add),
                    r=[kwi, 'rho_c', 'ini'], w=[kgi])
                self.copy('act', glr_[:, j:j + 1], gr[:, 511:512], r=[kgr], w=['glr_'])
                self.copy('act', gli_[:, j:j + 1], gi[:, 511:512], r=[kgi], w=['gli_'])
                self.tt('pool', p1, gr, cj, ALU.mult, r=[kgr] + tabk, w=[kp[0]])
                self.tt('pool', p2, gi, sj, ALU.mult, r=[kgi] + tabk, w=[kp[1]])
                self.tt('pool', p3, gi, cj, ALU.mult, r=[kgi] + tabk, w=[kp[2]])
                self.tt('dve', p4, gr, sj, ALU.mult, r=[kgr] + tabk, w=[kp[3]])
                self.tt('dve', Hr, p1, p2, ALU.subtract, r=[kp[0], kp[1]], w=[kHr])
                self.tt('dve', Hi, p3, p4, ALU.add, r=[kp[2], kp[3]], w=[kHi])
                bY = 6 + q % 2
                pY = self.ps[:, bY * 512:(bY + 1) * 512]
                first = (j % 4 == 0)
                last = (j % 4 == 3)
                self.mm(pY, CT4[:, j, 0, :], Hr, first, False, r=['CT', kHr], w=[('ps', bY)])
                self.mm(pY, CT4[:, j, 1, :], Hi, False, last, r=['CT', kHi], w=[('ps', bY)])
                if last:
                    self.stt(yq, ut3[:, q, :], dcol[:, q:q + 1], pY, ALU.mult, ALU.add, r=['ut', 'dcol', ('ps', bY)],
                             w=['yq'])
                    self.tt('dve', x2, yq, yq, ALU.mult, r=['yq'], w=['x2'])
                    self.ts('dve', x2, x2, 0.044715, 1.0, ALU.mult, ALU.add, r=['x2'], w=['x2'])
                    self.tt('dve', x2, x2, yq, ALU.mult, r=['x2', 'yq'], w=['x2'])
                    self.act(x2, x2, AF.Tanh, r=['x2'], w=['x2'], scale=0.7978845608028654)
                    self.stt(x2, x2, 1.0, yq, ALU.add, ALU.mult, r=['x2', 'yq'], w=['x2'])
                    self.ts('dve', gq3[:, q, :], x2, 0.5, None, ALU.mult, None, r=['x2'], w=[('gq', q)])
                    yield
            for mo in range(4):
                b, pb = self.bank()
                for q in range(4):
                    self.mm(pb, wgl3[:, q, mo * 128:(mo + 1) * 128], gq3[:, q, :], q == 0, q == 3,
                            r=['wgl', ('gq', q)], w=[('ps', b)])
                self.act(sgl, pb, AF.Sigmoid, r=[('ps', b)], w=['sgl'])
                self.tt('dve', yb3[:, mo, :], gq3[:, mo, :], sgl, ALU.mult, r=[('gq', mo), 'sgl'], w=['ybuf'])
            self.dma(self.Y[0, :, :, t0:t0 + TT].rearrange("k p n -> p k n"), yb3, r=['ybuf'])

    def stage_merge(self, l):
        m = self.mark()
        P = self.prm
        f3 = lambda a, k: a.rearrange("p (k n) -> p k n", k=k)
        stg = self.alloc(8 * 512)
        wbr = [self.alloc(4 * 1024, BF16) for _ in range(3)]
        wo = self.alloc(8 * 1024, BF16)
        for bi, nm in enumerate(["w_br_s5", "w_br_gla", "w_br_ssd"]):
            st = f3(stg[:, :4096], 4)
            self.dma(st, P[nm][l].rearrange("(k p) n -> p k n", p=128), w=['stg'])
            self.copy('pool', f3(wbr[bi], 4), st, r=['stg'], w=[('wbr', bi)])
        wo3 = f3(wo, 8)
        for half in range(2):
            st = f3(stg, 8)
            self.dma(st, P["w_out"][l][:, half * 512:(half + 1) * 512].rearrange("(k p) n -> p k n", p=128), w=['stg'])
            self.copy('pool', wo3[:, :, half * 512:(half + 1) * 512], st, r=['stg'], w=['wo'])
        yt = [self.alloc(4 * 512, BF16) for _ in range(3)]
        sg = self.alloc(24 * 512, BF16)
        hb = self.alloc(8 * 512)
        mg = self.alloc(512)
        tmpm = self.alloc(512)
        mgb = self.alloc(8 * 512, BF16)
        tmp = (self.alloc(8 * 512, BF16), self.alloc(512), self.alloc(512))
        wcol = self.alloc(8)
        kw = self.load_cols(wcol, P["ffn2_norm"][l], 8)
        sg3 = f3(sg, 24)
        for t in range(NT):
            t0 = t * TT
            for bi in range(3):
                self.dma(f3(yt[bi], 4), self.Y[bi, :, :, t0:t0 + TT].rearrange("k p n -> p k n"), w=[('yt', bi)])
            self.dma(sg3, self.SIG[:, :, t0:t0 + TT].rearrange("k p n -> p k n"), w=['sg'])
            hkeys = [('h', 0, k) for k in range(8)]
            self.dma(f3(hb, 8), self.H[:, :, t0:t0 + TT].rearrange("k p n -> p k n"), w=hkeys)
            for mo in range(8):
                for bi in range(3):
                    b, pb = self.bank()
                    y3 = f3(yt[bi], 4)
                    w3 = f3(wbr[bi], 4)
                    for kc in range(4):
                        self.mm(pb, w3[:, kc, mo * 128:(mo + 1) * 128], y3[:, kc, :], kc == 0, kc == 3,
                                r=[('wbr', bi), ('yt', bi)], w=[('ps', b)])
                    if bi == 0:
                        self.tt('dve', mg, pb, sg3[:, bi * 8 + mo, :], ALU.mult, r=[('ps', b), 'sg'], w=['mg'])
                    else:
                        self.tt('dve', tmpm, pb, sg3[:, bi * 8 + mo, :], ALU.mult, r=[('ps', b), 'sg'], w=['tmpm'])
                        self.tt('pool', mg, mg, tmpm, ALU.add, r=['mg', 'tmpm'], w=['mg'])
                self.copy('act', mgb[:, mo * 512:(mo + 1) * 512], mg, r=['mg'], w=[('mgb', mo)])
            for mo2 in range(8):
                b, pb = self.bank()
                for mo in range(8):
                    self.mm(pb, wo3[:, mo, mo2 * 128:(mo2 + 1) * 128], mgb[:, mo * 512:(mo + 1) * 512], mo == 0,
                            mo == 7, r=['wo', ('mgb', mo)], w=[('ps', b)])
                hk = hb[:, mo2 * 512:(mo2 + 1) * 512]
                self.tt('dve', hk, hk, pb, ALU.add, r=[hkeys[mo2], ('ps', b)], w=[hkeys[mo2]])
            self.dma(self.H[:, :, t0:t0 + TT].rearrange("k p n -> p k n"), f3(hb, 8), r=hkeys)
            self.rmsnorm_tile(hb, hkeys, wcol, kw, lambda k, t=t: self.xn_ap(k, t),
                              [('xn', k, t) for k in range(8)], tmp)
        self.release(m)

    def renorm(self, vec):
        m = self.mark()
        hb = [self.alloc(8 * 512) for _ in range(2)]
        tmp = (self.alloc(8 * 512, BF16), self.alloc(512), self.alloc(512))
        wcol = self.alloc(8)
        kw = self.load_cols(wcol, vec, 8)
        for t in range(NT):
            h = hb[t % 2]
            hkeys = [('h', t % 2, k) for k in range(8)]
            self.dma(h.rearrange("p (k n) -> p k n", k=8),
                     self.H[:, :, t * TT:(t + 1) * TT].rearrange("k p n -> p k n"), w=hkeys)
            self.rmsnorm_tile(h, hkeys, wcol, kw, lambda k, t=t: self.xn_ap(k, t),
                              [('xn', k, t) for k in range(8)], tmp)
        self.release(m)

    def stage_ple(self, l):
        m = self.mark()
        Wpg, Wpp = self.prm["ple_gate"][l], self.prm["ple_proj"][l]
        stg = [self.alloc(8 * 512)] * 2
        wg = self.alloc(8 * 1024, BF16)
        wp = self.alloc(2 * 1024, BF16)
        wg3 = wg.rearrange("p (k n) -> p k n", k=8)
        wp3 = wp.rearrange("p (k n) -> p k n", k=2)
        for half in range(2):
            st = stg[half][:, :8 * 512].rearrange("p (k n) -> p k n", k=8)
            self.dma(st, Wpg[:, half * 512:(half + 1) * 512].rearrange("(k p) n -> p k n", p=128),
                     w=[('stg', 0)])
            self.copy('pool', wg3[:, :, half * 512:(half + 1) * 512], st, r=[('stg', 0)], w=['wg'])
        st = stg[0][:, :2048].rearrange("p (k n) -> p k n", k=2)
        self.dma(st, Wpp.rearrange("(k p) n -> p k n", p=128), w=[('stg', 0)])
        self.copy('pool', wp3, st, r=[('stg', 0)], w=['wp'])
        pt = [[self.alloc(256) for _ in range(4)] for _ in range(2)]
        pf = [self.alloc(2 * 512, BF16) for _ in range(2)]
        hb = [self.alloc(8 * 512) for _ in range(2)]
        sg = [self.alloc(512) for _ in range(2)]
        tmp = (self.alloc(8 * 512, BF16), self.alloc(512), self.alloc(512))
        wcol = self.alloc(8)
        last = (l == DEPTH - 1)
        nxt = self.prm["final_norm"] if last else self.prm["ffn1_norm"][l + 1]
        kw = self.load_cols(wcol, nxt, 8)
        if last:
            yb = [self.alloc(8 * 512)] * 2
            ot = [self.alloc(1024) for _ in range(2)]
        it = 0
        for t in range(NT):
            h = hb[t % 2]
            hkeys = [('h', t % 2, k) for k in range(8)]
            self.dma(h.rearrange("p (k n) -> p k n", k=8),
                     self.H[:, :, t * TT:(t + 1) * TT].rearrange("k p n -> p k n"), w=hkeys)
            pb_ = pt[t % 2]
            for sub in range(4):
                r0 = t * TT + sub * 128
                self.dma(pb_[sub], self.p[l, r0:r0 + 128, :], w=[('pt', t % 2, sub)])
            pfb = pf[t % 2]
            for kc in range(2):
                b, pb = self.bank()
                for sub in range(4):
                    self.A('pe', lambda e, pb=pb, sub=sub, kc=kc, pb_=pb_: e.transpose(
                        out=pb[:, sub * 128:(sub + 1) * 128], in_=pb_[sub][:, kc * 128:(kc + 1) * 128],
                        identity=self.ident), r=[('pt', t % 2, sub), 'ident'], w=[('ps', b)])
                self.copy('act', pfb[:, kc * 512:(kc + 1) * 512], pb, r=[('ps', b)], w=[('pf', t % 2, kc)])
            for mo in range(8):
                bg, pg = self.bank()
                for k in range(8):
                    self.mm(pg, wg3[:, k, mo * 128:(mo + 1) * 128], self.xn_ap(k, t), k == 0, k == 7,
                            r=['wg', ('xn', k, t)], w=[('ps', bg)])
                bp, pp = self.bank()
                for kc in range(2):
                    self.mm(pp, wp3[:, kc, mo * 128:(mo + 1) * 128], pfb[:, kc * 512:(kc + 1) * 512],
                            kc == 0, kc == 1, r=['wp', ('pf', t % 2, kc)], w=[('ps', bp)])
                s_ = sg[it % 2]
                self.act(s_, pg, AF.Sigmoid, r=[('ps', bg)], w=[('sg', it % 2)])
                self.tt('dve', s_, s_, pp, ALU.mult, r=[('sg', it % 2), ('ps', bp)], w=[('sg', it % 2)])
                hk = h[:, mo * 512:(mo + 1) * 512]
                self.tt('pool', hk, hk, s_, ALU.add, r=[('sg', it % 2), hkeys[mo]], w=[hkeys[mo]])
                it += 1
            if not last:
                self.dma(self.H[:, :, t * TT:(t + 1) * TT].rearrange("k p n -> p k n"),
                         h.rearrange("p (k n) -> p k n", k=8), r=hkeys)
                self.rmsnorm_tile(h, hkeys, wcol, kw, lambda k, t=t: self.xn_ap(k, t),
                                  [('xn', k, t) for k in range(8)], tmp)
            else:
                y = yb[t % 2]
                ykeys = [('y', 0, k) for k in range(8)]
                self.rmsnorm_tile(h, hkeys, wcol, kw, lambda k, y=y: y[:, k * 512:(k + 1) * 512], ykeys, tmp)
                for sub in range(4):
                    ob = ot[sub % 2]
                    for half in range(2):
                        b, pb = self.bank()
                        for kk in range(4):
                            k = half * 4 + kk
                            self.A('pe', lambda e, pb=pb, kk=kk, k=k, sub=sub, y=y: e.transpose(
                                out=pb[:, kk * 128:(kk + 1) * 128],
                                in_=y[:, k * 512 + sub * 128:k * 512 + (sub + 1) * 128], identity=self.ident),
                                r=[ykeys[k], 'ident'], w=[('ps', b)])
                        self.copy('act' if half == 0 else 'dve', ob[:, half * 512:(half + 1) * 512], pb,
                                  r=[('ps', b)], w=[('ot', sub % 2, half)])
                    r0 = t * TT + sub * 128
                    self.dma(self.out[r0:r0 + 128, :], ob, r=[('ot', sub % 2, 0), ('ot', sub % 2, 1)])
        self.release(m)

    def stage_final(self):
        pass


_NC_CACHE = {}


def kernel(**inputs):
    if "nc" not in _NC_CACHE:
        nc = bass.Bass("TRN2", target_bir_lowering=False)
        kb = KB(nc)
        kb.build()
        _NC_CACHE["nc"] = nc
    nc = _NC_CACHE["nc"]
    consts = host_consts()
    x = np.ascontiguousarray(inputs["x"], dtype=np.float32)
    p = np.ascontiguousarray(inputs["p"], dtype=np.float32)
    in_maps = []
    for c in range(8):
        m = {"x": x[c], "p": np.ascontiguousarray(p[:, c])}
        for n in PARAM_NAMES:
            m[n] = np.ascontiguousarray(inputs[n], dtype=np.float32)
        m.update(consts)
        in_maps.append(m)
    res = run_bass_kernel_spmd(nc, in_maps, core_ids=list(range(8)))
    return np.stack([np.asarray(res.results[c]["out"]) for c in range(8)], axis=0).astype(np.float32)
```
